# Optimizing a Trainium2 kernel written in Bass

```python
import math
import jax, jax.numpy as jnp
from jax import lax
import numpy as np

D_MODEL = 2048
BATCH = 16
SEQ = 2048
DEPTH = 1
DEC_BATCH = 8
DEC_SEQ = 4096
PAST_LEN = 128

MEM_LEN = 256
GDN_HEADS = 8
GDN_HEAD_DIM = 128
GDN_WIDTH = GDN_HEADS * GDN_HEAD_DIM
GDN_CONV = 5
GDN_CHUNK = 64
DIFF_HEADS = 4
DIFF_HEAD_DIM = 128
DIFF_QK_WIDTH = DIFF_HEADS * 2 * DIFF_HEAD_DIM
DIFF_V_WIDTH = DIFF_HEADS * 2 * DIFF_HEAD_DIM
CROSS_HEADS = 4
CROSS_HEAD_DIM = 256
CROSS_WIDTH = CROSS_HEADS * CROSS_HEAD_DIM
N_BRANCH = 3
D_FF = 5504
FFN_CONV = 3
NUM_BUCKETS = 32
MAX_DISTANCE = 128
Q_BLOCK = 128
LN_EPS = 1e-5
RMS_EPS = 1e-6
L2_EPS = 1e-6
DEEPNORM_ALPHA = (2 * DEPTH) ** 0.25
DEEPNORM_BETA = (8 * DEPTH) ** -0.25
IN_SIZES = (GDN_WIDTH, GDN_WIDTH, GDN_WIDTH, GDN_WIDTH, 2 * GDN_HEADS, 2 * GDN_HEADS,
            DIFF_QK_WIDTH, DIFF_QK_WIDTH, DIFF_V_WIDTH, CROSS_WIDTH)
IN_COLS = sum(IN_SIZES)
IN_OFFSETS = tuple(int(o) for o in np.cumsum(IN_SIZES)[:-1])

kernel_name = 'hybrid_bidir_gdn_diffattn_encoder'


def layer_norm(x, g, b):
    xf = x.astype(jnp.float32)
    mu = jnp.mean(xf, -1, keepdims=True)
    xc = xf - mu
    var = jnp.mean(xc * xc, -1, keepdims=True)
    return (xc * lax.rsqrt(var + LN_EPS) * g.astype(jnp.float32) + b.astype(jnp.float32)).astype(x.dtype)


def rms_norm(x, w):
    return x * lax.rsqrt(jnp.mean(x * x, -1, keepdims=True) + RMS_EPS) * w.astype(jnp.float32)


def l2_normalize(x):
    return x * lax.rsqrt(jnp.sum(x * x, -1, keepdims=True) + L2_EPS)


def depthwise_conv(x, w):
    ksz, ch = w.shape
    return lax.conv_general_dilated(x, w[:, None, :].astype(x.dtype), window_strides=(1,),
                                    padding=[(ksz // 2, ksz // 2)],
                                    dimension_numbers=('NWC', 'WIO', 'NWC'),
                                    feature_group_count=ch)


def t5_bucket(rel):
    nb = NUM_BUCKETS // 2
    max_exact = nb // 2
    ret = jnp.where(rel > 0, nb, 0)
    n = jnp.abs(rel)
    nf = jnp.maximum(n, 1).astype(jnp.float32)
    large = max_exact + (jnp.log(nf / max_exact) / math.log(MAX_DISTANCE / max_exact)
                         * (nb - max_exact)).astype(jnp.int32)
    large = jnp.minimum(large, nb - 1)
    return ret + jnp.where(n < max_exact, n, large)


def gated_delta_chunked(q, k, v, g, beta):
    bsz, nh, seq, dk = q.shape
    dv = v.shape[-1]
    nc = seq // GDN_CHUNK
    q = q.reshape(bsz, nh, nc, GDN_CHUNK, dk)
    k = k.reshape(bsz, nh, nc, GDN_CHUNK, dk)
    v = v.reshape(bsz, nh, nc, GDN_CHUNK, dv)
    g = g.reshape(bsz, nh, nc, GDN_CHUNK)
    beta = beta.reshape(bsz, nh, nc, GDN_CHUNK)
    gc = jnp.cumsum(g, -1)
    idx = jnp.arange(GDN_CHUNK)
    incl = idx[:, None] >= idx[None, :]
    strict = idx[:, None] > idx[None, :]
    decay = jnp.exp(jnp.where(incl, gc[..., :, None] - gc[..., None, :], -jnp.inf))
    lmat = jnp.where(strict, beta[..., :, None] * jnp.einsum('bhncd,bhnjd->bhncj', k, k) * decay, 0.0)
    rhs = jnp.concatenate([beta[..., None] * v, (beta * jnp.exp(gc))[..., None] * k], -1)
    sol = lax.linalg.triangular_solve(lmat, rhs, left_side=True, lower=True, unit_diagonal=True)
    wv, wk = sol[..., :dv], sol[..., dv:]
    qk = jnp.einsum('bhncd,bhnjd->bhncj', q, k) * decay
    q_dec = q * jnp.exp(gc)[..., None]
    g_last = gc[..., -1]
    k_dec = k * jnp.exp(g_last[..., None] - gc)[..., None]

    def chunk_step(state, xs):
        wv_c, wk_c, qk_c, qd_c, kd_c, gl_c = xs
        u = wv_c - jnp.einsum('bhck,bhkv->bhcv', wk_c, state)
        o = jnp.einsum('bhck,bhkv->bhcv', qd_c, state) + jnp.einsum('bhcj,bhjv->bhcv', qk_c, u)
        state = jnp.exp(gl_c)[..., None, None] * state + jnp.einsum('bhck,bhcv->bhkv', kd_c, u)
        return state, o

    xs = tuple(jnp.moveaxis(t, 2, 0) for t in (wv, wk, qk, q_dec, k_dec, g_last))
    state0 = jnp.zeros((bsz, nh, dk, dv), jnp.float32)
    _, o = lax.scan(chunk_step, state0, xs)
    return jnp.moveaxis(o, 0, 2).reshape(bsz, nh, seq, dv)


def gdn_branch(q, k, v, z, a, b, conv_w, a_log, dt_bias, norm_w):
    bsz, seq, _ = q.shape
    dtype = q.dtype
    qkv = jax.nn.silu(depthwise_conv(jnp.concatenate([q, k, v], -1), conv_w).astype(jnp.float32))
    qkv = qkv.reshape(bsz, seq, 3, GDN_HEADS, GDN_HEAD_DIM).transpose(2, 0, 3, 1, 4)
    qh = l2_normalize(qkv[0]) * GDN_HEAD_DIM ** -0.5
    kh = l2_normalize(qkv[1])
    vh = qkv[2]
    a = a.astype(jnp.float32).reshape(bsz, seq, 2, GDN_HEADS).transpose(2, 0, 3, 1)
    b = b.astype(jnp.float32).reshape(bsz, seq, 2, GDN_HEADS).transpose(2, 0, 3, 1)
    a_log = a_log.astype(jnp.float32)[:, None, :, None]
    dt_bias = dt_bias.astype(jnp.float32)[:, None, :, None]
    g = -jnp.exp(a_log) * jax.nn.softplus(a + dt_bias)
    beta = jax.nn.sigmoid(b)
    o_fwd = gated_delta_chunked(qh, kh, vh, g[0], beta[0])
    flip = lambda t: jnp.flip(t, axis=2)
    o_bwd = flip(gated_delta_chunked(flip(qh), flip(kh), flip(vh), flip(g[1]), flip(beta[1])))
    o = rms_norm(o_fwd + o_bwd, norm_w).transpose(0, 2, 1, 3)
    o = o * jax.nn.silu(z.astype(jnp.float32).reshape(bsz, seq, GDN_HEADS, GDN_HEAD_DIM))
    return o.reshape(bsz, seq, GDN_WIDTH).astype(dtype)


def diff_attention_branch(q, k, v, lam_params, norm_w, rel_bias, lambda_init):
    bsz, seq, _ = q.shape
    nb = seq // Q_BLOCK
    qb = (q * DIFF_HEAD_DIM ** -0.5).reshape(bsz, nb, Q_BLOCK, DIFF_HEADS, 2, DIFF_HEAD_DIM)
    qb = qb.transpose(1, 0, 3, 4, 2, 5)
    kh = k.reshape(bsz, seq, DIFF_HEADS, 2, DIFF_HEAD_DIM).transpose(0, 2, 3, 1, 4)
    vh = v.reshape(bsz, seq, DIFF_HEADS, 2 * DIFF_HEAD_DIM).transpose(0, 2, 1, 3)
    lp = lam_params.astype(jnp.float32)
    lam = jnp.exp(jnp.sum(lp[0] * lp[1])) - jnp.exp(jnp.sum(lp[2] * lp[3])) + lambda_init
    kpos = jnp.arange(seq)

    def attend_block(args):
        q_blk, bi = args
        qpos = bi * Q_BLOCK + jnp.arange(Q_BLOCK)
        bias = rel_bias[t5_bucket(kpos[None, :] - qpos[:, None])]
        bias = jnp.transpose(bias, (2, 0, 1)).astype(jnp.float32)
        s = jnp.einsum('bhmqd,bhmkd->bhmqk', q_blk, kh,
                       preferred_element_type=jnp.float32) + bias[None, :, None]
        p = jax.nn.softmax(s, axis=-1)
        w = (p[:, :, 0] - lam * p[:, :, 1]).astype(vh.dtype)
        return jnp.einsum('bhqk,bhkd->bhqd', w, vh)

    o = lax.map(attend_block, (qb, jnp.arange(nb)))
    o = rms_norm(o.astype(jnp.float32), norm_w) * (1.0 - lambda_init)
    return o.transpose(1, 0, 3, 2, 4).reshape(bsz, seq, DIFF_V_WIDTH).astype(q.dtype)


def memory_cross_attention(q, mem, w_mem_kv):
    bsz, seq, _ = q.shape
    kv = (mem @ w_mem_kv).reshape(bsz, mem.shape[1], 2, CROSS_HEADS, CROSS_HEAD_DIM)
    qh = (q * CROSS_HEAD_DIM ** -0.5).reshape(bsz, seq, CROSS_HEADS, CROSS_HEAD_DIM)
    s = jnp.einsum('bqhd,bkhd->bhqk', qh, kv[:, :, 0], preferred_element_type=jnp.float32)
    p = jax.nn.softmax(s, axis=-1).astype(q.dtype)
    o = jnp.einsum('bhqk,bkhd->bqhd', p, kv[:, :, 1])
    return o.reshape(bsz, seq, CROSS_WIDTH)


def encoder_trunk(x, mem, rel_bias, w_in, gdn_conv, gdn_a_log, gdn_dt_bias, gdn_norm_w,
                  diff_lambda, diff_norm_w, w_mem_kv, w_gate, b_gate, w_branch_gdn,
                  w_branch_diff, w_branch_cross, w_out, ln1_g, ln1_b, w_up, ffn_conv,
                  w_down, ln2_g, ln2_b):
    bsz, seq, _ = x.shape
    for l in range(DEPTH):
        lambda_init = 0.8 - 0.6 * math.exp(-0.3 * l)
        proj = x @ w_in[l]
        gq, gk, gv, gz, ga, gb, dq, dk, dv, cq = jnp.split(proj, IN_OFFSETS, axis=-1)
        y_gdn = gdn_branch(gq, gk, gv, gz, ga, gb, gdn_conv[l], gdn_a_log[l], gdn_dt_bias[l], gdn_norm_w[l])
        y_diff = diff_attention_branch(dq, dk, dv, diff_lambda[l], diff_norm_w[l], rel_bias, lambda_init)
        y_cross = memory_cross_attention(cq, mem, w_mem_kv[l])
        gates = jax.nn.sigmoid(x @ w_gate[l] + b_gate[l]).reshape(bsz, seq, N_BRANCH, D_MODEL)
        merged = (gates[:, :, 0] * (y_gdn @ w_branch_gdn[l])
                  + gates[:, :, 1] * (y_diff @ w_branch_diff[l])
                  + gates[:, :, 2] * (y_cross @ w_branch_cross[l]))
        x = layer_norm(DEEPNORM_ALPHA * x + merged @ w_out[l], ln1_g[l], ln1_b[l])
        up = depthwise_conv(x @ w_up[l], ffn_conv[l])
        u_gate, u_val = jnp.split(up, 2, axis=-1)
        x = layer_norm(DEEPNORM_ALPHA * x + (jax.nn.silu(u_gate) * u_val) @ w_down[l], ln2_g[l], ln2_b[l])
    return x


def setup_inputs(seed: int = 0) -> dict:
    key = jax.random.key(seed)
    ks = jax.random.split(key, 32)
    f32 = jnp.float32

    def nrm(k, shape, scale):
        return jax.random.normal(k, shape, f32) * scale

    dt = jnp.exp(jax.random.uniform(ks[9], (DEPTH, 2, GDN_HEADS), f32, math.log(1e-3), math.log(1e-1)))
    return {
        'x_prompt': nrm(ks[0], (BATCH, SEQ, D_MODEL), 1.0),
        'x_sample': nrm(ks[1], (DEC_BATCH, DEC_SEQ, D_MODEL), 1.0),
        'mem_prompt': nrm(ks[2], (BATCH, MEM_LEN, D_MODEL), 1.0),
        'mem_sample': nrm(ks[3], (DEC_BATCH, MEM_LEN, D_MODEL), 1.0),
        'rel_bias': nrm(ks[4], (NUM_BUCKETS, DIFF_HEADS), 0.2),
        'w_in': nrm(ks[5], (DEPTH, D_MODEL, IN_COLS), D_MODEL ** -0.5),
        'gdn_conv': nrm(ks[6], (DEPTH, GDN_CONV, 3 * GDN_WIDTH), GDN_CONV ** -0.5),
        'gdn_a_log': jnp.log(jax.random.uniform(ks[7], (DEPTH, 2, GDN_HEADS), f32, 1.0, 16.0)),
        'gdn_dt_bias': dt + jnp.log(-jnp.expm1(-dt)),
        'gdn_norm_w': 1.0 + nrm(ks[8], (DEPTH, GDN_HEAD_DIM), 0.02),
        'diff_lambda': nrm(ks[10], (DEPTH, 4, DIFF_HEAD_DIM), 0.1),
        'diff_norm_w': 1.0 + nrm(ks[11], (DEPTH, 2 * DIFF_HEAD_DIM), 0.02),
        'w_mem_kv': nrm(ks[12], (DEPTH, D_MODEL, 2 * CROSS_WIDTH), D_MODEL ** -0.5),
        'w_gate': nrm(ks[13], (DEPTH, D_MODEL, N_BRANCH * D_MODEL), D_MODEL ** -0.5),
        'b_gate': nrm(ks[14], (DEPTH, N_BRANCH * D_MODEL), 0.1),
        'w_branch_gdn': nrm(ks[15], (DEPTH, GDN_WIDTH, D_MODEL), GDN_WIDTH ** -0.5 * DEEPNORM_BETA),
        'w_branch_diff': nrm(ks[16], (DEPTH, DIFF_V_WIDTH, D_MODEL), DIFF_V_WIDTH ** -0.5 * DEEPNORM_BETA),
        'w_branch_cross': nrm(ks[17], (DEPTH, CROSS_WIDTH, D_MODEL), CROSS_WIDTH ** -0.5 * DEEPNORM_BETA),
        'w_out': nrm(ks[18], (DEPTH, D_MODEL, D_MODEL), D_MODEL ** -0.5 * DEEPNORM_BETA),
        'ln1_g': 1.0 + nrm(ks[19], (DEPTH, D_MODEL), 0.02),
        'ln1_b': nrm(ks[20], (DEPTH, D_MODEL), 0.02),
        'w_up': nrm(ks[21], (DEPTH, D_MODEL, 2 * D_FF), D_MODEL ** -0.5),
        'ffn_conv': nrm(ks[22], (DEPTH, FFN_CONV, 2 * D_FF), FFN_CONV ** -0.5),
        'w_down': nrm(ks[23], (DEPTH, D_FF, D_MODEL), D_FF ** -0.5 * DEEPNORM_BETA),
        'ln2_g': 1.0 + nrm(ks[24], (DEPTH, D_MODEL), 0.02),
        'ln2_b': nrm(ks[25], (DEPTH, D_MODEL), 0.02),
    }


def reference(x_prompt, x_sample, mem_prompt, mem_sample, rel_bias, w_in, gdn_conv, gdn_a_log,
              gdn_dt_bias, gdn_norm_w, diff_lambda, diff_norm_w, w_mem_kv, w_gate, b_gate,
              w_branch_gdn, w_branch_diff, w_branch_cross, w_out, ln1_g, ln1_b, w_up, ffn_conv,
              w_down, ln2_g, ln2_b):
    y_prompt = encoder_trunk(x_prompt, mem_prompt, rel_bias, w_in, gdn_conv, gdn_a_log, gdn_dt_bias,
                             gdn_norm_w, diff_lambda, diff_norm_w, w_mem_kv, w_gate, b_gate,
                             w_branch_gdn, w_branch_diff, w_branch_cross, w_out, ln1_g, ln1_b,
                             w_up, ffn_conv, w_down, ln2_g, ln2_b)
    y_sample = encoder_trunk(x_sample, mem_sample, rel_bias, w_in, gdn_conv, gdn_a_log, gdn_dt_bias,
                             gdn_norm_w, diff_lambda, diff_norm_w, w_mem_kv, w_gate, b_gate,
                             w_branch_gdn, w_branch_diff, w_branch_cross, w_out, ln1_g, ln1_b,
                             w_up, ffn_conv, w_down, ln2_g, ln2_b)
    return (y_prompt, y_sample)
```

```python
import math
from contextlib import ExitStack
import numpy as np
import concourse.bass as bass
import concourse.mybir as mybir
from concourse.bass_utils import run_bass_kernel_spmd

F32 = mybir.dt.float32
BF16 = mybir.dt.bfloat16
AF = mybir.ActivationFunctionType
ALU = mybir.AluOpType
AX = mybir.AxisListType

D = 2048
GW = 1024
NH = 8
IN_COLS = 8224
DFF = 5504
NFC = DFF // 128
MEM = 256
ALPHA = 2.0 ** 0.25
LAMBDA_INIT = 0.8 - 0.6 * math.exp(0.0)
SAME_ENGINE_SYNC = True
NDMASEM = 24


class T_:
    __slots__ = ("t", "w", "r")

    def __init__(self, t):
        self.t = t
        self.w = {}
        self.r = {}


class Sy:
    def __init__(self, nc, es):
        self.nc = nc
        self.es = es
        self.eng = {"pe": nc.tensor, "act": nc.scalar, "dve": nc.vector, "pool": nc.gpsimd, "sp": nc.sync}
        self.sem = {}
        self.cnt = {}
        self.known = {k: {} for k in self.eng}
        self.semobj = {}
        for k in self.eng:
            s = es.enter_context(nc.semaphore("s_" + k))
            self.sem[k] = s
            self.cnt[k] = 0
            self.semobj[id(s)] = s
        self.dsem = []
        self.dval = []
        for i in range(NDMASEM):
            s = es.enter_context(nc.semaphore("d%d" % i))
            self.dsem.append(s)
            self.dval.append(0)
            self.semobj[id(s)] = s
        self.di = 0
        self.nwait = 0
        self.uid = 0
        self.mute = False

    def sb(self, es, shape, dt, name=None):
        self.uid += 1
        return T_(es.enter_context(self.nc.sbuf_tensor("%s_%d" % (name or "t", self.uid), list(shape), dt)))

    def ps(self, es, shape, dt, name=None):
        self.uid += 1
        return T_(es.enter_context(self.nc.psum_tensor("%s_%d" % (name or "p", self.uid), list(shape), dt)))

    def pool(self, es, n, shape, dt, name=None):
        return Ring([self.sb(es, shape, dt, name) for _ in range(n)])

    def _wait(self, e, waits):
        E = self.eng[e]
        kn = self.known[e]
        for sid, v in waits.items():
            if kn.get(sid, 0) < v:
                E.wait_ge(self.semobj[sid], v)
                kn[sid] = v
                self.nwait += 1

    def _collect(self, e, reads, writes):
        waits = {}
        own = id(self.sem[e])

        def add(d):
            for sid, v in d.items():
                if sid == own and (e == "pe" or not SAME_ENGINE_SYNC):
                    continue
                if waits.get(sid, 0) < v:
                    waits[sid] = v

        for t in reads:
            add(t.w)
        for t in writes:
            add(t.w)
            add(t.r)
        return waits

    def _commit(self, reads, writes, sid, v):
        for t in reads:
            if t.r.get(sid, 0) < v:
                t.r[sid] = v
        for t in writes:
            t.w = {sid: v}
            t.r = {}

    def op(self, e, reads, writes, fn):
        if self.mute:
            return
        self._wait(e, self._collect(e, reads, writes))
        inst = fn(self.eng[e])
        self.cnt[e] += 1
        inst.then_inc(self.sem[e], 1)
        self._commit(reads, writes, id(self.sem[e]), self.cnt[e])

    def dma(self, q, out_ap, in_ap, reads=(), writes=(), **kw):
        if self.mute:
            return
        i = self.di
        self.di = (self.di + 1) % NDMASEM
        s = self.dsem[i]
        waits = self._collect(q, reads, writes)
        if self.dval[i] > 0:
            waits[id(s)] = max(waits.get(id(s), 0), self.dval[i])
        self._wait(q, waits)
        self.dval[i] += 16
        self.eng[q].dma_start(out=out_ap, in_=in_ap, **kw).then_inc(s, 16)
        self._commit(reads, writes, id(s), self.dval[i])

    def barrier(self):
        if self.mute:
            return
        allw = {}
        for k in self.eng:
            if self.cnt[k] > 0:
                allw[id(self.sem[k])] = self.cnt[k]
        for i in range(NDMASEM):
            if self.dval[i] > 0:
                allw[id(self.dsem[i])] = self.dval[i]
        for e in self.eng:
            w = dict(allw)
            w.pop(id(self.sem[e]), None)
            self._wait(e, w)
        for e in ("act", "dve", "pool"):
            if self.cnt[e] > 0:
                self._wait(e, {id(self.sem[e]): self.cnt[e]})


class Ring:
    def __init__(self, tiles):
        self.tiles = tiles
        self.i = 0

    def next(self):
        t = self.tiles[self.i]
        self.i = (self.i + 1) % len(self.tiles)
        return t


def act_copy2(e, dst, src):
    e.copy(out=dst.t[:, 0:4, :], in_=src.t[:, 0:4, :])
    return e.copy(out=dst.t[:, 4:8, :], in_=src.t[:, 4:8, :])


def bc_last(ap, n):
    sh = list(ap.shape)
    return ap.unsqueeze(len(sh)).to_broadcast(sh + [n])


def bc_mid(ap, n):
    sh = list(ap.shape)
    return ap.unsqueeze(1).to_broadcast([sh[0], n] + sh[1:])


class _Stop(Exception):
    pass


def build_program(seqs, dbg=False, upto=99, start=0, ext_in=(), bisect=0):
    T = sum(seqs)
    NS = len(seqs)
    soff = [sum(seqs[:i]) for i in range(NS)]
    segs = []
    for s, L in enumerate(seqs):
        n = max(1, L // 2048)
        sl = L // n
        for i in range(n):
            segs.append((soff[s] + i * sl, sl, i > 0, i < n - 1))

    nc = bass.Bass("TRN2", target_bir_lowering=False)
    kin = "ExternalInput"
    kint = "ExternalOutput" if dbg else "Internal"

    def din(name, shape, dt=F32):
        return nc.dram_tensor(name, list(shape), dt, kind=kin).ap()

    def dsc(name, shape, dt):
        if name in ext_in:
            return nc.dram_tensor(name, list(shape), dt, kind="ExternalInput").ap()
        return nc.dram_tensor(name, list(shape), dt, kind=("Internal" if name.startswith("w") else kint)).ap()

    x_d = din("x", [T, D])
    mem_d = din("mem", [NS * MEM, D])
    w_in_d = din("w_in", [D, IN_COLS])
    w_gate_d = din("w_gate", [D, 3 * D])
    w_kv_d = din("w_kv", [D, 2 * GW])
    wb_d = [din("wb%d" % i, [GW, D]) for i in range(3)]
    w_out_d = din("w_out", [D, D])
    w_up_d = din("w_up", [D, 2 * DFF])
    w_down_d = din("w_down", [DFF, D])
    gconv_d = din("gconv", [128, 24, 5])
    fconv_d = din("fconv", [128, 2 * NFC, 3])
    bgate_d = din("bgate", [128, 48])
    alog_d = din("alog", [128, 16])
    dtb_d = din("dtb", [128, 16])
    gnw_d = din("gnw", [128, 128])
    dlam_d = din("dlam", [128, 4, 128])
    dnw_d = din("dnw", [128, 256])
    ln_d = din("lnp", [128, 4, D])
    relb_d = din("relb", [32, 4])
    cm_d = din("cmask", [128, 7, 128])
    oh_d = din("oh", [32, 1280])

    y_d = nc.dram_tensor("y", [T, D], F32, kind="ExternalOutput").ap()

    wi_b = dsc("wi_b", [D, IN_COLS], BF16)
    wg_b = dsc("wg_b", [D, 3 * D], BF16)
    wkv_b = dsc("wkv_b", [D, 2 * GW], BF16)
    wb_b = [dsc("wb_b%d" % i, [GW, D], BF16) for i in range(3)]
    wo_b = dsc("wo_b", [D, D], BF16)
    wu_b = dsc("wu_b", [D, 2 * DFF], BF16)
    wd_b = dsc("wd_b", [DFF, D], BF16)
    xT = dsc("xT", [D, T], BF16)
    memT = dsc("memT", [D, NS * MEM], BF16)
    gqkvT = dsc("gqkvT", [3 * GW, T], F32)
    dqkT = dsc("dqkT", [2 * GW, T], BF16)
    cqT = dsc("cqT", [GW, T], BF16)
    gatesT = dsc("gatesT", [3 * D, T], BF16)
    gz = dsc("gz", [T, GW], BF16)
    gab = dsc("gab", [T, 32], F32)
    dv = dsc("dv", [T, GW], BF16)
    kmT = dsc("kmT", [GW, NS * MEM], BF16)
    vm = dsc("vm", [NS * MEM, GW], BF16)
    qTn = dsc("qTn", [GW, T], BF16)
    kTn = dsc("kTn", [GW, T], BF16)
    ktok = dsc("ktok", [T, GW], BF16)
    vtok = dsc("vtok", [T, GW], BF16)
    odir = dsc("odir", [2, T, GW], F32)
    ybT = [dsc("ybT%d" % i, [GW, T], BF16) for i in range(3)]
    mT = dsc("mT", [D, T], BF16)
    x1 = dsc("x1", [T, D], F32)
    x1T = dsc("x1T", [D, T], BF16)
    hT = dsc("hT", [DFF, T], BF16)
    ypart = dsc("ypart", [T, GW], F32)
    tvec = dsc("tvec", [4, 1280], BF16)

    with ExitStack() as es0:
        S = Sy(nc, es0)
        try:
            cm = S.sb(es0, [128, 7, 128], F32, "cm")
            S.dma("sp", cm.t[:], cm_d[:, :, :], writes=[cm])
            ident = cm.t[:, 0, :]
            identb = S.sb(es0, [128, 128], BF16, "identb")
            onesb = S.sb(es0, [128, 128], BF16, "onesb")
            antib = S.sb(es0, [128, 128], BF16, "antib")
            S.op("dve", [cm], [identb], lambda e: e.tensor_copy(out=identb.t[:], in_=cm.t[:, 0, :]))
            S.op("dve", [cm], [onesb], lambda e: e.tensor_copy(out=onesb.t[:], in_=cm.t[:, 1, :]))
            S.op("dve", [cm], [antib], lambda e: e.tensor_copy(out=antib.t[:], in_=cm.t[:, 6, :]))

            S.mute = start > 0
            with ExitStack() as es:
                CB = 2048
                fin = S.pool(es, 3, [128, CB], F32, "wc_in")
                fout = S.pool(es, 3, [128, CB], BF16, "wc_out")
                k = 0
                for src, dst in ([(w_in_d, wi_b), (w_gate_d, wg_b), (w_kv_d, wkv_b)] +
                                 [(wb_d[i], wb_b[i]) for i in range(3)] +
                                 [(w_out_d, wo_b), (w_up_d, wu_b), (w_down_d, wd_b)]):
                    R, C = src.shape
                    for r0 in range(0, R, 128):
                        for c0 in range(0, C, CB):
                            cw = min(CB, C - c0)
                            a = fin.next()
                            b = fout.next()
                            S.dma("sp", a.t[:, 0:cw], src[r0:r0 + 128, c0:c0 + cw], writes=[a])
                            eng = ("dve", "act", "pool")[k % 3]
                            k += 1
                            if eng == "act":
                                S.op(eng, [a], [b], lambda e: e.copy(out=b.t[:, 0:cw], in_=a.t[:, 0:cw]))
                            else:
                                S.op(eng, [a], [b], lambda e: e.tensor_copy(out=b.t[:, 0:cw], in_=a.t[:, 0:cw]))
                            S.dma("pool", dst[r0:r0 + 128, c0:c0 + cw], b.t[:, 0:cw], reads=[b])
            S.barrier()
            if upto == 0:
                raise _Stop()
            S.mute = start > 1

            def xpose_phase(src, dst, ntok, sup):
                with ExitStack() as es:
                    xin = S.pool(es, 2, [128, sup // 128, D], F32, "xin")
                    xo = S.pool(es, 2, [128, 16, sup], BF16, "xo")
                    pp = Ring([S.ps(es, [128, 512], F32, "xp") for _ in range(4)])
                    ev = 0
                    for t0 in range(0, ntok, sup):
                        a = xin.next()
                        o = xo.next()
                        S.dma("sp", a.t[:], src[t0:t0 + sup, :].rearrange("(j p) d -> p j d", p=128), writes=[a])
                        for kc in range(16):
                            for j0 in range(0, sup // 128, 4):
                                nj = min(4, sup // 128 - j0)
                                p = pp.next()

                                def f(e, p=p, j0=j0, nj=nj, kc=kc):
                                    for j in range(nj):
                                        i = e.transpose(p.t[:, j * 128:(j + 1) * 128],
                                                        a.t[:, j0 + j, kc * 128:(kc + 1) * 128], ident)
                                    return i
                                S.op("pe", [a, cm], [p], f)
                                eng = ("dve", "act")[ev % 2]
                                ev += 1
                                if eng == "act":
                                    S.op(eng, [p], [o], lambda e: e.copy(out=o.t[:, kc, j0 * 128:(j0 + nj) * 128],
                                                                        in_=p.t[:, 0:nj * 128]))
                                else:
                                    S.op(eng, [p], [o], lambda e: e.tensor_copy(out=o.t[:, kc, j0 * 128:(j0 + nj) * 128],
                                                                               in_=p.t[:, 0:nj * 128]))
                        S.dma("pool", dst[:, t0:t0 + sup].rearrange("(c p) t -> p c t", p=128), o.t[:], reads=[o])

            xpose_phase(x_d, xT, T, 512 if T % 512 == 0 else 128)
            S.barrier2 = None
            for _e in (1,):
                S.barrier()
            xpose_phase(mem_d, memT, NS * MEM, MEM)
            S.barrier()
            if upto == 1:
                raise _Stop()
            S.mute = start > 2

            with ExitStack() as es:
                LM = max(s[1] for s in segs)
                xs_pool = S.pool(es, 1, [128, 16, LM], BF16, "xseg")
                wc_pool = S.pool(es, 2, [128, 16, 512], BF16, "wcb")
                stf = S.pool(es, 2, [128, LM], F32, "stf")
                stb = S.pool(es, 2, [128, LM], BF16, "stb")
                stt = S.pool(es, 2, [128, 4, 512], BF16, "stt")
                stg = S.pool(es, 2, [128, 32], F32, "stg")
                pp = Ring([S.ps(es, [128, 512], F32, "pj") for _ in range(6)])
                bg = S.sb(es, [128, 48], F32, "bg")
                S.dma("sp", bg.t[:], bgate_d[:, :], writes=[bg])
                evc = [0]

                def load_w(wsrc, c0, cw):
                    w = wc_pool.next()
                    S.dma("sp", w.t[:, :, 0:cw], wsrc[:, c0:c0 + cw].rearrange("(c p) n -> p c n", p=128), writes=[w])
                    return w

                def fm_block(xs, t0, L, wsrc, c0, ncols, dst, r0, dt, sig_chunk0=None):
                    for cb in range(0, ncols, 512):
                        cw = min(512, ncols - cb)
                        w = load_w(wsrc, c0 + cb, cw)
                        for fc in range(cw // 128):
                            st = (stf if dt == F32 else stb).next()
                            for tt in range(0, L, 512):
                                tw = min(512, L - tt)
                                p = pp.next()

                                def f(e, p=p, tt=tt, tw=tw, fc=fc):
                                    for kc in range(16):
                                        i = e.matmul(p.t[:, 0:tw], lhsT=w.t[:, kc, fc * 128:(fc + 1) * 128],
                                                     rhs=xs.t[:, kc, tt:tt + tw], start=(kc == 0), stop=(kc == 15))
                                    return i
                                S.op("pe", [w, xs], [p], f)
                                if sig_chunk0 is not None:
                                    ch = sig_chunk0 + (cb // 128) + fc
                                    S.op("act", [p, bg], [st], lambda e: e.activation(
                                        out=st.t[:, tt:tt + tw], in_=p.t[:, 0:tw], func=AF.Sigmoid,
                                        bias=bg.t[:, ch:ch + 1], scale=1.0))
                                else:
                                    evc[0] += 1
                                    if evc[0] % 2:
                                        S.op("dve", [p], [st], lambda e: e.tensor_copy(out=st.t[:, tt:tt + tw], in_=p.t[:, 0:tw]))
                                    else:
                                        S.op("act", [p], [st], lambda e: e.copy(out=st.t[:, tt:tt + tw], in_=p.t[:, 0:tw]))
                            rr = r0 + cb + fc * 128
                            S.dma("pool", dst[rr:rr + 128, t0:t0 + L], st.t[:, 0:L], reads=[st])

                def tm_block(xs, t0, L, wsrc, c0, ncols, dst, dc0, silu):
                    for cb in range(0, ncols, 512):
                        cw = min(512, ncols - cb)
                        w = load_w(wsrc, c0 + cb, cw)
                        for tg in range(0, L, 512):
                            ng = min(4, (L - tg) // 128)
                            st = stt.next()
                            for j in range(ng):
                                tt = tg + j * 128
                                p = pp.next()

                                def f(e, p=p, tt=tt):
                                    for kc in range(16):
                                        i = e.matmul(p.t[:, 0:cw], lhsT=xs.t[:, kc, tt:tt + 128],
                                                     rhs=w.t[:, kc, 0:cw], start=(kc == 0), stop=(kc == 15))
                                    return i
                                S.op("pe", [w, xs], [p], f)
                                if silu:
                                    S.op("act", [p], [st], lambda e: e.activation(out=st.t[:, j, 0:cw], in_=p.t[:, 0:cw], func=AF.Silu))
                                else:
                                    S.op("dve", [p], [st], lambda e: e.tensor_copy(out=st.t[:, j, 0:cw], in_=p.t[:, 0:cw]))
                            S.dma("pool", dst[t0 + tg:t0 + tg + ng * 128, dc0 + cb:dc0 + cb + cw].rearrange("(j p) n -> p j n", p=128),
                                  st.t[:, 0:ng, 0:cw], reads=[st])

                for (t0, L, _l, _r) in segs:
                    xs = xs_pool.next()
                    S.dma("sp", xs.t[:, :, 0:L], xT[:, t0:t0 + L].rearrange("(c p) t -> p c t", p=128), writes=[xs])
                    fm_block(xs, t0, L, wi_b, 0, 3072, gqkvT, 0, F32)
                    fm_block(xs, t0, L, wi_b, 4128, 2048, dqkT, 0, BF16)
                    fm_block(xs, t0, L, wi_b, 7200, 1024, cqT, 0, BF16)
                    fm_block(xs, t0, L, wg_b, 0, 3 * D, gatesT, 0, BF16, sig_chunk0=0)
                    tm_block(xs, t0, L, wi_b, 3072, 1024, gz, 0, True)
                    tm_block(xs, t0, L, wi_b, 6176, 1024, dv, 0, False)
                    w = wc_pool.next()
                    S.dma("sp", w.t[:, :, 0:32], wi_b[:, 4096:4128].rearrange("(c p) n -> p c n", p=128), writes=[w])
                    for tt in range(0, L, 128):
                        p = pp.next()
                        sg = stg.next()

                        def f(e, p=p, tt=tt):
                            for kc in range(16):
                                i = e.matmul(p.t[:, 0:32], lhsT=xs.t[:, kc, tt:tt + 128], rhs=w.t[:, kc, 0:32],
                                             start=(kc == 0), stop=(kc == 15))
                            return i
                        S.op("pe", [w, xs], [p], f)
                        S.op("dve", [p], [sg], lambda e: e.tensor_copy(out=sg.t[:], in_=p.t[:, 0:32]))
                        S.dma("pool", gab[t0 + tt:t0 + tt + 128, :], sg.t[:], reads=[sg])
                xs = xs_pool.next()
                NM = NS * MEM
                S.dma("sp", xs.t[:, :, 0:NM], memT[:, :].rearrange("(c p) t -> p c t", p=128), writes=[xs])
                fm_block(xs, 0, NM, wkv_b, 0, 1024, kmT, 0, BF16)
                tm_block(xs, 0, NM, wkv_b, 1024, 1024, vm, 0, False)
            S.barrier()
            if upto == 2:
                raise _Stop()
            S.mute = start > 3

            with ExitStack() as es:
                SM = max(seqs)
                gcw = S.sb(es, [128, 24, 5], F32, "gcw")
                S.dma("sp", gcw.t[:], gconv_d[:, :, :], writes=[gcw])
                xin = S.pool(es, 2, [128, SM + 4], F32, "cxin")
                acc = S.pool(es, 1, [128, SM], F32, "cacc")
                ys = S.pool(es, 2, [128, SM], F32, "cys")
                sq = S.pool(es, 2, [128, SM], BF16, "csq")
                rn = S.pool(es, 2, [128, 512], F32, "crn")
                yn = S.pool(es, 2, [128, SM], BF16, "cyn")
                ynf = S.pool(es, 1, [128, SM], F32, "cynf")
                tst = S.pool(es, 2, [128, 4, 128], BF16, "ctst")
                pss = Ring([S.ps(es, [128, 512], F32, "cps") for _ in range(2)])
                ptr = Ring([S.ps(es, [128, 512], F32, "cpt") for _ in range(2)])
                for s in range(NS):
                    t0, L = soff[s], seqs[s]
                    for c in range(24):
                        a = xin.next()
                        S.op("pool", [], [a], lambda e: e.memset(a.t[:, 0:2], 0.0))
                        S.op("pool", [], [a], lambda e: e.memset(a.t[:, L + 2:L + 4], 0.0))
                        S.dma("sp", a.t[:, 2:L + 2], gqkvT[c * 128:(c + 1) * 128, t0:t0 + L], writes=[a])
                        ac = acc.next()
                        S.op("dve", [a, gcw], [ac], lambda e: e.tensor_scalar(
                            out=ac.t[:, 0:L], in0=a.t[:, 0:L], scalar1=gcw.t[:, c, 0:1], scalar2=None, op0=ALU.mult))
                        for j in range(1, 5):
                            S.op("dve", [a, gcw, ac], [ac], lambda e: e.scalar_tensor_tensor(
                                out=ac.t[:, 0:L], in0=a.t[:, j:j + L], scalar=gcw.t[:, c, j:j + 1], in1=ac.t[:, 0:L],
                                op0=ALU.mult, op1=ALU.add))
                        y = ys.next()
                        S.op("act", [ac], [y], lambda e: e.activation(out=y.t[:, 0:L], in_=ac.t[:, 0:L], func=AF.Silu))
                        if c < 16:
                            q2 = sq.next()
                            S.op("pool", [y], [q2], lambda e: e.tensor_tensor(out=q2.t[:, 0:L], in0=y.t[:, 0:L], in1=y.t[:, 0:L], op=ALU.mult))
                            o = yn.next()
                            for tt in range(0, L, 512):
                                tw = min(512, L - tt)
                                p = pss.next()
                                S.op("pe", [q2, onesb], [p], lambda e: e.matmul(p.t[:, 0:tw], lhsT=onesb.t[:], rhs=q2.t[:, tt:tt + tw], start=True, stop=True))
                                r = rn.next()
                                S.op("act", [p], [r], lambda e: e.activation(out=r.t[:, 0:tw], in_=p.t[:, 0:tw], func=AF.Sqrt, bias=1e-6, scale=1.0))
                                S.op("dve", [r], [r], lambda e: e.reciprocal(out=r.t[:, 0:tw], in_=r.t[:, 0:tw]))
                                sc = (128.0 ** -0.5) if c < 8 else 1.0
                                S.op("dve", [r, y], [o], lambda e: e.scalar_tensor_tensor(
                                    out=o.t[:, tt:tt + tw], in0=y.t[:, tt:tt + tw], scalar=sc, in1=r.t[:, 0:tw], op0=ALU.mult, op1=ALU.mult))
                            dstT = qTn if c < 8 else kTn
                            h = c % 8
                            S.dma("pool", dstT[h * 128:(h + 1) * 128, t0:t0 + L], o.t[:, 0:L], reads=[o])
                        if c >= 8:
                            h = c % 8
                            if c < 16:
                                yf = ynf.next()
                                S.op("pool", [o], [yf], lambda e: e.tensor_copy(out=yf.t[:, 0:L], in_=o.t[:, 0:L]))
                            else:
                                yf = y
                            dst = ktok if c < 16 else vtok
                            for tg in range(0, L, 512):
                                ng = min(4, (L - tg) // 128)
                                p = ptr.next()

                                def f(e, p=p, tg=tg, ng=ng, yf=yf):
                                    for j in range(ng):
                                        i = e.transpose(p.t[:, j * 128:(j + 1) * 128], yf.t[:, tg + j * 128:tg + (j + 1) * 128], ident)
                                    return i
                                S.op("pe", [yf, cm], [p], f)
                                st = tst.next()
                                S.op("act", [p], [st], lambda e: e.copy(out=st.t[:, 0:ng, :], in_=p.t[:, 0:ng * 128].rearrange("p (j d) -> p j d", d=128)))
                                S.dma("pool", dst[t0 + tg:t0 + tg + ng * 128, h * 128:(h + 1) * 128].rearrange("(j p) d -> p j d", p=128),
                                      st.t[:, 0:ng, :], reads=[st])
            S.barrier()
            if upto == 3:
                raise _Stop()
            S.mute = start > 4

            with ExitStack() as es:
                alog = S.sb(es, [128, 16], F32, "alog")
                dtb = S.sb(es, [128, 16], F32, "dtb")
                S.dma("sp", alog.t[:], alog_d[:, :], writes=[alog])
                S.dma("sp", dtb.t[:], dtb_d[:, :], writes=[dtb])
                nea = S.sb(es, [128, 16], F32, "nea")
                S.op("act", [alog], [nea], lambda e: e.activation(out=nea.t[:], in_=alog.t[:], func=AF.Exp))
                S.op("dve", [nea], [nea], lambda e: e.tensor_scalar(out=nea.t[:], in0=nea.t[:], scalar1=-1.0, scalar2=None, op0=ALU.mult))
                qT_p = S.pool(es, 2, [128, 8, 128], BF16, "gqT")
                kT_p = S.pool(es, 2, [128, 8, 128], BF16, "gkT")
                kt_p = S.pool(es, 2, [128, 8, 128], BF16, "gkt")
                vt_p = S.pool(es, 2, [128, 8, 128], BF16, "gvt")
                ab_p = S.pool(es, 2, [128, 32], F32, "gab")
                sm = {n: S.sb(es, [128, 8], F32, "g" + n) for n in
                      ("g", "beta", "nbeta", "gc", "gl", "egc", "egl", "ekd", "bg", "tmp", "ngc")}
                dg = S.sb(es, [128, 8, 128], F32, "dg")
                Dm = S.sb(es, [128, 8, 128], F32, "Dm")
                DmT = S.sb(es, [128, 8, 128], F32, "DmT")
                Ei = S.sb(es, [128, 8, 128], F32, "Ei")
                Es = S.sb(es, [128, 8, 128], F32, "Es")
                EiT = S.sb(es, [128, 8, 128], F32, "EiT")
                Pm = Ring([S.sb(es, [128, 8, 128], F32, "Pm") for _ in range(2)])
                PTm = Ring([S.sb(es, [128, 8, 128], F32, "PTm") for _ in range(2)])
                TT = S.sb(es, [128, 8, 128], F32, "TT")
                QKmT = S.sb(es, [128, 8, 128], BF16, "QKmT")
                rv = S.sb(es, [128, 8, 128], F32, "rv")
                rk = S.sb(es, [128, 8, 128], F32, "rk")
                kdec = S.sb(es, [128, 8, 128], BF16, "kdec")
                wv = S.sb(es, [128, 8, 128], F32, "wv")
                wkT = S.sb(es, [128, 8, 128], BF16, "wkT")
                u_b = S.sb(es, [128, 8, 128], BF16, "u_b")
                St = S.sb(es, [128, 8, 128], F32, "St")
                Stmp = S.sb(es, [128, 8, 128], F32, "Stmp")
                S_b = S.sb(es, [128, 8, 128], BF16, "S_b")
                o1 = S.sb(es, [128, 8, 128], F32, "o1")
                o2 = S.sb(es, [128, 8, 128], F32, "o2")
                oo = S.pool(es, 2, [128, 8, 128], F32, "oo")
                PA = S.ps(es, [128, 8, 128], F32, "PA")
                PB = S.ps(es, [128, 8, 128], F32, "PB")
                PC = S.ps(es, [128, 8, 128], F32, "PC")
                PD = S.ps(es, [128, 512], F32, "PD")
                ones32 = cm.t[:, 1, :]

                def flat(t):
                    return t.t[:].rearrange("p h d -> p (h d)")

                for s in range(NS):
                    t0s, L = soff[s], seqs[s]
                    ntile = L // 128
                    for di in range(2):
                        if di == 0:
                            CS, MI_, MS_, MIT_ = cm.t[:, 3, :], cm.t[:, 2, :], cm.t[:, 4, :], cm.t[:, 3, :]
                        else:
                            CS, MI_, MS_, MIT_ = cm.t[:, 2, :], cm.t[:, 3, :], cm.t[:, 5, :], cm.t[:, 2, :]
                        S.op("pool", [], [St], lambda e: e.memset(St.t[:], 0.0))
                        S.op("pool", [], [S_b], lambda e: e.memset(S_b.t[:], 0.0))
                        order = range(ntile) if di == 0 else range(ntile - 1, -1, -1)
                        for j in order:
                            tt = t0s + j * 128
                            qT_, kT_, kt_, vt_, ab = qT_p.next(), kT_p.next(), kt_p.next(), vt_p.next(), ab_p.next()
                            S.dma("sp", qT_.t[:], qTn[:, tt:tt + 128].rearrange("(h p) t -> p h t", p=128), writes=[qT_])
                            S.dma("sp", kT_.t[:], kTn[:, tt:tt + 128].rearrange("(h p) t -> p h t", p=128), writes=[kT_])
                            S.dma("sp", kt_.t[:], ktok[tt:tt + 128, :].rearrange("p (h d) -> p h d", d=128), writes=[kt_])
                            S.dma("sp", vt_.t[:], vtok[tt:tt + 128, :].rearrange("p (h d) -> p h d", d=128), writes=[vt_])
                            S.dma("sp", ab.t[:], gab[tt:tt + 128, :], writes=[ab])
                            g, beta, nbeta, gc, gl = sm["g"], sm["beta"], sm["nbeta"], sm["gc"], sm["gl"]
                            egc, egl, ekd, bgm, tmp, ngc = sm["egc"], sm["egl"], sm["ekd"], sm["bg"], sm["tmp"], sm["ngc"]
                            a0 = di * 8
                            S.op("dve", [ab, dtb], [tmp], lambda e: e.tensor_tensor(out=tmp.t[:], in0=ab.t[:, a0:a0 + 8], in1=dtb.t[:, a0:a0 + 8], op=ALU.add))
                            S.op("act", [tmp], [tmp], lambda e: e.activation(out=tmp.t[:], in_=tmp.t[:], func=AF.Exp))
                            S.op("act", [tmp], [tmp], lambda e: e.activation(out=tmp.t[:], in_=tmp.t[:], func=AF.Ln, bias=1.0, scale=1.0))
                            S.op("dve", [tmp, nea], [g], lambda e: e.tensor_tensor(out=g.t[:], in0=tmp.t[:], in1=nea.t[:, a0:a0 + 8], op=ALU.mult))
                            S.op("act", [ab], [beta], lambda e: e.activation(out=beta.t[:], in_=ab.t[:, 16 + a0:24 + a0], func=AF.Sigmoid))
                            S.op("dve", [beta], [nbeta], lambda e: e.tensor_scalar(out=nbeta.t[:], in0=beta.t[:], scalar1=-1.0, scalar2=None, op0=ALU.mult))

                            def f(e):
                                e.matmul(PD.t[:, 0:8], lhsT=CS, rhs=g.t[:], start=True, stop=True)
                                return e.matmul(PD.t[:, 8:16], lhsT=ones32, rhs=g.t[:], start=True, stop=True)
                            S.op("pe", [g, cm], [PD], f)
                            S.op("dve", [PD], [gc], lambda e: e.tensor_copy(out=gc.t[:], in_=PD.t[:, 0:8]))
                            S.op("dve", [PD], [gl], lambda e: e.tensor_copy(out=gl.t[:], in_=PD.t[:, 8:16]))
                            S.op("dve", [gc], [ngc], lambda e: e.tensor_scalar(out=ngc.t[:], in0=gc.t[:], scalar1=-1.0, scalar2=None, op0=ALU.mult))
                            S.op("act", [gc], [egc], lambda e: e.activation(out=egc.t[:], in_=gc.t[:], func=AF.Exp))
                            S.op("act", [gl], [egl], lambda e: e.activation(out=egl.t[:], in_=gl.t[:], func=AF.Exp))
                            S.op("dve", [gl, gc], [ekd], lambda e: e.tensor_tensor(out=ekd.t[:], in0=gl.t[:], in1=gc.t[:], op=ALU.subtract))
                            S.op("act", [ekd], [ekd], lambda e: e.activation(out=ekd.t[:], in_=ekd.t[:], func=AF.Exp))
                            S.op("dve", [beta, egc], [bgm], lambda e: e.tensor_tensor(out=bgm.t[:], in0=beta.t[:], in1=egc.t[:], op=ALU.mult))
                            if bisect == 1:
                                S.mute = True
                            S.op("dve", [ngc, cm], [dg], lambda e: e.tensor_tensor(
                                out=dg.t[:], in0=bc_mid(cm.t[:, 0, :], 8), in1=bc_last(ngc.t[:], 128), op=ALU.mult))

                            def f(e):
                                for hh in range(0, 8, 4):
                                    i = e.matmul(PA.t[:, hh:hh + 4, :], lhsT=ones32, rhs=dg.t[:, hh:hh + 4, :], start=True, stop=True)
                                return i
                            S.op("pe", [dg, cm], [PA], f)
                            S.op("dve", [PA, gc], [Dm], lambda e: e.tensor_tensor(out=Dm.t[:], in0=PA.t[:], in1=bc_last(gc.t[:], 128), op=ALU.add))
                            S.op("pool", [Dm], [DmT], lambda e: e.tensor_scalar(out=DmT.t[:], in0=Dm.t[:], scalar1=0.0, scalar2=None, op0=ALU.max))
                            S.op("dve", [Dm], [Dm], lambda e: e.tensor_scalar(out=Dm.t[:], in0=Dm.t[:], scalar1=0.0, scalar2=None, op0=ALU.min))
                            S.op("act", [Dm], [Ei], lambda e: e.activation(out=Ei.t[:], in_=Dm.t[:], func=AF.Exp))
                            S.op("act", [DmT], [EiT], lambda e: e.activation(out=EiT.t[:], in_=DmT.t[:], func=AF.Exp, scale=-1.0))
                            S.op("dve", [Ei, cm], [Es], lambda e: e.tensor_tensor(out=Es.t[:], in0=Ei.t[:], in1=bc_mid(MS_, 8), op=ALU.mult))
                            S.op("dve", [EiT, cm], [EiT], lambda e: e.tensor_tensor(out=EiT.t[:], in0=EiT.t[:], in1=bc_mid(MIT_, 8), op=ALU.mult))
                            if bisect == 2:
                                S.mute = True

                            def f(e):
                                for h in range(8):
                                    i = e.matmul(PB.t[:, h, :], lhsT=kT_.t[:, h, :], rhs=kT_.t[:, h, :], start=True, stop=True)
                                return i
                            S.op("pe", [kT_], [PB], f)

                            def f(e):
                                for h in range(8):
                                    i = e.matmul(PC.t[:, h, :], lhsT=kT_.t[:, h, :], rhs=qT_.t[:, h, :], start=True, stop=True)
                                return i
                            S.op("pe", [kT_, qT_], [PC], f)
                            if bisect == 31:
                                S.mute = True
                            P0 = Pm.next()
                            PT0 = PTm.next()
                            S.op("dve", [PB, Es], [P0], lambda e: e.tensor_tensor(out=P0.t[:], in0=PB.t[:], in1=Es.t[:], op=ALU.mult))
                            S.op("dve", [P0, nbeta], [P0], lambda e: e.tensor_tensor(out=P0.t[:], in0=P0.t[:], in1=bc_last(nbeta.t[:], 128), op=ALU.mult))
                            S.op("dve", [PC, EiT], [QKmT], lambda e: e.tensor_tensor(out=QKmT.t[:], in0=PC.t[:], in1=EiT.t[:], op=ALU.mult))
                            if bisect == 32:
                                S.mute = True

                            def f(e):
                                for h in range(8):
                                    i = e.matmul(PA.t[:, h, :], lhsT=P0.t[:, h, :], rhs=ident, start=True, stop=True)
                                return i
                            S.op("pe", [P0, cm], [PA], f)
                            if bisect == 33:
                                S.mute = True
                            S.op("act", [PA], [PT0], lambda e: act_copy2(e, PT0, PA))
                            S.op("dve", [PT0, cm], [TT], lambda e: e.tensor_tensor(out=TT.t[:], in0=PT0.t[:], in1=bc_mid(cm.t[:, 0, :], 8), op=ALU.add))
                            if bisect == 3:
                                S.mute = True
                            Pc, PTc = P0, PT0
                            for lvl in range(1, 7):
                                Pn = Pm.next()
                                PTn = PTm.next()

                                def f(e, Pc=Pc, PTc=PTc):
                                    for h in range(8):
                                        i = e.matmul(PA.t[:, h, :], lhsT=PTc.t[:, h, :], rhs=Pc.t[:, h, :], start=True, stop=True)
                                    return i
                                S.op("pe", [Pc, PTc], [PA], f)
                                S.op("act", [PA], [Pn], lambda e: act_copy2(e, Pn, PA))
                                if lvl < 6:
                                    def f(e, Pc=Pc, PTc=PTc):
                                        for h in range(8):
                                            i = e.matmul(PB.t[:, h, :], lhsT=Pc.t[:, h, :], rhs=PTc.t[:, h, :], start=True, stop=True)
                                        return i
                                    S.op("pe", [Pc, PTc], [PB], f)
                                    S.op("dve", [PB], [PTn], lambda e: e.tensor_copy(out=PTn.t[:], in_=PB.t[:]))

                                def f(e, Pn=Pn):
                                    for h in range(8):
                                        i = e.matmul(PC.t[:, h, :], lhsT=Pn.t[:, h, :], rhs=TT.t[:, h, :], start=True, stop=True)
                                    return i
                                S.op("pe", [Pn, TT], [PC], f)
                                S.op("dve", [PC, TT], [TT], lambda e: e.tensor_tensor(out=TT.t[:], in0=PC.t[:], in1=TT.t[:], op=ALU.add))
                                Pc, PTc = Pn, PTn
                            if bisect == 4:
                                S.mute = True
                            S.op("pool", [vt_, beta], [rv], lambda e: e.tensor_tensor(out=rv.t[:], in0=vt_.t[:], in1=bc_last(beta.t[:], 128), op=ALU.mult))
                            S.op("pool", [kt_, bgm], [rk], lambda e: e.tensor_tensor(out=rk.t[:], in0=kt_.t[:], in1=bc_last(bgm.t[:], 128), op=ALU.mult))
                            S.op("pool", [kt_, ekd], [kdec], lambda e: e.tensor_tensor(out=kdec.t[:], in0=kt_.t[:], in1=bc_last(ekd.t[:], 128), op=ALU.mult))

                            def f(e):
                                for h in range(8):
                                    i = e.matmul(PA.t[:, h, :], lhsT=TT.t[:, h, :], rhs=rv.t[:, h, :], start=True, stop=True)
                                return i
                            S.op("pe", [TT, rv], [PA], f)

                            def f(e):
                                for h in range(8):
                                    i = e.matmul(PB.t[:, h, :], lhsT=rk.t[:, h, :], rhs=TT.t[:, h, :], start=True, stop=True)
                                return i
                            S.op("pe", [TT, rk], [PB], f)
                            S.op("act", [PA], [wv], lambda e: act_copy2(e, wv, PA))
                            S.op("dve", [PB], [wkT], lambda e: e.tensor_copy(out=wkT.t[:], in_=PB.t[:]))
                            if bisect == 5:
                                S.mute = True

                            def f(e):
                                for h in range(8):
                                    i = e.matmul(PA.t[:, h, :], lhsT=wkT.t[:, h, :], rhs=S_b.t[:, h, :], start=True, stop=True)
                                return i
                            S.op("pe", [wkT, S_b], [PA], f)

                            def f(e):
                                for h in range(8):
                                    i = e.matmul(PB.t[:, h, :], lhsT=qT_.t[:, h, :], rhs=S_b.t[:, h, :], start=True, stop=True)
                                return i
                            S.op("pe", [qT_, S_b], [PB], f)
                            S.op("dve", [PA, wv], [u_b], lambda e: e.tensor_tensor(out=u_b.t[:], in0=wv.t[:], in1=PA.t[:], op=ALU.subtract))
                            S.op("dve", [PB, egc], [o1], lambda e: e.tensor_tensor(out=o1.t[:], in0=PB.t[:], in1=bc_last(egc.t[:], 128), op=ALU.mult))

                            def f(e):
                                for h in range(8):
                                    i = e.matmul(PC.t[:, h, :], lhsT=QKmT.t[:, h, :], rhs=u_b.t[:, h, :], start=True, stop=True)
                                return i
                            S.op("pe", [QKmT, u_b], [PC], f)

                            def f(e):
                                for h in range(8):
                                    i = e.matmul(PA.t[:, h, :], lhsT=kdec.t[:, h, :], rhs=u_b.t[:, h, :], start=True, stop=True)
                                return i
                            S.op("pe", [kdec, u_b], [PA], f)
                            S.op("act", [PC], [o2], lambda e: act_copy2(e, o2, PC))
                            ot = oo.next()
                            S.op("pool", [o1, o2], [ot], lambda e: e.tensor_tensor(out=ot.t[:], in0=o1.t[:], in1=o2.t[:], op=ALU.add))
                            S.dma("pool", odir[di, tt:tt + 128, :], flat(ot), reads=[ot])
                            S.op("pool", [St, egl], [Stmp], lambda e: e.tensor_tensor(out=Stmp.t[:], in0=St.t[:], in1=bc_last(egl.t[:], 128), op=ALU.mult))
                            S.op("dve", [PA, Stmp], [St], lambda e: e.tensor_tensor(out=St.t[:], in0=PA.t[:], in1=Stmp.t[:], op=ALU.add))
                            S.op("act", [St], [S_b], lambda e: e.copy(out=S_b.t[:], in_=St.t[:]))
            S.barrier()
            if upto == 4:
                raise _Stop()
            S.mute = start > 5

            def tm_to_fm(es_, name):
                ptr = Ring([S.ps(es_, [128, 512], F32, name + "pt") for _ in range(2)])
                return ptr

            with ExitStack() as es:
                gnw = S.sb(es, [128, 128], F32, "gnw")
                S.dma("sp", gnw.t[:], gnw_d[:, :], writes=[gnw])
                of_p = S.pool(es, 2, [128, 8, 128], F32, "of")
                ob_p = S.pool(es, 2, [128, 8, 128], F32, "ob")
                z_p = S.pool(es, 2, [128, 8, 128], BF16, "zz")
                sq_p = S.pool(es, 2, [128, 8, 128], F32, "gsq")
                ss_p = S.pool(es, 2, [128, 8], F32, "gss")
                yg_p = S.pool(es, 2, [128, 8, 128], F32, "yg")
                stg_p = S.pool(es, 2, [128, 8, 512], BF16, "ygst")
                ptr = Ring([S.ps(es, [128, 512], F32, "gpt") for _ in range(4)])
                sup = 512 if all(L % 512 == 0 for L in seqs) else 128
                for tg in range(0, T, sup):
                    stg = stg_p.next()
                    for jj in range(sup // 128):
                        tt = tg + jj * 128
                        of, ob, zt = of_p.next(), ob_p.next(), z_p.next()
                        S.dma("sp", of.t[:], odir[0, tt:tt + 128, :].rearrange("p (h d) -> p h d", d=128), writes=[of])
                        S.dma("sp", ob.t[:], odir[1, tt:tt + 128, :].rearrange("p (h d) -> p h d", d=128), writes=[ob])
                        S.dma("sp", zt.t[:], gz[tt:tt + 128, :].rearrange("p (h d) -> p h d", d=128), writes=[zt])
                        S.op("pool", [of, ob], [of], lambda e: e.tensor_tensor(out=of.t[:], in0=of.t[:], in1=ob.t[:], op=ALU.add))
                        sqt, sst, yg = sq_p.next(), ss_p.next(), yg_p.next()
                        S.op("act", [of], [sqt], lambda e: e.activation(out=sqt.t[:], in_=of.t[:], func=AF.Square))
                        S.op("dve", [sqt], [sst], lambda e: e.tensor_reduce(out=sst.t[:], in_=sqt.t[:], axis=AX.X, op=ALU.add))
                        S.op("act", [sst], [sst], lambda e: e.activation(out=sst.t[:], in_=sst.t[:], func=AF.Sqrt, bias=1e-6, scale=1.0 / 128.0))
                        S.op("dve", [sst], [sst], lambda e: e.reciprocal(out=sst.t[:], in_=sst.t[:]))
                        S.op("dve", [of, sst], [yg], lambda e: e.tensor_tensor(out=yg.t[:], in0=of.t[:], in1=bc_last(sst.t[:], 128), op=ALU.mult))
                        S.op("dve", [yg, gnw], [yg], lambda e: e.tensor_tensor(out=yg.t[:], in0=yg.t[:], in1=bc_mid(gnw.t[:], 8), op=ALU.mult))
                        S.op("dve", [yg, zt], [yg], lambda e: e.tensor_tensor(out=yg.t[:], in0=yg.t[:], in1=zt.t[:], op=ALU.mult))
                        for h0 in range(0, 8, 4):
                            p = ptr.next()

                            def f(e, p=p, h0=h0, yg=yg):
                                for h in range(4):
                                    i = e.transpose(p.t[:, h * 128:(h + 1) * 128], yg.t[:, h0 + h, :], ident)
                                return i
                            S.op("pe", [yg, cm], [p], f)
                            S.op("act", [p], [stg], lambda e: e.copy(out=stg.t[:, h0:h0 + 4, jj * 128:(jj + 1) * 128],
                                                                      in_=p.t[:].rearrange("p (h t) -> p h t", t=128)))
                    S.dma("pool", ybT[0][:, tg:tg + sup].rearrange("(h p) t -> p h t", p=128), stg.t[:, :, 0:sup], reads=[stg])
            S.barrier()
            if upto == 5:
                raise _Stop()
            S.mute = start > 6

            with ExitStack() as es:
                relb = S.sb(es, [32, 4], F32, "relb")
                oh = S.sb(es, [32, 1280], F32, "oh")
                S.dma("sp", relb.t[:], relb_d[:, :], writes=[relb])
                S.dma("sp", oh.t[:], oh_d[:, :], writes=[oh])
                tvs = S.sb(es, [4, 1280], BF16, "tvs")
                pq = Ring([S.ps(es, [128, 512], F32, "dps") for _ in range(3)])
                po = [S.ps(es, [128, 512], F32, "dpo") for _ in range(4)]
                for c0 in range(0, 1280, 512):
                    cw = min(512, 1280 - c0)
                    p = pq.next()
                    S.op("pe", [relb, oh], [p], lambda e: e.matmul(p.t[0:4, 0:cw], lhsT=relb.t[:, :], rhs=oh.t[:, c0:c0 + cw], start=True, stop=True))
                    S.op("dve", [p], [tvs], lambda e: e.tensor_scalar(out=tvs.t[:, c0:c0 + cw], in0=p.t[0:4, 0:cw], scalar1=128.0 ** 0.5, scalar2=None, op0=ALU.mult))
                S.dma("pool", tvec[:, :], tvs.t[:], reads=[tvs])
                relbb = S.sb(es, [128, 128], F32, "relbb")
                S.dma("sp", relbb.t[:], bass.AP(relb_d.tensor, 0, [[0, 128], [1, 128]]), writes=[relbb])
                dl = S.sb(es, [128, 4, 128], F32, "dl")
                S.dma("sp", dl.t[:], dlam_d[:, :, :], writes=[dl])
                dnw = S.sb(es, [128, 256], F32, "dnw")
                S.dma("sp", dnw.t[:], dnw_d[:, :], writes=[dnw])
                lt = S.sb(es, [128, 2, 128], F32, "lt")
                l2 = S.sb(es, [128, 2], F32, "l2")
                nlam = S.sb(es, [128, 1], F32, "nlam")
                S.op("dve", [dl], [lt], lambda e: e.tensor_tensor(out=lt.t[:, 0, :], in0=dl.t[:, 0, :], in1=dl.t[:, 1, :], op=ALU.mult))
                S.op("dve", [dl, lt], [lt], lambda e: e.tensor_tensor(out=lt.t[:, 1, :], in0=dl.t[:, 2, :], in1=dl.t[:, 3, :], op=ALU.mult))
                S.op("dve", [lt], [l2], lambda e: e.tensor_reduce(out=l2.t[:], in_=lt.t[:], axis=AX.X, op=ALU.add))
                S.op("act", [l2], [l2], lambda e: e.activation(out=l2.t[:], in_=l2.t[:], func=AF.Exp))
                S.op("dve", [l2], [nlam], lambda e: e.tensor_tensor(out=nlam.t[:], in0=l2.t[:, 1:2], in1=l2.t[:, 0:1], op=ALU.subtract))
                S.op("dve", [nlam], [nlam], lambda e: e.tensor_scalar(out=nlam.t[:], in0=nlam.t[:], scalar1=-LAMBDA_INIT, scalar2=None, op0=ALU.add))
                S.barrier()
                SM = max(seqs)
                Rt = [[S.sb(es, [128, 512], BF16, "Rt") for _ in range(6)] for _ in range(4)]
                for h in range(4):
                    for dlt in range(-1, 5):
                        off = 639 - dlt * 128 - 127
                        S.dma("sp", Rt[h][dlt + 1].t[:], bass.AP(tvec.tensor, h * 1280 + off, [[1, 128], [1, 512]]), writes=[Rt[h][dlt + 1]])
                kT_p = S.pool(es, 1, [128, 2, SM], BF16, "dkT")
                qT_p = S.pool(es, 1, [128, 2, SM], BF16, "dqT")
                v_p = S.pool(es, 1, [128, SM // 128, 258], BF16, "dv")
                pT_p = S.pool(es, 3, [128, 512], BF16, "dpT")
                om = S.sb(es, [128, 4, 2, 256], F32, "om")
                rs = S.pool(es, 2, [128, 1], F32, "drs")
                od = S.pool(es, 2, [128, 256], F32, "dod")
                sqd = S.pool(es, 2, [128, 256], F32, "dsq")
                ssd = S.pool(es, 2, [128, 1], F32, "dss")
                stq = S.pool(es, 2, [128, 2, 512], BF16, "dstq")
                ptr = Ring([S.ps(es, [128, 512], F32, "dpt") for _ in range(1)])
                scale = 128.0 ** -0.5
                for s in range(NS):
                    t0s, L = soff[s], seqs[s]
                    nkb = L // 128
                    QT = 512 if L % 512 == 0 else 128
                    for h in range(4):
                        kT_, qT_, v_ = kT_p.next(), qT_p.next(), v_p.next()
                        S.dma("sp", qT_.t[:, :, 0:L], dqkT[h * 256:(h + 1) * 256, t0s:t0s + L].rearrange("(m p) t -> p m t", p=128), writes=[qT_])
                        S.dma("sp", kT_.t[:, :, 0:L], dqkT[1024 + h * 256:1024 + (h + 1) * 256, t0s:t0s + L].rearrange("(m p) t -> p m t", p=128), writes=[kT_])
                        S.dma("sp", v_.t[:, 0:nkb, 0:256], dv[t0s:t0s + L, h * 256:(h + 1) * 256].rearrange("(j p) d -> p j d", p=128), writes=[v_])
                        S.op("pool", [], [v_], lambda e: e.memset(v_.t[:, 0:nkb, 256:258], 1.0))
                        for q0 in range(0, L, QT):
                            nqs = QT // 128
                            for m in range(2):
                                for kb in range(nkb):
                                    k0 = kb * 128
                                    rmin = k0 - (q0 + QT - 1)
                                    rmax = k0 + 127 - q0
                                    near = not (rmin >= 91 or rmax <= -91)
                                    p = pq.next()
                                    if near:
                                        dlt = (k0 - q0) // 128
                                        R = Rt[h][dlt + 1]

                                        def f(e, p=p, R=R):
                                            e.matmul(p.t[:, 0:QT], lhsT=kT_.t[:, m, k0:k0 + 128], rhs=qT_.t[:, m, q0:q0 + QT], start=True, stop=False)
                                            return e.matmul(p.t[:, 0:QT], lhsT=antib.t[:], rhs=R.t[:, 0:QT], start=False, stop=True)
                                        S.op("pe", [kT_, qT_, antib, R], [p], f)
                                    else:
                                        S.op("pe", [kT_, qT_], [p], lambda e: e.matmul(p.t[:, 0:QT], lhsT=kT_.t[:, m, k0:k0 + 128], rhs=qT_.t[:, m, q0:q0 + QT], start=True, stop=True))
                                    pT = pT_p.next()
                                    if near:
                                        S.op("act", [p], [pT], lambda e: e.activation(out=pT.t[:, 0:QT], in_=p.t[:, 0:QT], func=AF.Exp, scale=scale))
                                    else:
                                        bcol = (31 if rmin > 0 else 15) * 4 + h
                                        S.op("act", [p, relbb], [pT], lambda e: e.activation(out=pT.t[:, 0:QT], in_=p.t[:, 0:QT], func=AF.Exp,
                                                                                              bias=relbb.t[:, bcol:bcol + 1], scale=scale))
                                    for qs in range(nqs):
                                        S.op("pe", [pT, v_], [po[qs]], lambda e: e.matmul(po[qs].t[:, 0:258], lhsT=pT.t[:, qs * 128:(qs + 1) * 128], rhs=v_.t[:, kb, :],
                                                                                         start=(kb == 0), stop=(kb == nkb - 1)))
                                for qs in range(nqs):
                                    r = rs.next()
                                    S.op("dve", [po[qs]], [r], lambda e: e.reciprocal(out=r.t[:], in_=po[qs].t[:, 256:257]))
                                    if m == 1:
                                        S.op("dve", [r, nlam], [r], lambda e: e.tensor_tensor(out=r.t[:], in0=r.t[:], in1=nlam.t[:], op=ALU.mult))
                                    S.op("dve", [po[qs], r], [om], lambda e: e.tensor_scalar(out=om.t[:, qs, m, :], in0=po[qs].t[:, 0:256], scalar1=r.t[:, 0:1], scalar2=None, op0=ALU.mult))
                            st = stq.next()
                            for qs in range(nqs):
                                o, sqq, ssq = od.next(), sqd.next(), ssd.next()
                                S.op("pool", [om], [o], lambda e: e.tensor_tensor(out=o.t[:], in0=om.t[:, qs, 0, :], in1=om.t[:, qs, 1, :], op=ALU.add))
                                S.op("act", [o], [sqq], lambda e: e.activation(out=sqq.t[:], in_=o.t[:], func=AF.Square))
                                S.op("dve", [sqq], [ssq], lambda e: e.tensor_reduce(out=ssq.t[:], in_=sqq.t[:], axis=AX.X, op=ALU.add))
                                S.op("act", [ssq], [ssq], lambda e: e.activation(out=ssq.t[:], in_=ssq.t[:], func=AF.Sqrt, bias=1e-6, scale=1.0 / 256.0))
                                S.op("dve", [ssq], [ssq], lambda e: e.reciprocal(out=ssq.t[:], in_=ssq.t[:]))
                                S.op("dve", [o, ssq], [o], lambda e: e.tensor_scalar(out=o.t[:], in0=o.t[:], scalar1=ssq.t[:, 0:1], scalar2=(1.0 - LAMBDA_INIT), op0=ALU.mult, op1=ALU.mult))
                                S.op("pool", [o, dnw], [o], lambda e: e.tensor_tensor(out=o.t[:], in0=o.t[:], in1=dnw.t[:], op=ALU.mult))
                                p = ptr.next()

                                def f(e, p=p, o=o):
                                    e.transpose(p.t[:, 0:128], o.t[:, 0:128], ident)
                                    return e.transpose(p.t[:, 128:256], o.t[:, 128:256], ident)
                                S.op("pe", [o, cm], [p], f)
                                S.op("act", [p], [st], lambda e: e.copy(out=st.t[:, :, qs * 128:(qs + 1) * 128], in_=p.t[:, 0:256].rearrange("p (c t) -> p c t", t=128)))
                            S.dma("pool", ybT[1][h * 256:(h + 1) * 256, t0s + q0:t0s + q0 + QT].rearrange("(c p) t -> p c t", p=128), st.t[:, :, 0:QT], reads=[st])
            S.barrier()
            if upto == 6:
                raise _Stop()
            S.mute = start > 7

            with ExitStack() as es:
                km_p = S.pool(es, 2, [128, 8, MEM], BF16, "ckm")
                vm_p = S.pool(es, 2, [128, 2, 4, 258], BF16, "cvm")
                q_p = S.pool(es, 2, [128, 8, 512], BF16, "cq")
                pT_p = S.pool(es, 3, [128, 512], BF16, "cpT")
                pq = Ring([S.ps(es, [128, 512], F32, "cps") for _ in range(2)])
                po = Ring([S.ps(es, [128, 512], F32, "cpo") for _ in range(4)])
                ptr = Ring([S.ps(es, [128, 512], F32, "cpt") for _ in range(2)])
                rs = S.pool(es, 2, [128, 1], F32, "crs")
                oc = S.pool(es, 2, [128, 256], F32, "coc")
                stq = S.pool(es, 2, [128, 8, 512], BF16, "cstq")
                scale = 256.0 ** -0.5
                for s in range(NS):
                    t0s, L = soff[s], seqs[s]
                    km, vmt = km_p.next(), vm_p.next()
                    S.dma("sp", km.t[:], kmT[:, s * MEM:(s + 1) * MEM].rearrange("(c p) t -> p c t", p=128), writes=[km])
                    for b in range(2):
                        S.dma("sp", vmt.t[:, b, :, 0:256], vm[s * MEM + b * 128:s * MEM + (b + 1) * 128, :].rearrange("p (h d) -> p h d", d=256), writes=[vmt])
                    S.op("pool", [], [vmt], lambda e: e.memset(vmt.t[:, :, :, 256:258], 1.0))
                    QT = 512 if L % 512 == 0 else 128
                    for q0 in range(0, L, QT):
                        nqs = QT // 128
                        qt = q_p.next()
                        S.dma("sp", qt.t[:, :, 0:QT], cqT[:, t0s + q0:t0s + q0 + QT].rearrange("(c p) t -> p c t", p=128), writes=[qt])
                        st = stq.next()
                        for h in range(4):
                            pTs = []
                            for kb in range(2):
                                p = pq.next()

                                def f(e, p=p, kb=kb):
                                    e.matmul(p.t[:, 0:QT], lhsT=km.t[:, 2 * h, kb * 128:(kb + 1) * 128], rhs=qt.t[:, 2 * h, 0:QT], start=True, stop=False)
                                    return e.matmul(p.t[:, 0:QT], lhsT=km.t[:, 2 * h + 1, kb * 128:(kb + 1) * 128], rhs=qt.t[:, 2 * h + 1, 0:QT], start=False, stop=True)
                                S.op("pe", [km, qt], [p], f)
                                pT = pT_p.next()
                                S.op("act", [p], [pT], lambda e: e.activation(out=pT.t[:, 0:QT], in_=p.t[:, 0:QT], func=AF.Exp, scale=scale))
                                pTs.append(pT)
                            for qs in range(nqs):
                                pp_ = po.next()

                                def f(e, pp_=pp_, qs=qs):
                                    e.matmul(pp_.t[:, 0:258], lhsT=pTs[0].t[:, qs * 128:(qs + 1) * 128], rhs=vmt.t[:, 0, h, :], start=True, stop=False)
                                    return e.matmul(pp_.t[:, 0:258], lhsT=pTs[1].t[:, qs * 128:(qs + 1) * 128], rhs=vmt.t[:, 1, h, :], start=False, stop=True)
                                S.op("pe", [pTs[0], pTs[1], vmt], [pp_], f)
                                r, o = rs.next(), oc.next()
                                S.op("dve", [pp_], [r], lambda e: e.reciprocal(out=r.t[:], in_=pp_.t[:, 256:257]))
                                S.op("dve", [pp_, r], [o], lambda e: e.tensor_scalar(out=o.t[:], in0=pp_.t[:, 0:256], scalar1=r.t[:, 0:1], scalar2=None, op0=ALU.mult))
                                p = ptr.next()

                                def f(e, p=p, o=o):
                                    e.transpose(p.t[:, 0:128], o.t[:, 0:128], ident)
                                    return e.transpose(p.t[:, 128:256], o.t[:, 128:256], ident)
                                S.op("pe", [o, cm], [p], f)
                                S.op("act", [p], [st], lambda e: e.copy(out=st.t[:, 2 * h:2 * h + 2, qs * 128:(qs + 1) * 128], in_=p.t[:, 0:256].rearrange("p (c t) -> p c t", t=128)))
                        S.dma("pool", ybT[2][:, t0s + q0:t0s + q0 + QT].rearrange("(c p) t -> p c t", p=128), st.t[:, :, 0:QT], reads=[st])
            S.barrier()
            if upto == 7:
                raise _Stop()
            S.mute = start > 8

            with ExitStack() as es:
                wb_p = S.pool(es, 2, [128, 3, 8, 512], BF16, "mwb")
                y_p = S.pool(es, 2, [128, 3, 8, 512], BF16, "my")
                g_p = S.pool(es, 2, [128, 3, 4, 512], BF16, "mg")
                t_p = S.pool(es, 3, [128, 3, 512], F32, "mt")
                st_p = S.pool(es, 2, [128, 4, 512], BF16, "mst")
                pq = Ring([S.ps(es, [128, 512], F32, "mps") for _ in range(6)])
                sup = 512 if T % 512 == 0 else 128
                for fb in range(4):
                    wt = wb_p.next()
                    for b in range(3):
                        S.dma("sp", wt.t[:, b, :, :], wb_b[b][:, fb * 512:(fb + 1) * 512].rearrange("(c p) n -> p c n", p=128), writes=[wt])
                    for tg in range(0, T, sup):
                        yt, gt = y_p.next(), g_p.next()
                        for b in range(3):
                            S.dma("sp", yt.t[:, b, :, 0:sup], ybT[b][:, tg:tg + sup].rearrange("(c p) t -> p c t", p=128), writes=[yt])
                            S.dma("sp", gt.t[:, b, :, 0:sup], gatesT[b * D + fb * 512:b * D + (fb + 1) * 512, tg:tg + sup].rearrange("(c p) t -> p c t", p=128), writes=[gt])
                        st = st_p.next()
                        for fc in range(4):
                            tmp = t_p.next()
                            for b in range(3):
                                p = pq.next()

                                def f(e, p=p, b=b, fc=fc):
                                    for kc in range(8):
                                        i = e.matmul(p.t[:, 0:sup], lhsT=wt.t[:, b, kc, fc * 128:(fc + 1) * 128], rhs=yt.t[:, b, kc, 0:sup], start=(kc == 0), stop=(kc == 7))
                                    return i
                                S.op("pe", [wt, yt], [p], f)
                                S.op("dve", [p, gt], [tmp], lambda e: e.tensor_tensor(out=tmp.t[:, b, 0:sup], in0=p.t[:, 0:sup], in1=gt.t[:, b, fc, 0:sup], op=ALU.mult))
                            S.op("pool", [tmp], [tmp], lambda e: e.tensor_tensor(out=tmp.t[:, 0, 0:sup], in0=tmp.t[:, 0, 0:sup], in1=tmp.t[:, 1, 0:sup], op=ALU.add))
                            S.op("pool", [tmp], [st], lambda e: e.tensor_tensor(out=st.t[:, fc, 0:sup], in0=tmp.t[:, 0, 0:sup], in1=tmp.t[:, 2, 0:sup], op=ALU.add))
                        S.dma("pool", mT[fb * 512:(fb + 1) * 512, tg:tg + sup].rearrange("(c p) t -> p c t", p=128), st.t[:, :, 0:sup], reads=[st])
            S.barrier()
            if upto == 8:
                raise _Stop()
            S.mute = start > 9

            def layer_norm(r, lnp, gi, es_pools):
                stats, mv, outt = es_pools
                stt = stats.next()
                for c in range(4):
                    S.op("dve", [r], [stt], lambda e: e.bn_stats(out=stt.t[:, c, :], in_=r.t[:, c * 512:(c + 1) * 512]))
                m = mv.next()
                S.op("dve", [stt], [m], lambda e: e.bn_aggr(out=m.t[:, 0:2], in_=stt.t[:]))
                S.op("act", [m], [m], lambda e: e.activation(out=m.t[:, 2:3], in_=m.t[:, 1:2], func=AF.Sqrt, bias=1e-5, scale=1.0))
                S.op("dve", [m], [m], lambda e: e.reciprocal(out=m.t[:, 2:3], in_=m.t[:, 2:3]))
                o = outt.next()
                S.op("dve", [r, m], [o], lambda e: e.tensor_scalar(out=o.t[:], in0=r.t[:], scalar1=m.t[:, 0:1], scalar2=m.t[:, 2:3], op0=ALU.subtract, op1=ALU.mult))
                S.op("pool", [o, lnp], [o], lambda e: e.tensor_tensor(out=o.t[:], in0=o.t[:], in1=lnp.t[:, 0, :], op=ALU.mult))
                S.op("dve", [o, lnp], [o], lambda e: e.tensor_tensor(out=o.t[:], in0=o.t[:], in1=lnp.t[:, 1, :], op=ALU.add))
                return o

            with ExitStack() as es:
                lnp = S.sb(es, [128, 2, D], F32, "lnp")
                S.dma("sp", lnp.t[:], ln_d[:, 0:2, :], writes=[lnp])
                wo = S.sb(es, [128, 16, D], BF16, "wo")
                for c in range(0, 16, 4):
                    S.dma("sp", wo.t[:, c:c + 4, :], wo_b[c * 128:(c + 4) * 128, :].rearrange("(c p) n -> p c n", p=128), writes=[wo])
                sup = 512 if T % 512 == 0 else 128
                m_p = S.pool(es, 2, [128, 16, sup], BF16, "omT")
                x_p = S.pool(es, 2, [128, D], F32, "ox")
                r_p = S.pool(es, 2, [128, D], F32, "or")
                pools = (S.pool(es, 2, [128, 4, 6], F32, "ost"), S.pool(es, 2, [128, 4], F32, "omv"), S.pool(es, 2, [128, D], F32, "oout"))
                xo = S.pool(es, 2, [128, 16, sup], BF16, "oxo")
                pq = Ring([S.ps(es, [128, 512], F32, "ops") for _ in range(4)])
                ptr = Ring([S.ps(es, [128, 512], F32, "opt") for _ in range(4)])
                for tg in range(0, T, sup):
                    mt = m_p.next()
                    S.dma("sp", mt.t[:], mT[:, tg:tg + sup].rearrange("(c p) t -> p c t", p=128), writes=[mt])
                    xot = xo.next()
                    for jj in range(sup // 128):
                        tt = tg + jj * 128
                        xt = x_p.next()
                        S.dma("sp", xt.t[:], x_d[tt:tt + 128, :], writes=[xt])
                        r = r_p.next()
                        for nb in range(4):
                            p = pq.next()

                            def f(e, p=p, nb=nb):
                                for kc in range(16):
                                    i = e.matmul(p.t[:], lhsT=mt.t[:, kc, jj * 128:(jj + 1) * 128], rhs=wo.t[:, kc, nb * 512:(nb + 1) * 512], start=(kc == 0), stop=(kc == 15))
                                return i
                            S.op("pe", [mt, wo], [p], f)
                            S.op("dve", [p, xt], [r], lambda e: e.scalar_tensor_tensor(out=r.t[:, nb * 512:(nb + 1) * 512], in0=xt.t[:, nb * 512:(nb + 1) * 512], scalar=ALPHA,
                                                                                      in1=p.t[:], op0=ALU.mult, op1=ALU.add))
                        o = layer_norm(r, lnp, 0, pools)
                        S.dma("pool", x1[tt:tt + 128, :], o.t[:], reads=[o])
                        for kc0 in range(0, 16, 4):
                            p = ptr.next()

                            def f(e, p=p, kc0=kc0, o=o):
                                for c in range(4):
                                    i = e.transpose(p.t[:, c * 128:(c + 1) * 128], o.t[:, (kc0 + c) * 128:(kc0 + c + 1) * 128], ident)
                                return i
                            S.op("pe", [o, cm], [p], f)
                            S.op("act", [p], [xot], lambda e: e.copy(out=xot.t[:, kc0:kc0 + 4, jj * 128:(jj + 1) * 128], in_=p.t[:].rearrange("p (c t) -> p c t", t=128)))
                    S.dma("pool", x1T[:, tg:tg + sup].rearrange("(c p) t -> p c t", p=128), xot.t[:], reads=[xot])
            S.barrier()
            if upto == 9:
                raise _Stop()
            S.mute = start > 10

            with ExitStack() as es:
                LM = max(s[1] for s in segs)
                fcw = S.sb(es, [128, 2 * NFC, 3], F32, "fcw")
                S.dma("sp", fcw.t[:], fconv_d[:, :, :], writes=[fcw])
                xs_p = S.pool(es, 1, [128, 16, LM + 2], BF16, "fxs")
                w_p = S.pool(es, 3, [128, 2, 16, 128], BF16, "fw")
                u_p = S.pool(es, 2, [128, 2, LM + 2], F32, "fu")
                c_p = S.pool(es, 2, [128, 2, LM], F32, "fc")
                h_p = S.pool(es, 2, [128, LM], BF16, "fh")
                pq = Ring([S.ps(es, [128, 512], F32, "fps") for _ in range(6)])
                for (t0, L, hl, hr) in segs:
                    xs = xs_p.next()
                    lo = 1 - (1 if hl else 0)
                    hi = L + 1 + (1 if hr else 0)
                    if not hl:
                        S.op("pool", [], [xs], lambda e: e.memset(xs.t[:, :, 0:1], 0.0))
                    if not hr:
                        S.op("pool", [], [xs], lambda e: e.memset(xs.t[:, :, L + 1:L + 2], 0.0))
                    S.dma("sp", xs.t[:, :, lo:hi], x1T[:, t0 - 1 + lo:t0 - 1 + hi].rearrange("(c p) t -> p c t", p=128), writes=[xs])
                    cols = list(range(0, L + 2, 512))
                    for c in range(NFC):
                        w = w_p.next()
                        for gv in range(2):
                            cc = gv * DFF + c * 128
                            S.dma("sp", w.t[:, gv, :, :], wu_b[:, cc:cc + 128].rearrange("(c p) n -> p c n", p=128), writes=[w])
                        u = u_p.next()
                        for gv in range(2):
                            for ci, c0 in enumerate(cols):
                                cw = min(512, L + 2 - c0)
                                p = pq.next()

                                def f(e, p=p, gv=gv, c0=c0, cw=cw):
                                    for kc in range(16):
                                        i = e.matmul(p.t[:, 0:cw], lhsT=w.t[:, gv, kc, :], rhs=xs.t[:, kc, c0:c0 + cw], start=(kc == 0), stop=(kc == 15))
                                    return i
                                S.op("pe", [w, xs], [p], f)
                                if (ci + gv) % 2 == 0:
                                    S.op("act", [p], [u], lambda e: e.copy(out=u.t[:, gv, c0:c0 + cw], in_=p.t[:, 0:cw]))
                                else:
                                    S.op("dve", [p], [u], lambda e: e.tensor_copy(out=u.t[:, gv, c0:c0 + cw], in_=p.t[:, 0:cw]))
                        cv = c_p.next()
                        for gv in range(2):
                            ch = gv * NFC + c
                            eng = "dve"
                            S.op(eng, [u, fcw], [cv], lambda e: e.tensor_scalar(out=cv.t[:, gv, 0:L], in0=u.t[:, gv, 0:L], scalar1=fcw.t[:, ch, 0:1], scalar2=None, op0=ALU.mult))
                            for j in (1, 2):
                                S.op(eng, [u, fcw, cv], [cv], lambda e: e.scalar_tensor_tensor(out=cv.t[:, gv, 0:L], in0=u.t[:, gv, j:j + L], scalar=fcw.t[:, ch, j:j + 1],
                                                                                                 in1=cv.t[:, gv, 0:L], op0=ALU.mult, op1=ALU.add))
                        S.op("act", [cv], [cv], lambda e: e.activation(out=cv.t[:, 0, 0:L], in_=cv.t[:, 0, 0:L], func=AF.Silu))
                        hh = h_p.next()
                        S.op("dve", [cv], [hh], lambda e: e.tensor_tensor(out=hh.t[:, 0:L], in0=cv.t[:, 0, 0:L], in1=cv.t[:, 1, 0:L], op=ALU.mult))
                        S.dma("pool", hT[c * 128:(c + 1) * 128, t0:t0 + L], hh.t[:, 0:L], reads=[hh])
            S.barrier()
            if upto == 10:
                raise _Stop()
            S.mute = start > 11

            with ExitStack() as es:
                lnp = S.sb(es, [128, 2, D], F32, "lnp2")
                S.dma("sp", lnp.t[:], ln_d[:, 2:4, :], writes=[lnp])
                wd = S.sb(es, [128, NFC, 1024], BF16, "wd")
                TG = 256 if T % 256 == 0 else 128
                h_p = S.pool(es, 2, [128, NFC, TG], BF16, "dh")
                yp_p = S.pool(es, 2, [128, 1024], F32, "dyp")
                x_p = S.pool(es, 2, [128, D], F32, "dx1")
                r_p = S.pool(es, 1, [128, D], F32, "dr")
                pools = (S.pool(es, 2, [128, 4, 6], F32, "dst"), S.pool(es, 2, [128, 4], F32, "dmv"), S.pool(es, 2, [128, D], F32, "dout"))
                pq = Ring([S.ps(es, [128, 512], F32, "dps") for _ in range(6)])
                for half in range(2):
                    for c in range(0, NFC, 8):
                        ce = min(NFC, c + 8)
                        S.dma("sp", wd.t[:, c:ce, :], wd_b[c * 128:ce * 128, half * 1024:(half + 1) * 1024].rearrange("(c p) n -> p c n", p=128), writes=[wd])
                    for tg in range(0, T, TG):
                        ht = h_p.next()
                        S.dma("sp", ht.t[:], hT[:, tg:tg + TG].rearrange("(c p) t -> p c t", p=128), writes=[ht])
                        for jj in range(TG // 128):
                            tt = tg + jj * 128
                            ps2 = [pq.next(), pq.next()]
                            for nb in range(2):
                                p = ps2[nb]

                                def f(e, p=p, nb=nb):
                                    for kc in range(NFC):
                                        i = e.matmul(p.t[:], lhsT=ht.t[:, kc, jj * 128:(jj + 1) * 128], rhs=wd.t[:, kc, nb * 512:(nb + 1) * 512], start=(kc == 0), stop=(kc == NFC - 1))
                                    return i
                                S.op("pe", [ht, wd], [p], f)
                            if half == 0:
                                yp = yp_p.next()
                                S.op("act", [ps2[0]], [yp], lambda e: e.copy(out=yp.t[:, 0:512], in_=ps2[0].t[:]))
                                S.op("dve", [ps2[1]], [yp], lambda e: e.tensor_copy(out=yp.t[:, 512:1024], in_=ps2[1].t[:]))
                                S.dma("pool", ypart[tt:tt + 128, :], yp.t[:], reads=[yp])
                            else:
                                yp, xt, r = yp_p.next(), x_p.next(), r_p.next()
                                S.dma("sp", yp.t[:], ypart[tt:tt + 128, :], writes=[yp])
                                S.dma("sp", xt.t[:], x1[tt:tt + 128, :], writes=[xt])
                                S.op("dve", [xt, yp], [r], lambda e: e.scalar_tensor_tensor(out=r.t[:, 0:1024], in0=xt.t[:, 0:1024], scalar=ALPHA, in1=yp.t[:], op0=ALU.mult, op1=ALU.add))
                                for nb in range(2):
                                    S.op("dve", [ps2[nb], xt], [r], lambda e: e.scalar_tensor_tensor(
                                        out=r.t[:, 1024 + nb * 512:1024 + (nb + 1) * 512], in0=xt.t[:, 1024 + nb * 512:1024 + (nb + 1) * 512], scalar=ALPHA,
                                        in1=ps2[nb].t[:], op0=ALU.mult, op1=ALU.add))
                                o = layer_norm(r, lnp, 2, pools)
                                S.dma("pool", y_d[tt:tt + 128, :], o.t[:], reads=[o])
                    S.barrier()
            S.barrier()
            if upto == 11:
                raise _Stop()
            S.mute = start > 12

        except _Stop:
            pass
        S.mute = False
        S.barrier()

    return nc


def _t5_bucket(rel):
    nb = 16
    max_exact = 8
    ret = np.where(rel > 0, nb, 0)
    n = np.abs(rel)
    nf = np.maximum(n, 1).astype(np.float32)
    large = max_exact + (np.log(nf / np.float32(max_exact)) / np.float32(math.log(128 / max_exact)) * np.float32(nb - max_exact)).astype(np.int32)
    large = np.minimum(large, nb - 1)
    return ret + np.where(n < max_exact, n, large)


def _consts():
    p = np.arange(128)[:, None]
    f = np.arange(128)[None, :]
    cm = np.zeros((128, 7, 128), np.float32)
    cm[:, 0] = (p == f)
    cm[:, 1] = 1.0
    cm[:, 2] = (p >= f)
    cm[:, 3] = (p <= f)
    cm[:, 4] = (p > f)
    cm[:, 5] = (p < f)
    cm[:, 6] = (p + f == 127)
    rel = 639 - np.arange(1280)
    b = _t5_bucket(rel)
    oh = np.zeros((32, 1280), np.float32)
    oh[b, np.arange(1280)] = 1.0
    return cm, oh


def make_in_maps(inp, per_core_x, per_core_mem):
    f = lambda a: np.ascontiguousarray(a, dtype=np.float32)
    cm, oh = _consts()
    bc = lambda v, n=128: np.ascontiguousarray(np.broadcast_to(np.asarray(v, np.float32).reshape(1, -1), (n, np.asarray(v).size)))
    shared = {
        "w_in": f(inp["w_in"][0]), "w_gate": f(inp["w_gate"][0]), "w_kv": f(inp["w_mem_kv"][0]),
        "wb0": f(inp["w_branch_gdn"][0]), "wb1": f(inp["w_branch_diff"][0]), "wb2": f(inp["w_branch_cross"][0]),
        "w_out": f(inp["w_out"][0]), "w_up": f(inp["w_up"][0]), "w_down": f(inp["w_down"][0]),
        "gconv": f(np.asarray(inp["gdn_conv"][0]).reshape(5, 24, 128).transpose(2, 1, 0)),
        "fconv": f(np.asarray(inp["ffn_conv"][0]).reshape(3, 2 * NFC, 128).transpose(2, 1, 0)),
        "bgate": f(np.asarray(inp["b_gate"][0]).reshape(48, 128).T),
        "alog": bc(np.asarray(inp["gdn_a_log"][0]).reshape(-1)),
        "dtb": bc(np.asarray(inp["gdn_dt_bias"][0]).reshape(-1)),
        "gnw": bc(inp["gdn_norm_w"][0]),
        "dlam": f(np.broadcast_to(np.asarray(inp["diff_lambda"][0], np.float32)[None], (128, 4, 128))),
        "dnw": bc(inp["diff_norm_w"][0]),
        "lnp": f(np.broadcast_to(np.stack([np.asarray(inp[k][0], np.float32) for k in ("ln1_g", "ln1_b", "ln2_g", "ln2_b")])[None], (128, 4, D))),
        "relb": f(inp["rel_bias"]),
        "cmask": cm, "oh": oh,
    }
    maps = []
    for xc, mc in zip(per_core_x, per_core_mem):
        d = dict(shared)
        d["x"] = f(xc)
        d["mem"] = f(mc)
        maps.append(d)
    return maps


_NC_CACHE = {}


def kernel(**inp):
    xp = np.asarray(inp["x_prompt"], np.float32)
    xs = np.asarray(inp["x_sample"], np.float32)
    mp = np.asarray(inp["mem_prompt"], np.float32)
    ms = np.asarray(inp["mem_sample"], np.float32)
    n = 8
    seqs = (2048, 2048, 4096)
    px, pm = [], []
    for c in range(n):
        px.append(np.concatenate([xp[2 * c], xp[2 * c + 1], xs[c]], axis=0))
        pm.append(np.concatenate([mp[2 * c], mp[2 * c + 1], ms[c]], axis=0))
    maps = make_in_maps(inp, px, pm)
    if seqs not in _NC_CACHE:
        _NC_CACHE[seqs] = build_program(list(seqs))
    nc = _NC_CACHE[seqs]
    res = run_bass_kernel_spmd(nc, maps, core_ids=list(range(n)))
    yp = np.empty_like(xp)
    ys = np.empty_like(xs)
    for c in range(n):
        y = res.results[c]["y"]
        yp[2 * c] = y[0:2048]
        yp[2 * c + 1] = y[2048:4096]
        ys[c] = y[4096:8192]
    return (yp, ys)
```

```python
import math
from contextlib import ExitStack
import numpy as np
import concourse.bass as bass
import concourse.mybir as mybir
from concourse.bass_utils import run_bass_kernel_spmd

F32 = mybir.dt.float32
BF16 = mybir.dt.bfloat16
AF = mybir.ActivationFunctionType
ALU = mybir.AluOpType
AX = mybir.AxisListType

D = 2048
GW = 1024
NH = 8
IN_COLS = 8224
DFF = 5504
NFC = DFF // 128
MEM = 256
ALPHA = 2.0 ** 0.25
LAMBDA_INIT = 0.8 - 0.6 * math.exp(0.0)
SAME_ENGINE_SYNC = True
NDMASEM = 24


class T_:
    __slots__ = ("t", "w", "r")

    def __init__(self, t):
        self.t = t
        self.w = {}
        self.r = {}


class Sy:
    def __init__(self, nc, es):
        self.nc = nc
        self.es = es
        self.eng = {"pe": nc.tensor, "act": nc.scalar, "dve": nc.vector, "pool": nc.gpsimd, "sp": nc.sync}
        self.sem = {}
        self.cnt = {}
        self.known = {k: {} for k in self.eng}
        self.semobj = {}
        for k in self.eng:
            s = es.enter_context(nc.semaphore("s_" + k))
            self.sem[k] = s
            self.cnt[k] = 0
            self.semobj[id(s)] = s
        self.dsem = []
        self.dval = []
        for i in range(NDMASEM):
            s = es.enter_context(nc.semaphore("d%d" % i))
            self.dsem.append(s)
            self.dval.append(0)
            self.semobj[id(s)] = s
        self.di = 0
        self.nwait = 0
        self.uid = 0
        self.mute = False

    def sb(self, es, shape, dt, name=None):
        self.uid += 1
        return T_(es.enter_context(self.nc.sbuf_tensor("%s_%d" % (name or "t", self.uid), list(shape), dt)))

    def ps(self, es, shape, dt, name=None):
        self.uid += 1
        return T_(es.enter_context(self.nc.psum_tensor("%s_%d" % (name or "p", self.uid), list(shape), dt)))

    def pool(self, es, n, shape, dt, name=None):
        return Ring([self.sb(es, shape, dt, name) for _ in range(n)])

    def _wait(self, e, waits):
        E = self.eng[e]
        kn = self.known[e]
        for sid, v in waits.items():
            if kn.get(sid, 0) < v:
                E.wait_ge(self.semobj[sid], v)
                kn[sid] = v
                self.nwait += 1

    def _collect(self, e, reads, writes):
        waits = {}
        own = id(self.sem[e])

        def add(d):
            for sid, v in d.items():
                if sid == own and (e == "pe" or not SAME_ENGINE_SYNC):
                    continue
                if waits.get(sid, 0) < v:
                    waits[sid] = v

        for t in reads:
            add(t.w)
        for t in writes:
            add(t.w)
            add(t.r)
        return waits

    def _commit(self, reads, writes, sid, v):
        for t in reads:
            if t.r.get(sid, 0) < v:
                t.r[sid] = v
        for t in writes:
            t.w = {sid: v}
            t.r = {}

    def op(self, e, reads, writes, fn):
        if self.mute:
            return
        self._wait(e, self._collect(e, reads, writes))
        inst = fn(self.eng[e])
        self.cnt[e] += 1
        inst.then_inc(self.sem[e], 1)
        self._commit(reads, writes, id(self.sem[e]), self.cnt[e])

    def dma(self, q, out_ap, in_ap, reads=(), writes=(), **kw):
        if self.mute:
            return
        i = self.di
        self.di = (self.di + 1) % NDMASEM
        s = self.dsem[i]
        waits = self._collect(q, reads, writes)
        if self.dval[i] > 0:
            waits[id(s)] = max(waits.get(id(s), 0), self.dval[i])
        self._wait(q, waits)
        self.dval[i] += 16
        self.eng[q].dma_start(out=out_ap, in_=in_ap, **kw).then_inc(s, 16)
        self._commit(reads, writes, id(s), self.dval[i])

    def barrier(self):
        if self.mute:
            return
        allw = {}
        for k in self.eng:
            if self.cnt[k] > 0:
                allw[id(self.sem[k])] = self.cnt[k]
        for i in range(NDMASEM):
            if self.dval[i] > 0:
                allw[id(self.dsem[i])] = self.dval[i]
        for e in self.eng:
            w = dict(allw)
            w.pop(id(self.sem[e]), None)
            self._wait(e, w)
        for e in ("act", "dve", "pool"):
            if self.cnt[e] > 0:
                self._wait(e, {id(self.sem[e]): self.cnt[e]})


class Ring:
    def __init__(self, tiles):
        self.tiles = tiles
        self.i = 0

    def next(self):
        t = self.tiles[self.i]
        self.i = (self.i + 1) % len(self.tiles)
        return t


def act_copy2(e, dst, src):
    e.copy(out=dst.t[:, 0:4, :], in_=src.t[:, 0:4, :])
    return e.copy(out=dst.t[:, 4:8, :], in_=src.t[:, 4:8, :])


def bc_last(ap, n):
    sh = list(ap.shape)
    return ap.unsqueeze(len(sh)).to_broadcast(sh + [n])


def bc_mid(ap, n):
    sh = list(ap.shape)
    return ap.unsqueeze(1).to_broadcast([sh[0], n] + sh[1:])


class _Stop(Exception):
    pass


def build_program(seqs, dbg=False, upto=99, start=0, ext_in=(), bisect=0):
    T = sum(seqs)
    NS = len(seqs)
    soff = [sum(seqs[:i]) for i in range(NS)]
    segs = []
    for s, L in enumerate(seqs):
        n = max(1, L // 2048)
        sl = L // n
        for i in range(n):
            segs.append((soff[s] + i * sl, sl, i > 0, i < n - 1))

    nc = bass.Bass("TRN2", target_bir_lowering=False)
    kin = "ExternalInput"
    kint = "ExternalOutput" if dbg else "Internal"

    def din(name, shape, dt=F32):
        return nc.dram_tensor(name, list(shape), dt, kind=kin).ap()

    def dsc(name, shape, dt):
        if name in ext_in:
            return nc.dram_tensor(name, list(shape), dt, kind="ExternalInput").ap()
        return nc.dram_tensor(name, list(shape), dt, kind=("Internal" if name.startswith("w") else kint)).ap()

    x_d = din("x", [T, D])
    mem_d = din("mem", [NS * MEM, D])
    w_in_d = din("w_in", [D, IN_COLS])
    w_gate_d = din("w_gate", [D, 3 * D])
    w_kv_d = din("w_kv", [D, 2 * GW])
    wb_d = [din("wb%d" % i, [GW, D]) for i in range(3)]
    w_out_d = din("w_out", [D, D])
    w_up_d = din("w_up", [D, 2 * DFF])
    w_down_d = din("w_down", [DFF, D])
    gconv_d = din("gconv", [128, 24, 5])
    fconv_d = din("fconv", [128, 2 * NFC, 3])
    bgate_d = din("bgate", [128, 48])
    alog_d = din("alog", [128, 16])
    dtb_d = din("dtb", [128, 16])
    gnw_d = din("gnw", [128, 128])
    dlam_d = din("dlam", [128, 4, 128])
    dnw_d = din("dnw", [128, 256])
    ln_d = din("lnp", [128, 4, D])
    relb_d = din("relb", [32, 4])
    cm_d = din("cmask", [128, 7, 128])
    oh_d = din("oh", [32, 1280])

    y_d = nc.dram_tensor("y", [T, D], F32, kind="ExternalOutput").ap()

    wi_b = dsc("wi_b", [D, IN_COLS], BF16)
    wg_b = dsc("wg_b", [D, 3 * D], BF16)
    wkv_b = dsc("wkv_b", [D, 2 * GW], BF16)
    wb_b = [dsc("wb_b%d" % i, [GW, D], BF16) for i in range(3)]
    wo_b = dsc("wo_b", [D, D], BF16)
    wu_b = dsc("wu_b", [D, 2 * DFF], BF16)
    wd_b = dsc("wd_b", [DFF, D], BF16)
    xT = dsc("xT", [D, T], BF16)
    memT = dsc("memT", [D, NS * MEM], BF16)
    gqkvT = dsc("gqkvT", [3 * GW, T], F32)
    dqkT = dsc("dqkT", [2 * GW, T], BF16)
    cqT = dsc("cqT", [GW, T], BF16)
    gatesT = dsc("gatesT", [3 * D, T], BF16)
    gz = dsc("gz", [T, GW], BF16)
    gab = dsc("gab", [T, 32], F32)
    dv = dsc("dv", [T, GW], BF16)
    kmT = dsc("kmT", [GW, NS * MEM], BF16)
    vm = dsc("vm", [NS * MEM, GW], BF16)
    qTn = dsc("qTn", [GW, T], BF16)
    kTn = dsc("kTn", [GW, T], BF16)
    ktok = dsc("ktok", [T, GW], BF16)
    vtok = dsc("vtok", [T, GW], BF16)
    odir = dsc("odir", [2, T, GW], F32)
    ybT = [dsc("ybT%d" % i, [GW, T], BF16) for i in range(3)]
    mT = dsc("mT", [D, T], BF16)
    x1 = dsc("x1", [T, D], F32)
    x1T = dsc("x1T", [D, T], BF16)
    hT = dsc("hT", [DFF, T], BF16)
    ypart = dsc("ypart", [T, GW], F32)
    tvec = dsc("tvec", [4, 1280], BF16)

    with ExitStack() as es0:
        S = Sy(nc, es0)
        try:
            cm = S.sb(es0, [128, 7, 128], F32, "cm")
            S.dma("sp", cm.t[:], cm_d[:, :, :], writes=[cm])
            ident = cm.t[:, 0, :]
            identb = S.sb(es0, [128, 128], BF16, "identb")
            onesb = S.sb(es0, [128, 128], BF16, "onesb")
            antib = S.sb(es0, [128, 128], BF16, "antib")
            S.op("dve", [cm], [identb], lambda e: e.tensor_copy(out=identb.t[:], in_=cm.t[:, 0, :]))
            S.op("dve", [cm], [onesb], lambda e: e.tensor_copy(out=onesb.t[:], in_=cm.t[:, 1, :]))
            S.op("dve", [cm], [antib], lambda e: e.tensor_copy(out=antib.t[:], in_=cm.t[:, 6, :]))

            S.mute = start > 0
            with ExitStack() as es:
                CB = 2048
                fin = S.pool(es, 3, [128, CB], F32, "wc_in")
                fout = S.pool(es, 3, [128, CB], BF16, "wc_out")
                k = 0
                for src, dst in ([(w_in_d, wi_b), (w_gate_d, wg_b), (w_kv_d, wkv_b)] +
                                 [(wb_d[i], wb_b[i]) for i in range(3)] +
                                 [(w_out_d, wo_b), (w_up_d, wu_b), (w_down_d, wd_b)]):
                    R, C = src.shape
                    for r0 in range(0, R, 128):
                        for c0 in range(0, C, CB):
                            cw = min(CB, C - c0)
                            a = fin.next()
                            b = fout.next()
                            S.dma("sp", a.t[:, 0:cw], src[r0:r0 + 128, c0:c0 + cw], writes=[a])
                            eng = ("dve", "act", "pool")[k % 3]
                            k += 1
                            if eng == "act":
                                S.op(eng, [a], [b], lambda e: e.copy(out=b.t[:, 0:cw], in_=a.t[:, 0:cw]))
                            else:
                                S.op(eng, [a], [b], lambda e: e.tensor_copy(out=b.t[:, 0:cw], in_=a.t[:, 0:cw]))
                            S.dma("pool", dst[r0:r0 + 128, c0:c0 + cw], b.t[:, 0:cw], reads=[b])
            S.barrier()
            if upto == 0:
                raise _Stop()
            S.mute = start > 1

            def xpose_phase(src, dst, ntok, sup):
                with ExitStack() as es:
                    xin = S.pool(es, 2, [128, sup // 128, D], F32, "xin")
                    xo = S.pool(es, 2, [128, 16, sup], BF16, "xo")
                    pp = Ring([S.ps(es, [128, 512], F32, "xp") for _ in range(4)])
                    ev = 0
                    for t0 in range(0, ntok, sup):
                        a = xin.next()
                        o = xo.next()
                        S.dma("sp", a.t[:], src[t0:t0 + sup, :].rearrange("(j p) d -> p j d", p=128), writes=[a])
                        for kc in range(16):
                            for j0 in range(0, sup // 128, 4):
                                nj = min(4, sup // 128 - j0)
                                p = pp.next()

                                def f(e, p=p, j0=j0, nj=nj, kc=kc):
                                    for j in range(nj):
                                        i = e.transpose(p.t[:, j * 128:(j + 1) * 128],
                                                        a.t[:, j0 + j, kc * 128:(kc + 1) * 128], ident)
                                    return i
                                S.op("pe", [a, cm], [p], f)
                                eng = ("dve", "act")[ev % 2]
                                ev += 1
                                if eng == "act":
                                    S.op(eng, [p], [o], lambda e: e.copy(out=o.t[:, kc, j0 * 128:(j0 + nj) * 128],
                                                                        in_=p.t[:, 0:nj * 128]))
                                else:
                                    S.op(eng, [p], [o], lambda e: e.tensor_copy(out=o.t[:, kc, j0 * 128:(j0 + nj) * 128],
                                                                               in_=p.t[:, 0:nj * 128]))
                        S.dma("pool", dst[:, t0:t0 + sup].rearrange("(c p) t -> p c t", p=128), o.t[:], reads=[o])

            xpose_phase(x_d, xT, T, 512 if T % 512 == 0 else 128)
            S.barrier2 = None
            for _e in (1,):
                S.barrier()
            xpose_phase(mem_d, memT, NS * MEM, MEM)
            S.barrier()
            if upto == 1:
                raise _Stop()
            S.mute = start > 2

            with ExitStack() as es:
                LM = max(s[1] for s in segs)
                xs_pool = S.pool(es, 1, [128, 16, LM], BF16, "xseg")
                wc_pool = S.pool(es, 2, [128, 16, 512], BF16, "wcb")
                stf = S.pool(es, 2, [128, LM], F32, "stf")
                stb = S.pool(es, 2, [128, LM], BF16, "stb")
                stt = S.pool(es, 2, [128, 4, 512], BF16, "stt")
                stg = S.pool(es, 2, [128, 32], F32, "stg")
                pp = Ring([S.ps(es, [128, 512], F32, "pj") for _ in range(6)])
                bg = S.sb(es, [128, 48], F32, "bg")
                S.dma("sp", bg.t[:], bgate_d[:, :], writes=[bg])
                evc = [0]

                def load_w(wsrc, c0, cw):
                    w = wc_pool.next()
                    S.dma("sp", w.t[:, :, 0:cw], wsrc[:, c0:c0 + cw].rearrange("(c p) n -> p c n", p=128), writes=[w])
                    return w

                def fm_block(xs, t0, L, wsrc, c0, ncols, dst, r0, dt, sig_chunk0=None):
                    for cb in range(0, ncols, 512):
                        cw = min(512, ncols - cb)
                        w = load_w(wsrc, c0 + cb, cw)
                        for fc in range(cw // 128):
                            st = (stf if dt == F32 else stb).next()
                            for tt in range(0, L, 512):
                                tw = min(512, L - tt)
                                p = pp.next()

                                def f(e, p=p, tt=tt, tw=tw, fc=fc):
                                    for kc in range(16):
                                        i = e.matmul(p.t[:, 0:tw], lhsT=w.t[:, kc, fc * 128:(fc + 1) * 128],
                                                     rhs=xs.t[:, kc, tt:tt + tw], start=(kc == 0), stop=(kc == 15))
                                    return i
                                S.op("pe", [w, xs], [p], f)
                                if sig_chunk0 is not None:
                                    ch = sig_chunk0 + (cb // 128) + fc
                                    S.op("act", [p, bg], [st], lambda e: e.activation(
                                        out=st.t[:, tt:tt + tw], in_=p.t[:, 0:tw], func=AF.Sigmoid,
                                        bias=bg.t[:, ch:ch + 1], scale=1.0))
                                else:
                                    evc[0] += 1
                                    if evc[0] % 2:
                                        S.op("dve", [p], [st], lambda e: e.tensor_copy(out=st.t[:, tt:tt + tw], in_=p.t[:, 0:tw]))
                                    else:
                                        S.op("act", [p], [st], lambda e: e.copy(out=st.t[:, tt:tt + tw], in_=p.t[:, 0:tw]))
                            rr = r0 + cb + fc * 128
                            S.dma("pool", dst[rr:rr + 128, t0:t0 + L], st.t[:, 0:L], reads=[st])

                def tm_block(xs, t0, L, wsrc, c0, ncols, dst, dc0, silu):
                    for cb in range(0, ncols, 512):
                        cw = min(512, ncols - cb)
                        w = load_w(wsrc, c0 + cb, cw)
                        for tg in range(0, L, 512):
                            ng = min(4, (L - tg) // 128)
                            st = stt.next()
                            for j in range(ng):
                                tt = tg + j * 128
                                p = pp.next()

                                def f(e, p=p, tt=tt):
                                    for kc in range(16):
                                        i = e.matmul(p.t[:, 0:cw], lhsT=xs.t[:, kc, tt:tt + 128],
                                                     rhs=w.t[:, kc, 0:cw], start=(kc == 0), stop=(kc == 15))
                                    return i
                                S.op("pe", [w, xs], [p], f)
                                if silu:
                                    S.op("act", [p], [st], lambda e: e.activation(out=st.t[:, j, 0:cw], in_=p.t[:, 0:cw], func=AF.Silu))
                                else:
                                    S.op("dve", [p], [st], lambda e: e.tensor_copy(out=st.t[:, j, 0:cw], in_=p.t[:, 0:cw]))
                            S.dma("pool", dst[t0 + tg:t0 + tg + ng * 128, dc0 + cb:dc0 + cb + cw].rearrange("(j p) n -> p j n", p=128),
                                  st.t[:, 0:ng, 0:cw], reads=[st])

                for (t0, L, _l, _r) in segs:
                    xs = xs_pool.next()
                    S.dma("sp", xs.t[:, :, 0:L], xT[:, t0:t0 + L].rearrange("(c p) t -> p c t", p=128), writes=[xs])
                    fm_block(xs, t0, L, wi_b, 0, 3072, gqkvT, 0, F32)
                    fm_block(xs, t0, L, wi_b, 4128, 2048, dqkT, 0, BF16)
                    fm_block(xs, t0, L, wi_b, 7200, 1024, cqT, 0, BF16)
                    fm_block(xs, t0, L, wg_b, 0, 3 * D, gatesT, 0, BF16, sig_chunk0=0)
                    tm_block(xs, t0, L, wi_b, 3072, 1024, gz, 0, True)
                    tm_block(xs, t0, L, wi_b, 6176, 1024, dv, 0, False)
                    w = wc_pool.next()
                    S.dma("sp", w.t[:, :, 0:32], wi_b[:, 4096:4128].rearrange("(c p) n -> p c n", p=128), writes=[w])
                    for tt in range(0, L, 128):
                        p = pp.next()
                        sg = stg.next()

                        def f(e, p=p, tt=tt):
                            for kc in range(16):
                                i = e.matmul(p.t[:, 0:32], lhsT=xs.t[:, kc, tt:tt + 128], rhs=w.t[:, kc, 0:32],
                                             start=(kc == 0), stop=(kc == 15))
                            return i
                        S.op("pe", [w, xs], [p], f)
                        S.op("dve", [p], [sg], lambda e: e.tensor_copy(out=sg.t[:], in_=p.t[:, 0:32]))
                        S.dma("pool", gab[t0 + tt:t0 + tt + 128, :], sg.t[:], reads=[sg])
                xs = xs_pool.next()
                NM = NS * MEM
                S.dma("sp", xs.t[:, :, 0:NM], memT[:, :].rearrange("(c p) t -> p c t", p=128), writes=[xs])
                fm_block(xs, 0, NM, wkv_b, 0, 1024, kmT, 0, BF16)
                tm_block(xs, 0, NM, wkv_b, 1024, 1024, vm, 0, False)
            S.barrier()
            if upto == 2:
                raise _Stop()
            S.mute = start > 3

            with ExitStack() as es:
                SM = max(seqs)
                gcw = S.sb(es, [128, 24, 5], F32, "gcw")
                S.dma("sp", gcw.t[:], gconv_d[:, :, :], writes=[gcw])
                xin = S.pool(es, 2, [128, SM + 4], F32, "cxin")
                acc = S.pool(es, 1, [128, SM], F32, "cacc")
                ys = S.pool(es, 2, [128, SM], F32, "cys")
                sq = S.pool(es, 2, [128, SM], BF16, "csq")
                rn = S.pool(es, 2, [128, 512], F32, "crn")
                yn = S.pool(es, 2, [128, SM], BF16, "cyn")
                ynf = S.pool(es, 1, [128, SM], F32, "cynf")
                tst = S.pool(es, 2, [128, 4, 128], BF16, "ctst")
                pss = Ring([S.ps(es, [128, 512], F32, "cps") for _ in range(2)])
                ptr = Ring([S.ps(es, [128, 512], F32, "cpt") for _ in range(2)])
                for s in range(NS):
                    t0, L = soff[s], seqs[s]
                    for c in range(24):
                        a = xin.next()
                        S.op("pool", [], [a], lambda e: e.memset(a.t[:, 0:2], 0.0))
                        S.op("pool", [], [a], lambda e: e.memset(a.t[:, L + 2:L + 4], 0.0))
                        S.dma("sp", a.t[:, 2:L + 2], gqkvT[c * 128:(c + 1) * 128, t0:t0 + L], writes=[a])
                        ac = acc.next()
                        S.op("dve", [a, gcw], [ac], lambda e: e.tensor_scalar(
                            out=ac.t[:, 0:L], in0=a.t[:, 0:L], scalar1=gcw.t[:, c, 0:1], scalar2=None, op0=ALU.mult))
                        for j in range(1, 5):
                            S.op("dve", [a, gcw, ac], [ac], lambda e: e.scalar_tensor_tensor(
                                out=ac.t[:, 0:L], in0=a.t[:, j:j + L], scalar=gcw.t[:, c, j:j + 1], in1=ac.t[:, 0:L],
                                op0=ALU.mult, op1=ALU.add))
                        y = ys.next()
                        S.op("act", [ac], [y], lambda e: e.activation(out=y.t[:, 0:L], in_=ac.t[:, 0:L], func=AF.Silu))
                        if c < 16:
                            q2 = sq.next()
                            S.op("pool", [y], [q2], lambda e: e.tensor_tensor(out=q2.t[:, 0:L], in0=y.t[:, 0:L], in1=y.t[:, 0:L], op=ALU.mult))
                            o = yn.next()
                            for tt in range(0, L, 512):
                                tw = min(512, L - tt)
                                p = pss.next()
                                S.op("pe", [q2, onesb], [p], lambda e: e.matmul(p.t[:, 0:tw], lhsT=onesb.t[:], rhs=q2.t[:, tt:tt + tw], start=True, stop=True))
                                r = rn.next()
                                S.op("act", [p], [r], lambda e: e.activation(out=r.t[:, 0:tw], in_=p.t[:, 0:tw], func=AF.Sqrt, bias=1e-6, scale=1.0))
                                S.op("dve", [r], [r], lambda e: e.reciprocal(out=r.t[:, 0:tw], in_=r.t[:, 0:tw]))
                                sc = (128.0 ** -0.5) if c < 8 else 1.0
                                S.op("dve", [r, y], [o], lambda e: e.scalar_tensor_tensor(
                                    out=o.t[:, tt:tt + tw], in0=y.t[:, tt:tt + tw], scalar=sc, in1=r.t[:, 0:tw], op0=ALU.mult, op1=ALU.mult))
                            dstT = qTn if c < 8 else kTn
                            h = c % 8
                            S.dma("pool", dstT[h * 128:(h + 1) * 128, t0:t0 + L], o.t[:, 0:L], reads=[o])
                        if c >= 8:
                            h = c % 8
                            if c < 16:
                                yf = ynf.next()
                                S.op("pool", [o], [yf], lambda e: e.tensor_copy(out=yf.t[:, 0:L], in_=o.t[:, 0:L]))
                            else:
                                yf = y
                            dst = ktok if c < 16 else vtok
                            for tg in range(0, L, 512):
                                ng = min(4, (L - tg) // 128)
                                p = ptr.next()

                                def f(e, p=p, tg=tg, ng=ng, yf=yf):
                                    for j in range(ng):
                                        i = e.transpose(p.t[:, j * 128:(j + 1) * 128], yf.t[:, tg + j * 128:tg + (j + 1) * 128], ident)
                                    return i
                                S.op("pe", [yf, cm], [p], f)
                                st = tst.next()
                                S.op("act", [p], [st], lambda e: e.copy(out=st.t[:, 0:ng, :], in_=p.t[:, 0:ng * 128].rearrange("p (j d) -> p j d", d=128)))
                                S.dma("pool", dst[t0 + tg:t0 + tg + ng * 128, h * 128:(h + 1) * 128].rearrange("(j p) d -> p j d", p=128),
                                      st.t[:, 0:ng, :], reads=[st])
            S.barrier()
            if upto == 3:
                raise _Stop()
            S.mute = start > 4

            with ExitStack() as es:
                alog = S.sb(es, [128, 16], F32, "alog")
                dtb = S.sb(es, [128, 16], F32, "dtb")
                S.dma("sp", alog.t[:], alog_d[:, :], writes=[alog])
                S.dma("sp", dtb.t[:], dtb_d[:, :], writes=[dtb])
                nea = S.sb(es, [128, 16], F32, "nea")
                S.op("act", [alog], [nea], lambda e: e.activation(out=nea.t[:], in_=alog.t[:], func=AF.Exp))
                S.op("dve", [nea], [nea], lambda e: e.tensor_scalar(out=nea.t[:], in0=nea.t[:], scalar1=-1.0, scalar2=None, op0=ALU.mult))
                qT_p = S.pool(es, 2, [128, 8, 128], BF16, "gqT")
                kT_p = S.pool(es, 2, [128, 8, 128], BF16, "gkT")
                kt_p = S.pool(es, 2, [128, 8, 128], BF16, "gkt")
                vt_p = S.pool(es, 2, [128, 8, 128], BF16, "gvt")
                ab_p = S.pool(es, 2, [128, 32], F32, "gab")
                sm = {n: S.sb(es, [128, 8], F32, "g" + n) for n in
                      ("g", "beta", "nbeta", "gc", "gl", "egc", "egl", "ekd", "bg", "tmp", "ngc")}
                dg = S.sb(es, [128, 8, 128], F32, "dg")
                Dm = S.sb(es, [128, 8, 128], F32, "Dm")
                DmT = S.sb(es, [128, 8, 128], F32, "DmT")
                Ei = S.sb(es, [128, 8, 128], F32, "Ei")
                Es = S.sb(es, [128, 8, 128], F32, "Es")
                EiT = S.sb(es, [128, 8, 128], F32, "EiT")
                Pm = Ring([S.sb(es, [128, 8, 128], F32, "Pm") for _ in range(2)])
                PTm = Ring([S.sb(es, [128, 8, 128], F32, "PTm") for _ in range(2)])
                TT = S.sb(es, [128, 8, 128], F32, "TT")
                QKmT = S.sb(es, [128, 8, 128], BF16, "QKmT")
                rv = S.sb(es, [128, 8, 128], F32, "rv")
                rk = S.sb(es, [128, 8, 128], F32, "rk")
                kdec = S.sb(es, [128, 8, 128], BF16, "kdec")
                wv = S.sb(es, [128, 8, 128], F32, "wv")
                wkT = S.sb(es, [128, 8, 128], BF16, "wkT")
                u_b = S.sb(es, [128, 8, 128], BF16, "u_b")
                St = S.sb(es, [128, 8, 128], F32, "St")
                Stmp = S.sb(es, [128, 8, 128], F32, "Stmp")
                S_b = S.sb(es, [128, 8, 128], BF16, "S_b")
                o1 = S.sb(es, [128, 8, 128], F32, "o1")
                o2 = S.sb(es, [128, 8, 128], F32, "o2")
                oo = S.pool(es, 2, [128, 8, 128], F32, "oo")
                PA = S.ps(es, [128, 8, 128], F32, "PA")
                PB = S.ps(es, [128, 8, 128], F32, "PB")
                PC = S.ps(es, [128, 8, 128], F32, "PC")
                PD = S.ps(es, [128, 512], F32, "PD")
                ones32 = cm.t[:, 1, :]

                def flat(t):
                    return t.t[:].rearrange("p h d -> p (h d)")

                for s in range(NS):
                    t0s, L = soff[s], seqs[s]
                    ntile = L // 128
                    for di in range(2):
                        if di == 0:
                            CS, MI_, MS_, MIT_ = cm.t[:, 3, :], cm.t[:, 2, :], cm.t[:, 4, :], cm.t[:, 3, :]
                        else:
                            CS, MI_, MS_, MIT_ = cm.t[:, 2, :], cm.t[:, 3, :], cm.t[:, 5, :], cm.t[:, 2, :]
                        S.op("pool", [], [St], lambda e: e.memset(St.t[:], 0.0))
                        S.op("pool", [], [S_b], lambda e: e.memset(S_b.t[:], 0.0))
                        order = range(ntile) if di == 0 else range(ntile - 1, -1, -1)
                        for j in order:
                            tt = t0s + j * 128
                            qT_, kT_, kt_, vt_, ab = qT_p.next(), kT_p.next(), kt_p.next(), vt_p.next(), ab_p.next()
                            S.dma("sp", qT_.t[:], qTn[:, tt:tt + 128].rearrange("(h p) t -> p h t", p=128), writes=[qT_])
                            S.dma("sp", kT_.t[:], kTn[:, tt:tt + 128].rearrange("(h p) t -> p h t", p=128), writes=[kT_])
                            S.dma("sp", kt_.t[:], ktok[tt:tt + 128, :].rearrange("p (h d) -> p h d", d=128), writes=[kt_])
                            S.dma("sp", vt_.t[:], vtok[tt:tt + 128, :].rearrange("p (h d) -> p h d", d=128), writes=[vt_])
                            S.dma("sp", ab.t[:], gab[tt:tt + 128, :], writes=[ab])
                            g, beta, nbeta, gc, gl = sm["g"], sm["beta"], sm["nbeta"], sm["gc"], sm["gl"]
                            egc, egl, ekd, bgm, tmp, ngc = sm["egc"], sm["egl"], sm["ekd"], sm["bg"], sm["tmp"], sm["ngc"]
                            a0 = di * 8
                            S.op("dve", [ab, dtb], [tmp], lambda e: e.tensor_tensor(out=tmp.t[:], in0=ab.t[:, a0:a0 + 8], in1=dtb.t[:, a0:a0 + 8], op=ALU.add))
                            S.op("act", [tmp], [tmp], lambda e: e.activation(out=tmp.t[:], in_=tmp.t[:], func=AF.Exp))
                            S.op("act", [tmp], [tmp], lambda e: e.activation(out=tmp.t[:], in_=tmp.t[:], func=AF.Ln, bias=1.0, scale=1.0))
                            S.op("dve", [tmp, nea], [g], lambda e: e.tensor_tensor(out=g.t[:], in0=tmp.t[:], in1=nea.t[:, a0:a0 + 8], op=ALU.mult))
                            S.op("act", [ab], [beta], lambda e: e.activation(out=beta.t[:], in_=ab.t[:, 16 + a0:24 + a0], func=AF.Sigmoid))
                            S.op("dve", [beta], [nbeta], lambda e: e.tensor_scalar(out=nbeta.t[:], in0=beta.t[:], scalar1=-1.0, scalar2=None, op0=ALU.mult))

                            def f(e):
                                e.matmul(PD.t[:, 0:8], lhsT=CS, rhs=g.t[:], start=True, stop=True)
                                return e.matmul(PD.t[:, 8:16], lhsT=ones32, rhs=g.t[:], start=True, stop=True)
                            S.op("pe", [g, cm], [PD], f)
                            S.op("dve", [PD], [gc], lambda e: e.tensor_copy(out=gc.t[:], in_=PD.t[:, 0:8]))
                            S.op("dve", [PD], [gl], lambda e: e.tensor_copy(out=gl.t[:], in_=PD.t[:, 8:16]))
                            S.op("dve", [gc], [ngc], lambda e: e.tensor_scalar(out=ngc.t[:], in0=gc.t[:], scalar1=-1.0, scalar2=None, op0=ALU.mult))
                            S.op("act", [gc], [egc], lambda e: e.activation(out=egc.t[:], in_=gc.t[:], func=AF.Exp))
                            S.op("act", [gl], [egl], lambda e: e.activation(out=egl.t[:], in_=gl.t[:], func=AF.Exp))
                            S.op("dve", [gl, gc], [ekd], lambda e: e.tensor_tensor(out=ekd.t[:], in0=gl.t[:], in1=gc.t[:], op=ALU.subtract))
                            S.op("act", [ekd], [ekd], lambda e: e.activation(out=ekd.t[:], in_=ekd.t[:], func=AF.Exp))
                            S.op("dve", [beta, egc], [bgm], lambda e: e.tensor_tensor(out=bgm.t[:], in0=beta.t[:], in1=egc.t[:], op=ALU.mult))
                            if bisect == 1:
                                S.mute = True
                            S.op("dve", [ngc, cm], [dg], lambda e: e.tensor_tensor(
                                out=dg.t[:], in0=bc_mid(cm.t[:, 0, :], 8), in1=bc_last(ngc.t[:], 128), op=ALU.mult))

                            def f(e):
                                for hh in range(0, 8, 4):
                                    i = e.matmul(PA.t[:, hh:hh + 4, :], lhsT=ones32, rhs=dg.t[:, hh:hh + 4, :], start=True, stop=True)
                                return i
                            S.op("pe", [dg, cm], [PA], f)
                            S.op("dve", [PA, gc], [Dm], lambda e: e.tensor_tensor(out=Dm.t[:], in0=PA.t[:], in1=bc_last(gc.t[:], 128), op=ALU.add))
                            S.op("pool", [Dm], [DmT], lambda e: e.tensor_scalar(out=DmT.t[:], in0=Dm.t[:], scalar1=0.0, scalar2=None, op0=ALU.max))
                            S.op("dve", [Dm], [Dm], lambda e: e.tensor_scalar(out=Dm.t[:], in0=Dm.t[:], scalar1=0.0, scalar2=None, op0=ALU.min))
                            S.op("act", [Dm], [Ei], lambda e: e.activation(out=Ei.t[:], in_=Dm.t[:], func=AF.Exp))
                            S.op("act", [DmT], [EiT], lambda e: e.activation(out=EiT.t[:], in_=DmT.t[:], func=AF.Exp, scale=-1.0))
                            S.op("dve", [Ei, cm], [Es], lambda e: e.tensor_tensor(out=Es.t[:], in0=Ei.t[:], in1=bc_mid(MS_, 8), op=ALU.mult))
                            S.op("dve", [EiT, cm], [EiT], lambda e: e.tensor_tensor(out=EiT.t[:], in0=EiT.t[:], in1=bc_mid(MIT_, 8), op=ALU.mult))
                            if bisect == 2:
                                S.mute = True

                            def f(e):
                                for h in range(8):
                                    i = e.matmul(PB.t[:, h, :], lhsT=kT_.t[:, h, :], rhs=kT_.t[:, h, :], start=True, stop=True)
                                return i
                            S.op("pe", [kT_], [PB], f)

                            def f(e):
                                for h in range(8):
                                    i = e.matmul(PC.t[:, h, :], lhsT=kT_.t[:, h, :], rhs=qT_.t[:, h, :], start=True, stop=True)
                                return i
                            S.op("pe", [kT_, qT_], [PC], f)
                            if bisect == 31:
                                S.mute = True
                            P0 = Pm.next()
                            PT0 = PTm.next()
                            S.op("dve", [PB, Es], [P0], lambda e: e.tensor_tensor(out=P0.t[:], in0=PB.t[:], in1=Es.t[:], op=ALU.mult))
                            S.op("dve", [P0, nbeta], [P0], lambda e: e.tensor_tensor(out=P0.t[:], in0=P0.t[:], in1=bc_last(nbeta.t[:], 128), op=ALU.mult))
                            S.op("dve", [PC, EiT], [QKmT], lambda e: e.tensor_tensor(out=QKmT.t[:], in0=PC.t[:], in1=EiT.t[:], op=ALU.mult))
                            if bisect == 32:
                                S.mute = True

                            def f(e):
                                for h in range(8):
                                    i = e.matmul(PA.t[:, h, :], lhsT=P0.t[:, h, :], rhs=ident, start=True, stop=True)
                                return i
                            S.op("pe", [P0, cm], [PA], f)
                            if bisect == 33:
                                S.mute = True
                            S.op("act", [PA], [PT0], lambda e: act_copy2(e, PT0, PA))
                            S.op("dve", [PT0, cm], [TT], lambda e: e.tensor_tensor(out=TT.t[:], in0=PT0.t[:], in1=bc_mid(cm.t[:, 0, :], 8), op=ALU.add))
                            if bisect == 3:
                                S.mute = True
                            Pc, PTc = P0, PT0
                            for lvl in range(1, 7):
                                Pn = Pm.next()
                                PTn = PTm.next()

                                def f(e, Pc=Pc, PTc=PTc):
                                    for h in range(8):
                                        i = e.matmul(PA.t[:, h, :], lhsT=PTc.t[:, h, :], rhs=Pc.t[:, h, :], start=True, stop=True)
                                    return i
                                S.op("pe", [Pc, PTc], [PA], f)
                                S.op("act", [PA], [Pn], lambda e: act_copy2(e, Pn, PA))
                                if lvl < 6:
                                    def f(e, Pc=Pc, PTc=PTc):
                                        for h in range(8):
                                            i = e.matmul(PB.t[:, h, :], lhsT=Pc.t[:, h, :], rhs=PTc.t[:, h, :], start=True, stop=True)
                                        return i
                                    S.op("pe", [Pc, PTc], [PB], f)
                                    S.op("dve", [PB], [PTn], lambda e: e.tensor_copy(out=PTn.t[:], in_=PB.t[:]))

                                def f(e, Pn=Pn):
                                    for h in range(8):
                                        i = e.matmul(PC.t[:, h, :], lhsT=Pn.t[:, h, :], rhs=TT.t[:, h, :], start=True, stop=True)
                                    return i
                                S.op("pe", [Pn, TT], [PC], f)
                                S.op("dve", [PC, TT], [TT], lambda e: e.tensor_tensor(out=TT.t[:], in0=PC.t[:], in1=TT.t[:], op=ALU.add))
                                Pc, PTc = Pn, PTn
                            if bisect == 4:
                                S.mute = True
                            S.op("pool", [vt_, beta], [rv], lambda e: e.tensor_tensor(out=rv.t[:], in0=vt_.t[:], in1=bc_last(beta.t[:], 128), op=ALU.mult))
                            S.op("pool", [kt_, bgm], [rk], lambda e: e.tensor_tensor(out=rk.t[:], in0=kt_.t[:], in1=bc_last(bgm.t[:], 128), op=ALU.mult))
                            S.op("pool", [kt_, ekd], [kdec], lambda e: e.tensor_tensor(out=kdec.t[:], in0=kt_.t[:], in1=bc_last(ekd.t[:], 128), op=ALU.mult))

                            def f(e):
                                for h in range(8):
                                    i = e.matmul(PA.t[:, h, :], lhsT=TT.t[:, h, :], rhs=rv.t[:, h, :], start=True, stop=True)
                                return i
                            S.op("pe", [TT, rv], [PA], f)

                            def f(e):
                                for h in range(8):
                                    i = e.matmul(PB.t[:, h, :], lhsT=rk.t[:, h, :], rhs=TT.t[:, h, :], start=True, stop=True)
                                return i
                            S.op("pe", [TT, rk], [PB], f)
                            S.op("act", [PA], [wv], lambda e: act_copy2(e, wv, PA))
                            S.op("dve", [PB], [wkT], lambda e: e.tensor_copy(out=wkT.t[:], in_=PB.t[:]))
                            if bisect == 5:
                                S.mute = True

                            def f(e):
                                for h in range(8):
                                    i = e.matmul(PA.t[:, h, :], lhsT=wkT.t[:, h, :], rhs=S_b.t[:, h, :], start=True, stop=True)
                                return i
                            S.op("pe", [wkT, S_b], [PA], f)

                            def f(e):
                                for h in range(8):
                                    i = e.matmul(PB.t[:, h, :], lhsT=qT_.t[:, h, :], rhs=S_b.t[:, h, :], start=True, stop=True)
                                return i
                            S.op("pe", [qT_, S_b], [PB], f)
                            S.op("dve", [PA, wv], [u_b], lambda e: e.tensor_tensor(out=u_b.t[:], in0=wv.t[:], in1=PA.t[:], op=ALU.subtract))
                            S.op("dve", [PB, egc], [o1], lambda e: e.tensor_tensor(out=o1.t[:], in0=PB.t[:], in1=bc_last(egc.t[:], 128), op=ALU.mult))

                            def f(e):
                                for h in range(8):
                                    i = e.matmul(PC.t[:, h, :], lhsT=QKmT.t[:, h, :], rhs=u_b.t[:, h, :], start=True, stop=True)
                                return i
                            S.op("pe", [QKmT, u_b], [PC], f)

                            def f(e):
                                for h in range(8):
                                    i = e.matmul(PA.t[:, h, :], lhsT=kdec.t[:, h, :], rhs=u_b.t[:, h, :], start=True, stop=True)
                                return i
                            S.op("pe", [kdec, u_b], [PA], f)
                            S.op("act", [PC], [o2], lambda e: act_copy2(e, o2, PC))
                            ot = oo.next()
                            S.op("pool", [o1, o2], [ot], lambda e: e.tensor_tensor(out=ot.t[:], in0=o1.t[:], in1=o2.t[:], op=ALU.add))
                            S.dma("pool", odir[di, tt:tt + 128, :], flat(ot), reads=[ot])
                            S.op("pool", [St, egl], [Stmp], lambda e: e.tensor_tensor(out=Stmp.t[:], in0=St.t[:], in1=bc_last(egl.t[:], 128), op=ALU.mult))
                            S.op("dve", [PA, Stmp], [St], lambda e: e.tensor_tensor(out=St.t[:], in0=PA.t[:], in1=Stmp.t[:], op=ALU.add))
                            S.op("act", [St], [S_b], lambda e: e.copy(out=S_b.t[:], in_=St.t[:]))
            S.barrier()
            if upto == 4:
                raise _Stop()
            S.mute = start > 5

            def tm_to_fm(es_, name):
                ptr = Ring([S.ps(es_, [128, 512], F32, name + "pt") for _ in range(2)])
                return ptr

            with ExitStack() as es:
                gnw = S.sb(es, [128, 128], F32, "gnw")
                S.dma("sp", gnw.t[:], gnw_d[:, :], writes=[gnw])
                of_p = S.pool(es, 2, [128, 8, 128], F32, "of")
                ob_p = S.pool(es, 2, [128, 8, 128], F32, "ob")
                z_p = S.pool(es, 2, [128, 8, 128], BF16, "zz")
                sq_p = S.pool(es, 2, [128, 8, 128], F32, "gsq")
                ss_p = S.pool(es, 2, [128, 8], F32, "gss")
                yg_p = S.pool(es, 2, [128, 8, 128], F32, "yg")
                stg_p = S.pool(es, 2, [128, 8, 512], BF16, "ygst")
                ptr = Ring([S.ps(es, [128, 512], F32, "gpt") for _ in range(4)])
                sup = 512 if all(L % 512 == 0 for L in seqs) else 128
                for tg in range(0, T, sup):
                    stg = stg_p.next()
                    for jj in range(sup // 128):
                        tt = tg + jj * 128
                        of, ob, zt = of_p.next(), ob_p.next(), z_p.next()
                        S.dma("sp", of.t[:], odir[0, tt:tt + 128, :].rearrange("p (h d) -> p h d", d=128), writes=[of])
                        S.dma("sp", ob.t[:], odir[1, tt:tt + 128, :].rearrange("p (h d) -> p h d", d=128), writes=[ob])
                        S.dma("sp", zt.t[:], gz[tt:tt + 128, :].rearrange("p (h d) -> p h d", d=128), writes=[zt])
                        S.op("pool", [of, ob], [of], lambda e: e.tensor_tensor(out=of.t[:], in0=of.t[:], in1=ob.t[:], op=ALU.add))
                        sqt, sst, yg = sq_p.next(), ss_p.next(), yg_p.next()
                        S.op("act", [of], [sqt], lambda e: e.activation(out=sqt.t[:], in_=of.t[:], func=AF.Square))
                        S.op("dve", [sqt], [sst], lambda e: e.tensor_reduce(out=sst.t[:], in_=sqt.t[:], axis=AX.X, op=ALU.add))
                        S.op("act", [sst], [sst], lambda e: e.activation(out=sst.t[:], in_=sst.t[:], func=AF.Sqrt, bias=1e-6, scale=1.0 / 128.0))
                        S.op("dve", [sst], [sst], lambda e: e.reciprocal(out=sst.t[:], in_=sst.t[:]))
                        S.op("dve", [of, sst], [yg], lambda e: e.tensor_tensor(out=yg.t[:], in0=of.t[:], in1=bc_last(sst.t[:], 128), op=ALU.mult))
                        S.op("dve", [yg, gnw], [yg], lambda e: e.tensor_tensor(out=yg.t[:], in0=yg.t[:], in1=bc_mid(gnw.t[:], 8), op=ALU.mult))
                        S.op("dve", [yg, zt], [yg], lambda e: e.tensor_tensor(out=yg.t[:], in0=yg.t[:], in1=zt.t[:], op=ALU.mult))
                        for h0 in range(0, 8, 4):
                            p = ptr.next()

                            def f(e, p=p, h0=h0, yg=yg):
                                for h in range(4):
                                    i = e.transpose(p.t[:, h * 128:(h + 1) * 128], yg.t[:, h0 + h, :], ident)
                                return i
                            S.op("pe", [yg, cm], [p], f)
                            S.op("act", [p], [stg], lambda e: e.copy(out=stg.t[:, h0:h0 + 4, jj * 128:(jj + 1) * 128],
                                                                      in_=p.t[:].rearrange("p (h t) -> p h t", t=128)))
                    S.dma("pool", ybT[0][:, tg:tg + sup].rearrange("(h p) t -> p h t", p=128), stg.t[:, :, 0:sup], reads=[stg])
            S.barrier()
            if upto == 5:
                raise _Stop()
            S.mute = start > 6

            with ExitStack() as es:
                relb = S.sb(es, [32, 4], F32, "relb")
                oh = S.sb(es, [32, 1280], F32, "oh")
                S.dma("sp", relb.t[:], relb_d[:, :], writes=[relb])
                S.dma("sp", oh.t[:], oh_d[:, :], writes=[oh])
                tvs = S.sb(es, [4, 1280], BF16, "tvs")
                pq = Ring([S.ps(es, [128, 512], F32, "dps") for _ in range(3)])
                po = [S.ps(es, [128, 512], F32, "dpo") for _ in range(4)]
                for c0 in range(0, 1280, 512):
                    cw = min(512, 1280 - c0)
                    p = pq.next()
                    S.op("pe", [relb, oh], [p], lambda e: e.matmul(p.t[0:4, 0:cw], lhsT=relb.t[:, :], rhs=oh.t[:, c0:c0 + cw], start=True, stop=True))
                    S.op("dve", [p], [tvs], lambda e: e.tensor_scalar(out=tvs.t[:, c0:c0 + cw], in0=p.t[0:4, 0:cw], scalar1=128.0 ** 0.5, scalar2=None, op0=ALU.mult))
                S.dma("pool", tvec[:, :], tvs.t[:], reads=[tvs])
                relbb = S.sb(es, [128, 128], F32, "relbb")
                S.dma("sp", relbb.t[:], bass.AP(relb_d.tensor, 0, [[0, 128], [1, 128]]), writes=[relbb])
                dl = S.sb(es, [128, 4, 128], F32, "dl")
                S.dma("sp", dl.t[:], dlam_d[:, :, :], writes=[dl])
                dnw = S.sb(es, [128, 256], F32, "dnw")
                S.dma("sp", dnw.t[:], dnw_d[:, :], writes=[dnw])
                lt = S.sb(es, [128, 2, 128], F32, "lt")
                l2 = S.sb(es, [128, 2], F32, "l2")
                nlam = S.sb(es, [128, 1], F32, "nlam")
                S.op("dve", [dl], [lt], lambda e: e.tensor_tensor(out=lt.t[:, 0, :], in0=dl.t[:, 0, :], in1=dl.t[:, 1, :], op=ALU.mult))
                S.op("dve", [dl, lt], [lt], lambda e: e.tensor_tensor(out=lt.t[:, 1, :], in0=dl.t[:, 2, :], in1=dl.t[:, 3, :], op=ALU.mult))
                S.op("dve", [lt], [l2], lambda e: e.tensor_reduce(out=l2.t[:], in_=lt.t[:], axis=AX.X, op=ALU.add))
                S.op("act", [l2], [l2], lambda e: e.activation(out=l2.t[:], in_=l2.t[:], func=AF.Exp))
                S.op("dve", [l2], [nlam], lambda e: e.tensor_tensor(out=nlam.t[:], in0=l2.t[:, 1:2], in1=l2.t[:, 0:1], op=ALU.subtract))
                S.op("dve", [nlam], [nlam], lambda e: e.tensor_scalar(out=nlam.t[:], in0=nlam.t[:], scalar1=-LAMBDA_INIT, scalar2=None, op0=ALU.add))
                S.barrier()
                SM = max(seqs)
                Rt = [[S.sb(es, [128, 512], BF16, "Rt") for _ in range(6)] for _ in range(4)]
                for h in range(4):
                    for dlt in range(-1, 5):
                        off = 639 - dlt * 128 - 127
                        S.dma("sp", Rt[h][dlt + 1].t[:], bass.AP(tvec.tensor, h * 1280 + off, [[1, 128], [1, 512]]), writes=[Rt[h][dlt + 1]])
                kT_p = S.pool(es, 1, [128, 2, SM], BF16, "dkT")
                qT_p = S.pool(es, 1, [128, 2, SM], BF16, "dqT")
                v_p = S.pool(es, 1, [128, SM // 128, 258], BF16, "dv")
                pT_p = S.pool(es, 3, [128, 512], BF16, "dpT")
                om = S.sb(es, [128, 4, 2, 256], F32, "om")
                rs = S.pool(es, 2, [128, 1], F32, "drs")
                od = S.pool(es, 2, [128, 256], F32, "dod")
                sqd = S.pool(es, 2, [128, 256], F32, "dsq")
                ssd = S.pool(es, 2, [128, 1], F32, "dss")
                stq = S.pool(es, 2, [128, 2, 512], BF16, "dstq")
                ptr = Ring([S.ps(es, [128, 512], F32, "dpt") for _ in range(1)])
                scale = 128.0 ** -0.5
                for s in range(NS):
                    t0s, L = soff[s], seqs[s]
                    nkb = L // 128
                    QT = 512 if L % 512 == 0 else 128
                    for h in range(4):
                        kT_, qT_, v_ = kT_p.next(), qT_p.next(), v_p.next()
                        S.dma("sp", qT_.t[:, :, 0:L], dqkT[h * 256:(h + 1) * 256, t0s:t0s + L].rearrange("(m p) t -> p m t", p=128), writes=[qT_])
                        S.dma("sp", kT_.t[:, :, 0:L], dqkT[1024 + h * 256:1024 + (h + 1) * 256, t0s:t0s + L].rearrange("(m p) t -> p m t", p=128), writes=[kT_])
                        S.dma("sp", v_.t[:, 0:nkb, 0:256], dv[t0s:t0s + L, h * 256:(h + 1) * 256].rearrange("(j p) d -> p j d", p=128), writes=[v_])
                        S.op("pool", [], [v_], lambda e: e.memset(v_.t[:, 0:nkb, 256:258], 1.0))
                        for q0 in range(0, L, QT):
                            nqs = QT // 128
                            items = [(m, kb) for m in range(2) for kb in range(nkb)]
                            LA = 2

                            def emit_qk(m, kb):
                                k0 = kb * 128
                                rmin = k0 - (q0 + QT - 1)
                                rmax = k0 + 127 - q0
                                near = not (rmin >= 91 or rmax <= -91)
                                p = pq.next()
                                if near:
                                    dlt = (k0 - q0) // 128
                                    R = Rt[h][dlt + 1]

                                    def f(e):
                                        e.matmul(p.t[:, 0:QT], lhsT=kT_.t[:, m, k0:k0 + 128], rhs=qT_.t[:, m, q0:q0 + QT], start=True, stop=False)
                                        return e.matmul(p.t[:, 0:QT], lhsT=antib.t[:], rhs=R.t[:, 0:QT], start=False, stop=True)
                                    S.op("pe", [kT_, qT_, antib, R], [p], f)
                                    return p, None
                                S.op("pe", [kT_, qT_], [p], lambda e: e.matmul(p.t[:, 0:QT], lhsT=kT_.t[:, m, k0:k0 + 128], rhs=qT_.t[:, m, q0:q0 + QT], start=True, stop=True))
                                return p, (31 if rmin > 0 else 15) * 4 + h

                            pend = {}
                            for idx in range(min(LA, len(items))):
                                pend[idx] = emit_qk(*items[idx])
                            for idx, (m, kb) in enumerate(items):
                                if idx + LA < len(items):
                                    pend[idx + LA] = emit_qk(*items[idx + LA])
                                p, bcol = pend.pop(idx)
                                pT = pT_p.next()
                                if bcol is None:
                                    S.op("act", [p], [pT], lambda e: e.activation(out=pT.t[:, 0:QT], in_=p.t[:, 0:QT], func=AF.Exp, scale=scale))
                                else:
                                    S.op("act", [p, relbb], [pT], lambda e: e.activation(out=pT.t[:, 0:QT], in_=p.t[:, 0:QT], func=AF.Exp,
                                                                                          bias=relbb.t[:, bcol:bcol + 1], scale=scale))

                                def f(e):
                                    for qs in range(nqs):
                                        i = e.matmul(po[qs].t[:, 0:258], lhsT=pT.t[:, qs * 128:(qs + 1) * 128], rhs=v_.t[:, kb, :],
                                                     start=(kb == 0), stop=(kb == nkb - 1))
                                    return i
                                S.op("pe", [pT, v_], po[0:nqs], f)
                                if kb == nkb - 1:
                                    for qs in range(nqs):
                                        r = rs.next()
                                        S.op("dve", [po[qs]], [r], lambda e: e.reciprocal(out=r.t[:], in_=po[qs].t[:, 256:257]))
                                        if m == 1:
                                            S.op("dve", [r, nlam], [r], lambda e: e.tensor_tensor(out=r.t[:], in0=r.t[:], in1=nlam.t[:], op=ALU.mult))
                                        S.op("dve", [po[qs], r], [om], lambda e: e.tensor_scalar(out=om.t[:, qs, m, :], in0=po[qs].t[:, 0:256], scalar1=r.t[:, 0:1], scalar2=None, op0=ALU.mult))
                            st = stq.next()
                            for qs in range(nqs):
                                o, sqq, ssq = od.next(), sqd.next(), ssd.next()
                                S.op("pool", [om], [o], lambda e: e.tensor_tensor(out=o.t[:], in0=om.t[:, qs, 0, :], in1=om.t[:, qs, 1, :], op=ALU.add))
                                S.op("act", [o], [sqq], lambda e: e.activation(out=sqq.t[:], in_=o.t[:], func=AF.Square))
                                S.op("dve", [sqq], [ssq], lambda e: e.tensor_reduce(out=ssq.t[:], in_=sqq.t[:], axis=AX.X, op=ALU.add))
                                S.op("act", [ssq], [ssq], lambda e: e.activation(out=ssq.t[:], in_=ssq.t[:], func=AF.Sqrt, bias=1e-6, scale=1.0 / 256.0))
                                S.op("dve", [ssq], [ssq], lambda e: e.reciprocal(out=ssq.t[:], in_=ssq.t[:]))
                                S.op("dve", [o, ssq], [o], lambda e: e.tensor_scalar(out=o.t[:], in0=o.t[:], scalar1=ssq.t[:, 0:1], scalar2=(1.0 - LAMBDA_INIT), op0=ALU.mult, op1=ALU.mult))
                                S.op("pool", [o, dnw], [o], lambda e: e.tensor_tensor(out=o.t[:], in0=o.t[:], in1=dnw.t[:], op=ALU.mult))
                                p = ptr.next()

                                def f(e, p=p, o=o):
                                    e.transpose(p.t[:, 0:128], o.t[:, 0:128], ident)
                                    return e.transpose(p.t[:, 128:256], o.t[:, 128:256], ident)
                                S.op("pe", [o, cm], [p], f)
                                S.op("act", [p], [st], lambda e: e.copy(out=st.t[:, :, qs * 128:(qs + 1) * 128], in_=p.t[:, 0:256].rearrange("p (c t) -> p c t", t=128)))
                            S.dma("pool", ybT[1][h * 256:(h + 1) * 256, t0s + q0:t0s + q0 + QT].rearrange("(c p) t -> p c t", p=128), st.t[:, :, 0:QT], reads=[st])
            S.barrier()
            if upto == 6:
                raise _Stop()
            S.mute = start > 7

            with ExitStack() as es:
                km_p = S.pool(es, 2, [128, 8, MEM], BF16, "ckm")
                vm_p = S.pool(es, 2, [128, 2, 4, 258], BF16, "cvm")
                q_p = S.pool(es, 2, [128, 8, 512], BF16, "cq")
                pT_p = S.pool(es, 3, [128, 512], BF16, "cpT")
                pq = Ring([S.ps(es, [128, 512], F32, "cps") for _ in range(2)])
                po = Ring([S.ps(es, [128, 512], F32, "cpo") for _ in range(4)])
                ptr = Ring([S.ps(es, [128, 512], F32, "cpt") for _ in range(2)])
                rs = S.pool(es, 2, [128, 1], F32, "crs")
                oc = S.pool(es, 2, [128, 256], F32, "coc")
                stq = S.pool(es, 2, [128, 8, 512], BF16, "cstq")
                scale = 256.0 ** -0.5
                for s in range(NS):
                    t0s, L = soff[s], seqs[s]
                    km, vmt = km_p.next(), vm_p.next()
                    S.dma("sp", km.t[:], kmT[:, s * MEM:(s + 1) * MEM].rearrange("(c p) t -> p c t", p=128), writes=[km])
                    for b in range(2):
                        S.dma("sp", vmt.t[:, b, :, 0:256], vm[s * MEM + b * 128:s * MEM + (b + 1) * 128, :].rearrange("p (h d) -> p h d", d=256), writes=[vmt])
                    S.op("pool", [], [vmt], lambda e: e.memset(vmt.t[:, :, :, 256:258], 1.0))
                    QT = 512 if L % 512 == 0 else 128
                    for q0 in range(0, L, QT):
                        nqs = QT // 128
                        qt = q_p.next()
                        S.dma("sp", qt.t[:, :, 0:QT], cqT[:, t0s + q0:t0s + q0 + QT].rearrange("(c p) t -> p c t", p=128), writes=[qt])
                        st = stq.next()
                        for h in range(4):
                            pTs = []
                            for kb in range(2):
                                p = pq.next()

                                def f(e, p=p, kb=kb):
                                    e.matmul(p.t[:, 0:QT], lhsT=km.t[:, 2 * h, kb * 128:(kb + 1) * 128], rhs=qt.t[:, 2 * h, 0:QT], start=True, stop=False)
                                    return e.matmul(p.t[:, 0:QT], lhsT=km.t[:, 2 * h + 1, kb * 128:(kb + 1) * 128], rhs=qt.t[:, 2 * h + 1, 0:QT], start=False, stop=True)
                                S.op("pe", [km, qt], [p], f)
                                pT = pT_p.next()
                                S.op("act", [p], [pT], lambda e: e.activation(out=pT.t[:, 0:QT], in_=p.t[:, 0:QT], func=AF.Exp, scale=scale))
                                pTs.append(pT)
                            for qs in range(nqs):
                                pp_ = po.next()

                                def f(e, pp_=pp_, qs=qs):
                                    e.matmul(pp_.t[:, 0:258], lhsT=pTs[0].t[:, qs * 128:(qs + 1) * 128], rhs=vmt.t[:, 0, h, :], start=True, stop=False)
                                    return e.matmul(pp_.t[:, 0:258], lhsT=pTs[1].t[:, qs * 128:(qs + 1) * 128], rhs=vmt.t[:, 1, h, :], start=False, stop=True)
                                S.op("pe", [pTs[0], pTs[1], vmt], [pp_], f)
                                r, o = rs.next(), oc.next()
                                S.op("dve", [pp_], [r], lambda e: e.reciprocal(out=r.t[:], in_=pp_.t[:, 256:257]))
                                S.op("dve", [pp_, r], [o], lambda e: e.tensor_scalar(out=o.t[:], in0=pp_.t[:, 0:256], scalar1=r.t[:, 0:1], scalar2=None, op0=ALU.mult))
                                p = ptr.next()

                                def f(e, p=p, o=o):
                                    e.transpose(p.t[:, 0:128], o.t[:, 0:128], ident)
                                    return e.transpose(p.t[:, 128:256], o.t[:, 128:256], ident)
                                S.op("pe", [o, cm], [p], f)
                                S.op("act", [p], [st], lambda e: e.copy(out=st.t[:, 2 * h:2 * h + 2, qs * 128:(qs + 1) * 128], in_=p.t[:, 0:256].rearrange("p (c t) -> p c t", t=128)))
                        S.dma("pool", ybT[2][:, t0s + q0:t0s + q0 + QT].rearrange("(c p) t -> p c t", p=128), st.t[:, :, 0:QT], reads=[st])
            S.barrier()
            if upto == 7:
                raise _Stop()
            S.mute = start > 8

            with ExitStack() as es:
                wb_p = S.pool(es, 2, [128, 3, 8, 512], BF16, "mwb")
                y_p = S.pool(es, 2, [128, 3, 8, 512], BF16, "my")
                g_p = S.pool(es, 2, [128, 3, 4, 512], BF16, "mg")
                t_p = S.pool(es, 3, [128, 3, 512], F32, "mt")
                st_p = S.pool(es, 2, [128, 4, 512], BF16, "mst")
                pq = Ring([S.ps(es, [128, 512], F32, "mps") for _ in range(6)])
                sup = 512 if T % 512 == 0 else 128
                for fb in range(4):
                    wt = wb_p.next()
                    for b in range(3):
                        S.dma("sp", wt.t[:, b, :, :], wb_b[b][:, fb * 512:(fb + 1) * 512].rearrange("(c p) n -> p c n", p=128), writes=[wt])
                    for tg in range(0, T, sup):
                        yt, gt = y_p.next(), g_p.next()
                        for b in range(3):
                            S.dma("sp", yt.t[:, b, :, 0:sup], ybT[b][:, tg:tg + sup].rearrange("(c p) t -> p c t", p=128), writes=[yt])
                            S.dma("sp", gt.t[:, b, :, 0:sup], gatesT[b * D + fb * 512:b * D + (fb + 1) * 512, tg:tg + sup].rearrange("(c p) t -> p c t", p=128), writes=[gt])
                        st = st_p.next()
                        for fc in range(4):
                            tmp = t_p.next()
                            for b in range(3):
                                p = pq.next()

                                def f(e, p=p, b=b, fc=fc):
                                    for kc in range(8):
                                        i = e.matmul(p.t[:, 0:sup], lhsT=wt.t[:, b, kc, fc * 128:(fc + 1) * 128], rhs=yt.t[:, b, kc, 0:sup], start=(kc == 0), stop=(kc == 7))
                                    return i
                                S.op("pe", [wt, yt], [p], f)
                                S.op("dve", [p, gt], [tmp], lambda e: e.tensor_tensor(out=tmp.t[:, b, 0:sup], in0=p.t[:, 0:sup], in1=gt.t[:, b, fc, 0:sup], op=ALU.mult))
                            S.op("pool", [tmp], [tmp], lambda e: e.tensor_tensor(out=tmp.t[:, 0, 0:sup], in0=tmp.t[:, 0, 0:sup], in1=tmp.t[:, 1, 0:sup], op=ALU.add))
                            S.op("pool", [tmp], [st], lambda e: e.tensor_tensor(out=st.t[:, fc, 0:sup], in0=tmp.t[:, 0, 0:sup], in1=tmp.t[:, 2, 0:sup], op=ALU.add))
                        S.dma("pool", mT[fb * 512:(fb + 1) * 512, tg:tg + sup].rearrange("(c p) t -> p c t", p=128), st.t[:, :, 0:sup], reads=[st])
            S.barrier()
            if upto == 8:
                raise _Stop()
            S.mute = start > 9

            def layer_norm(r, lnp, gi, es_pools):
                stats, mv, outt = es_pools
                stt = stats.next()
                for c in range(4):
                    S.op("dve", [r], [stt], lambda e: e.bn_stats(out=stt.t[:, c, :], in_=r.t[:, c * 512:(c + 1) * 512]))
                m = mv.next()
                S.op("dve", [stt], [m], lambda e: e.bn_aggr(out=m.t[:, 0:2], in_=stt.t[:]))
                S.op("act", [m], [m], lambda e: e.activation(out=m.t[:, 2:3], in_=m.t[:, 1:2], func=AF.Sqrt, bias=1e-5, scale=1.0))
                S.op("dve", [m], [m], lambda e: e.reciprocal(out=m.t[:, 2:3], in_=m.t[:, 2:3]))
                o = outt.next()
                S.op("dve", [r, m], [o], lambda e: e.tensor_scalar(out=o.t[:], in0=r.t[:], scalar1=m.t[:, 0:1], scalar2=m.t[:, 2:3], op0=ALU.subtract, op1=ALU.mult))
                S.op("pool", [o, lnp], [o], lambda e: e.tensor_tensor(out=o.t[:], in0=o.t[:], in1=lnp.t[:, 0, :], op=ALU.mult))
                S.op("dve", [o, lnp], [o], lambda e: e.tensor_tensor(out=o.t[:], in0=o.t[:], in1=lnp.t[:, 1, :], op=ALU.add))
                return o

            with ExitStack() as es:
                lnp = S.sb(es, [128, 2, D], F32, "lnp")
                S.dma("sp", lnp.t[:], ln_d[:, 0:2, :], writes=[lnp])
                wo = S.sb(es, [128, 16, D], BF16, "wo")
                for c in range(0, 16, 4):
                    S.dma("sp", wo.t[:, c:c + 4, :], wo_b[c * 128:(c + 4) * 128, :].rearrange("(c p) n -> p c n", p=128), writes=[wo])
                sup = 512 if T % 512 == 0 else 128
                m_p = S.pool(es, 2, [128, 16, sup], BF16, "omT")
                x_p = S.pool(es, 2, [128, D], F32, "ox")
                r_p = S.pool(es, 2, [128, D], F32, "or")
                pools = (S.pool(es, 2, [128, 4, 6], F32, "ost"), S.pool(es, 2, [128, 4], F32, "omv"), S.pool(es, 2, [128, D], F32, "oout"))
                xo = S.pool(es, 2, [128, 16, sup], BF16, "oxo")
                pq = Ring([S.ps(es, [128, 512], F32, "ops") for _ in range(4)])
                ptr = Ring([S.ps(es, [128, 512], F32, "opt") for _ in range(4)])
                for tg in range(0, T, sup):
                    mt = m_p.next()
                    S.dma("sp", mt.t[:], mT[:, tg:tg + sup].rearrange("(c p) t -> p c t", p=128), writes=[mt])
                    xot = xo.next()
                    for jj in range(sup // 128):
                        tt = tg + jj * 128
                        xt = x_p.next()
                        S.dma("sp", xt.t[:], x_d[tt:tt + 128, :], writes=[xt])
                        r = r_p.next()
                        for nb in range(4):
                            p = pq.next()

                            def f(e, p=p, nb=nb):
                                for kc in range(16):
                                    i = e.matmul(p.t[:], lhsT=mt.t[:, kc, jj * 128:(jj + 1) * 128], rhs=wo.t[:, kc, nb * 512:(nb + 1) * 512], start=(kc == 0), stop=(kc == 15))
                                return i
                            S.op("pe", [mt, wo], [p], f)
                            S.op("dve", [p, xt], [r], lambda e: e.scalar_tensor_tensor(out=r.t[:, nb * 512:(nb + 1) * 512], in0=xt.t[:, nb * 512:(nb + 1) * 512], scalar=ALPHA,
                                                                                      in1=p.t[:], op0=ALU.mult, op1=ALU.add))
                        o = layer_norm(r, lnp, 0, pools)
                        S.dma("pool", x1[tt:tt + 128, :], o.t[:], reads=[o])
                        for kc0 in range(0, 16, 4):
                            p = ptr.next()

                            def f(e, p=p, kc0=kc0, o=o):
                                for c in range(4):
                                    i = e.transpose(p.t[:, c * 128:(c + 1) * 128], o.t[:, (kc0 + c) * 128:(kc0 + c + 1) * 128], ident)
                                return i
                            S.op("pe", [o, cm], [p], f)
                            S.op("act", [p], [xot], lambda e: e.copy(out=xot.t[:, kc0:kc0 + 4, jj * 128:(jj + 1) * 128], in_=p.t[:].rearrange("p (c t) -> p c t", t=128)))
                    S.dma("pool", x1T[:, tg:tg + sup].rearrange("(c p) t -> p c t", p=128), xot.t[:], reads=[xot])
            S.barrier()
            if upto == 9:
                raise _Stop()
            S.mute = start > 10

            with ExitStack() as es:
                LM = max(s[1] for s in segs)
                fcw = S.sb(es, [128, 2 * NFC, 3], F32, "fcw")
                S.dma("sp", fcw.t[:], fconv_d[:, :, :], writes=[fcw])
                xs_p = S.pool(es, 1, [128, 16, LM + 2], BF16, "fxs")
                w_p = S.pool(es, 3, [128, 2, 16, 128], BF16, "fw")
                u_p = S.pool(es, 2, [128, 2, LM + 2], F32, "fu")
                c_p = S.pool(es, 2, [128, 2, LM], F32, "fc")
                h_p = S.pool(es, 2, [128, LM], BF16, "fh")
                pq = Ring([S.ps(es, [128, 512], F32, "fps") for _ in range(6)])
                for (t0, L, hl, hr) in segs:
                    xs = xs_p.next()
                    lo = 1 - (1 if hl else 0)
                    hi = L + 1 + (1 if hr else 0)
                    if not hl:
                        S.op("pool", [], [xs], lambda e: e.memset(xs.t[:, :, 0:1], 0.0))
                    if not hr:
                        S.op("pool", [], [xs], lambda e: e.memset(xs.t[:, :, L + 1:L + 2], 0.0))
                    S.dma("sp", xs.t[:, :, lo:hi], x1T[:, t0 - 1 + lo:t0 - 1 + hi].rearrange("(c p) t -> p c t", p=128), writes=[xs])
                    cols = list(range(0, L + 2, 512))
                    for c in range(NFC):
                        w = w_p.next()
                        for gv in range(2):
                            cc = gv * DFF + c * 128
                            S.dma("sp", w.t[:, gv, :, :], wu_b[:, cc:cc + 128].rearrange("(c p) n -> p c n", p=128), writes=[w])
                        u = u_p.next()
                        for gv in range(2):
                            for ci, c0 in enumerate(cols):
                                cw = min(512, L + 2 - c0)
                                p = pq.next()

                                def f(e, p=p, gv=gv, c0=c0, cw=cw):
                                    for kc in range(16):
                                        i = e.matmul(p.t[:, 0:cw], lhsT=w.t[:, gv, kc, :], rhs=xs.t[:, kc, c0:c0 + cw], start=(kc == 0), stop=(kc == 15))
                                    return i
                                S.op("pe", [w, xs], [p], f)
                                if (ci + gv) % 2 == 0:
                                    S.op("act", [p], [u], lambda e: e.copy(out=u.t[:, gv, c0:c0 + cw], in_=p.t[:, 0:cw]))
                                else:
                                    S.op("dve", [p], [u], lambda e: e.tensor_copy(out=u.t[:, gv, c0:c0 + cw], in_=p.t[:, 0:cw]))
                        cv = c_p.next()
                        for gv in range(2):
                            ch = gv * NFC + c
                            eng = "dve"
                            S.op(eng, [u, fcw], [cv], lambda e: e.tensor_scalar(out=cv.t[:, gv, 0:L], in0=u.t[:, gv, 0:L], scalar1=fcw.t[:, ch, 0:1], scalar2=None, op0=ALU.mult))
                            for j in (1, 2):
                                S.op(eng, [u, fcw, cv], [cv], lambda e: e.scalar_tensor_tensor(out=cv.t[:, gv, 0:L], in0=u.t[:, gv, j:j + L], scalar=fcw.t[:, ch, j:j + 1],
                                                                                                 in1=cv.t[:, gv, 0:L], op0=ALU.mult, op1=ALU.add))
                        S.op("act", [cv], [cv], lambda e: e.activation(out=cv.t[:, 0, 0:L], in_=cv.t[:, 0, 0:L], func=AF.Silu))
                        hh = h_p.next()
                        S.op("dve", [cv], [hh], lambda e: e.tensor_tensor(out=hh.t[:, 0:L], in0=cv.t[:, 0, 0:L], in1=cv.t[:, 1, 0:L], op=ALU.mult))
                        S.dma("pool", hT[c * 128:(c + 1) * 128, t0:t0 + L], hh.t[:, 0:L], reads=[hh])
            S.barrier()
            if upto == 10:
                raise _Stop()
            S.mute = start > 11

            with ExitStack() as es:
                lnp = S.sb(es, [128, 2, D], F32, "lnp2")
                S.dma("sp", lnp.t[:], ln_d[:, 2:4, :], writes=[lnp])
                wd = S.sb(es, [128, NFC, 1024], BF16, "wd")
                TG = 256 if T % 256 == 0 else 128
                h_p = S.pool(es, 2, [128, NFC, TG], BF16, "dh")
                yp_p = S.pool(es, 2, [128, 1024], F32, "dyp")
                x_p = S.pool(es, 2, [128, D], F32, "dx1")
                r_p = S.pool(es, 1, [128, D], F32, "dr")
                pools = (S.pool(es, 2, [128, 4, 6], F32, "dst"), S.pool(es, 2, [128, 4], F32, "dmv"), S.pool(es, 2, [128, D], F32, "dout"))
                pq = Ring([S.ps(es, [128, 512], F32, "dps") for _ in range(6)])
                for half in range(2):
                    for c in range(0, NFC, 8):
                        ce = min(NFC, c + 8)
                        S.dma("sp", wd.t[:, c:ce, :], wd_b[c * 128:ce * 128, half * 1024:(half + 1) * 1024].rearrange("(c p) n -> p c n", p=128), writes=[wd])
                    for tg in range(0, T, TG):
                        ht = h_p.next()
                        S.dma("sp", ht.t[:], hT[:, tg:tg + TG].rearrange("(c p) t -> p c t", p=128), writes=[ht])
                        for jj in range(TG // 128):
                            tt = tg + jj * 128
                            ps2 = [pq.next(), pq.next()]
                            for nb in range(2):
                                p = ps2[nb]

                                def f(e, p=p, nb=nb):
                                    for kc in range(NFC):
                                        i = e.matmul(p.t[:], lhsT=ht.t[:, kc, jj * 128:(jj + 1) * 128], rhs=wd.t[:, kc, nb * 512:(nb + 1) * 512], start=(kc == 0), stop=(kc == NFC - 1))
                                    return i
                                S.op("pe", [ht, wd], [p], f)
                            if half == 0:
                                yp = yp_p.next()
                                S.op("act", [ps2[0]], [yp], lambda e: e.copy(out=yp.t[:, 0:512], in_=ps2[0].t[:]))
                                S.op("dve", [ps2[1]], [yp], lambda e: e.tensor_copy(out=yp.t[:, 512:1024], in_=ps2[1].t[:]))
                                S.dma("pool", ypart[tt:tt + 128, :], yp.t[:], reads=[yp])
                            else:
                                yp, xt, r = yp_p.next(), x_p.next(), r_p.next()
                                S.dma("sp", yp.t[:], ypart[tt:tt + 128, :], writes=[yp])
                                S.dma("sp", xt.t[:], x1[tt:tt + 128, :], writes=[xt])
                                S.op("dve", [xt, yp], [r], lambda e: e.scalar_tensor_tensor(out=r.t[:, 0:1024], in0=xt.t[:, 0:1024], scalar=ALPHA, in1=yp.t[:], op0=ALU.mult, op1=ALU.add))
                                for nb in range(2):
                                    S.op("dve", [ps2[nb], xt], [r], lambda e: e.scalar_tensor_tensor(
                                        out=r.t[:, 1024 + nb * 512:1024 + (nb + 1) * 512], in0=xt.t[:, 1024 + nb * 512:1024 + (nb + 1) * 512], scalar=ALPHA,
                                        in1=ps2[nb].t[:], op0=ALU.mult, op1=ALU.add))
                                o = layer_norm(r, lnp, 2, pools)
                                S.dma("pool", y_d[tt:tt + 128, :], o.t[:], reads=[o])
                    S.barrier()
            S.barrier()
            if upto == 11:
                raise _Stop()
            S.mute = start > 12

        except _Stop:
            pass
        S.mute = False
        S.barrier()

    return nc


def _t5_bucket(rel):
    nb = 16
    max_exact = 8
    ret = np.where(rel > 0, nb, 0)
    n = np.abs(rel)
    nf = np.maximum(n, 1).astype(np.float32)
    large = max_exact + (np.log(nf / np.float32(max_exact)) / np.float32(math.log(128 / max_exact)) * np.float32(nb - max_exact)).astype(np.int32)
    large = np.minimum(large, nb - 1)
    return ret + np.where(n < max_exact, n, large)


def _consts():
    p = np.arange(128)[:, None]
    f = np.arange(128)[None, :]
    cm = np.zeros((128, 7, 128), np.float32)
    cm[:, 0] = (p == f)
    cm[:, 1] = 1.0
    cm[:, 2] = (p >= f)
    cm[:, 3] = (p <= f)
    cm[:, 4] = (p > f)
    cm[:, 5] = (p < f)
    cm[:, 6] = (p + f == 127)
    rel = 639 - np.arange(1280)
    b = _t5_bucket(rel)
    oh = np.zeros((32, 1280), np.float32)
    oh[b, np.arange(1280)] = 1.0
    return cm, oh


def make_in_maps(inp, per_core_x, per_core_mem):
    f = lambda a: np.ascontiguousarray(a, dtype=np.float32)
    cm, oh = _consts()
    bc = lambda v, n=128: np.ascontiguousarray(np.broadcast_to(np.asarray(v, np.float32).reshape(1, -1), (n, np.asarray(v).size)))
    shared = {
        "w_in": f(inp["w_in"][0]), "w_gate": f(inp["w_gate"][0]), "w_kv": f(inp["w_mem_kv"][0]),
        "wb0": f(inp["w_branch_gdn"][0]), "wb1": f(inp["w_branch_diff"][0]), "wb2": f(inp["w_branch_cross"][0]),
        "w_out": f(inp["w_out"][0]), "w_up": f(inp["w_up"][0]), "w_down": f(inp["w_down"][0]),
        "gconv": f(np.asarray(inp["gdn_conv"][0]).reshape(5, 24, 128).transpose(2, 1, 0)),
        "fconv": f(np.asarray(inp["ffn_conv"][0]).reshape(3, 2 * NFC, 128).transpose(2, 1, 0)),
        "bgate": f(np.asarray(inp["b_gate"][0]).reshape(48, 128).T),
        "alog": bc(np.asarray(inp["gdn_a_log"][0]).reshape(-1)),
        "dtb": bc(np.asarray(inp["gdn_dt_bias"][0]).reshape(-1)),
        "gnw": bc(inp["gdn_norm_w"][0]),
        "dlam": f(np.broadcast_to(np.asarray(inp["diff_lambda"][0], np.float32)[None], (128, 4, 128))),
        "dnw": bc(inp["diff_norm_w"][0]),
        "lnp": f(np.broadcast_to(np.stack([np.asarray(inp[k][0], np.float32) for k in ("ln1_g", "ln1_b", "ln2_g", "ln2_b")])[None], (128, 4, D))),
        "relb": f(inp["rel_bias"]),
        "cmask": cm, "oh": oh,
    }
    maps = []
    for xc, mc in zip(per_core_x, per_core_mem):
        d = dict(shared)
        d["x"] = f(xc)
        d["mem"] = f(mc)
        maps.append(d)
    return maps


_NC_CACHE = {}


def kernel(**inp):
    xp = np.asarray(inp["x_prompt"], np.float32)
    xs = np.asarray(inp["x_sample"], np.float32)
    mp = np.asarray(inp["mem_prompt"], np.float32)
    ms = np.asarray(inp["mem_sample"], np.float32)
    n = 8
    seqs = (2048, 2048, 4096)
    px, pm = [], []
    for c in range(n):
        px.append(np.concatenate([xp[2 * c], xp[2 * c + 1], xs[c]], axis=0))
        pm.append(np.concatenate([mp[2 * c], mp[2 * c + 1], ms[c]], axis=0))
    maps = make_in_maps(inp, px, pm)
    if seqs not in _NC_CACHE:
        _NC_CACHE[seqs] = build_program(list(seqs))
    nc = _NC_CACHE[seqs]
    res = run_bass_kernel_spmd(nc, maps, core_ids=list(range(n)))
    yp = np.empty_like(xp)
    ys = np.empty_like(xs)
    for c in range(n):
        y = res.results[c]["y"]
        yp[2 * c] = y[0:2048]
        yp[2 * c + 1] = y[2048:4096]
        ys[c] = y[4096:8192]
    return (yp, ys)
```

```python
import math
from contextlib import ExitStack
import numpy as np
import concourse.bass as bass
import concourse.mybir as mybir
from concourse.bass_utils import run_bass_kernel_spmd

F32 = mybir.dt.float32
BF16 = mybir.dt.bfloat16
AF = mybir.ActivationFunctionType
ALU = mybir.AluOpType
AX = mybir.AxisListType

D = 2048
GW = 1024
NH = 8
IN_COLS = 8224
DFF = 5504
NFC = DFF // 128
MEM = 256
ALPHA = 2.0 ** 0.25
LAMBDA_INIT = 0.8 - 0.6 * math.exp(0.0)
SAME_ENGINE_SYNC = True
NDMASEM = 24


class T_:
    __slots__ = ("t", "w", "r")

    def __init__(self, t):
        self.t = t
        self.w = {}
        self.r = {}


class Sy:
    def __init__(self, nc, es):
        self.nc = nc
        self.es = es
        self.eng = {"pe": nc.tensor, "act": nc.scalar, "dve": nc.vector, "pool": nc.gpsimd, "sp": nc.sync}
        self.sem = {}
        self.cnt = {}
        self.known = {k: {} for k in self.eng}
        self.semobj = {}
        for k in self.eng:
            s = es.enter_context(nc.semaphore("s_" + k))
            self.sem[k] = s
            self.cnt[k] = 0
            self.semobj[id(s)] = s
        self.dsem = []
        self.dval = []
        for i in range(NDMASEM):
            s = es.enter_context(nc.semaphore("d%d" % i))
            self.dsem.append(s)
            self.dval.append(0)
            self.semobj[id(s)] = s
        self.di = 0
        self.nwait = 0
        self.uid = 0
        self.mute = False

    def sb(self, es, shape, dt, name=None):
        self.uid += 1
        return T_(es.enter_context(self.nc.sbuf_tensor("%s_%d" % (name or "t", self.uid), list(shape), dt)))

    def ps(self, es, shape, dt, name=None):
        self.uid += 1
        return T_(es.enter_context(self.nc.psum_tensor("%s_%d" % (name or "p", self.uid), list(shape), dt)))

    def pool(self, es, n, shape, dt, name=None):
        return Ring([self.sb(es, shape, dt, name) for _ in range(n)])

    def _wait(self, e, waits):
        E = self.eng[e]
        kn = self.known[e]
        for sid, v in waits.items():
            if kn.get(sid, 0) < v:
                E.wait_ge(self.semobj[sid], v)
                kn[sid] = v
                self.nwait += 1

    def _collect(self, e, reads, writes):
        waits = {}
        own = id(self.sem[e])

        def add(d):
            for sid, v in d.items():
                if sid == own and (e == "pe" or not SAME_ENGINE_SYNC):
                    continue
                if waits.get(sid, 0) < v:
                    waits[sid] = v

        for t in reads:
            add(t.w)
        for t in writes:
            add(t.w)
            add(t.r)
        return waits

    def _commit(self, reads, writes, sid, v):
        for t in reads:
            if t.r.get(sid, 0) < v:
                t.r[sid] = v
        for t in writes:
            t.w = {sid: v}
            t.r = {}

    def op(self, e, reads, writes, fn):
        if self.mute:
            return
        self._wait(e, self._collect(e, reads, writes))
        inst = fn(self.eng[e])
        self.cnt[e] += 1
        inst.then_inc(self.sem[e], 1)
        self._commit(reads, writes, id(self.sem[e]), self.cnt[e])

    def dma(self, q, out_ap, in_ap, reads=(), writes=(), **kw):
        if self.mute:
            return
        i = self.di
        self.di = (self.di + 1) % NDMASEM
        s = self.dsem[i]
        waits = self._collect(q, reads, writes)
        if self.dval[i] > 0:
            waits[id(s)] = max(waits.get(id(s), 0), self.dval[i])
        self._wait(q, waits)
        self.dval[i] += 16
        self.eng[q].dma_start(out=out_ap, in_=in_ap, **kw).then_inc(s, 16)
        self._commit(reads, writes, id(s), self.dval[i])

    def barrier(self):
        if self.mute:
            return
        allw = {}
        for k in self.eng:
            if self.cnt[k] > 0:
                allw[id(self.sem[k])] = self.cnt[k]
        for i in range(NDMASEM):
            if self.dval[i] > 0:
                allw[id(self.dsem[i])] = self.dval[i]
        for e in self.eng:
            w = dict(allw)
            w.pop(id(self.sem[e]), None)
            self._wait(e, w)
        for e in ("act", "dve", "pool"):
            if self.cnt[e] > 0:
                self._wait(e, {id(self.sem[e]): self.cnt[e]})


class Ring:
    def __init__(self, tiles):
        self.tiles = tiles
        self.i = 0

    def next(self):
        t = self.tiles[self.i]
        self.i = (self.i + 1) % len(self.tiles)
        return t


def act_copy2(e, dst, src):
    e.copy(out=dst.t[:, 0:4, :], in_=src.t[:, 0:4, :])
    return e.copy(out=dst.t[:, 4:8, :], in_=src.t[:, 4:8, :])


def bc_last(ap, n):
    sh = list(ap.shape)
    return ap.unsqueeze(len(sh)).to_broadcast(sh + [n])


def bc_mid(ap, n):
    sh = list(ap.shape)
    return ap.unsqueeze(1).to_broadcast([sh[0], n] + sh[1:])


class _Stop(Exception):
    pass


def build_program(seqs, dbg=False, upto=99, start=0, ext_in=(), bisect=0):
    T = sum(seqs)
    NS = len(seqs)
    soff = [sum(seqs[:i]) for i in range(NS)]
    segs = []
    for s, L in enumerate(seqs):
        n = max(1, L // 2048)
        sl = L // n
        for i in range(n):
            segs.append((soff[s] + i * sl, sl, i > 0, i < n - 1))

    nc = bass.Bass("TRN2", target_bir_lowering=False)
    kin = "ExternalInput"
    kint = "ExternalOutput" if dbg else "Internal"

    def din(name, shape, dt=F32):
        return nc.dram_tensor(name, list(shape), dt, kind=kin).ap()

    def dsc(name, shape, dt):
        if name in ext_in:
            return nc.dram_tensor(name, list(shape), dt, kind="ExternalInput").ap()
        return nc.dram_tensor(name, list(shape), dt, kind=("Internal" if name.startswith("w") else kint)).ap()

    x_d = din("x", [T, D])
    mem_d = din("mem", [NS * MEM, D])
    w_in_d = din("w_in", [D, IN_COLS])
    w_gate_d = din("w_gate", [D, 3 * D])
    w_kv_d = din("w_kv", [D, 2 * GW])
    wb_d = [din("wb%d" % i, [GW, D]) for i in range(3)]
    w_out_d = din("w_out", [D, D])
    w_up_d = din("w_up", [D, 2 * DFF])
    w_down_d = din("w_down", [DFF, D])
    gconv_d = din("gconv", [128, 24, 5])
    fconv_d = din("fconv", [128, 2 * NFC, 3])
    bgate_d = din("bgate", [128, 48])
    alog_d = din("alog", [128, 16])
    dtb_d = din("dtb", [128, 16])
    gnw_d = din("gnw", [128, 128])
    dlam_d = din("dlam", [128, 4, 128])
    dnw_d = din("dnw", [128, 256])
    ln_d = din("lnp", [128, 4, D])
    relb_d = din("relb", [32, 4])
    cm_d = din("cmask", [128, 7, 128])
    oh_d = din("oh", [32, 1280])

    y_d = nc.dram_tensor("y", [T, D], F32, kind="ExternalOutput").ap()

    wi_b = dsc("wi_b", [D, IN_COLS], BF16)
    wg_b = dsc("wg_b", [D, 3 * D], BF16)
    wkv_b = dsc("wkv_b", [D, 2 * GW], BF16)
    wb_b = [dsc("wb_b%d" % i, [GW, D], BF16) for i in range(3)]
    wo_b = dsc("wo_b", [D, D], BF16)
    wu_b = dsc("wu_b", [D, 2 * DFF], BF16)
    wd_b = dsc("wd_b", [DFF, D], BF16)
    xT = dsc("xT", [D, T], BF16)
    memT = dsc("memT", [D, NS * MEM], BF16)
    gqkvT = dsc("gqkvT", [3 * GW, T], F32)
    dqkT = dsc("dqkT", [2 * GW, T], BF16)
    cqT = dsc("cqT", [GW, T], BF16)
    gatesT = dsc("gatesT", [3 * D, T], BF16)
    gz = dsc("gz", [T, GW], BF16)
    gab = dsc("gab", [T, 32], F32)
    dv = dsc("dv", [T, GW], BF16)
    kmT = dsc("kmT", [GW, NS * MEM], BF16)
    vm = dsc("vm", [NS * MEM, GW], BF16)
    qTn = dsc("qTn", [GW, T], BF16)
    kTn = dsc("kTn", [GW, T], BF16)
    ktok = dsc("ktok", [T, GW], BF16)
    vtok = dsc("vtok", [T, GW], BF16)
    odir = dsc("odir", [2, T, GW], F32)
    ybT = [dsc("ybT%d" % i, [GW, T], BF16) for i in range(3)]
    mT = dsc("mT", [D, T], BF16)
    x1 = dsc("x1", [T, D], F32)
    x1T = dsc("x1T", [D, T], BF16)
    hT = dsc("hT", [DFF, T], BF16)
    ypart = dsc("ypart", [T, GW], F32)
    tvec = dsc("tvec", [4, 1280], BF16)

    with ExitStack() as es0:
        S = Sy(nc, es0)
        try:
            cm = S.sb(es0, [128, 7, 128], F32, "cm")
            S.dma("sp", cm.t[:], cm_d[:, :, :], writes=[cm])
            ident = cm.t[:, 0, :]
            identb = S.sb(es0, [128, 128], BF16, "identb")
            onesb = S.sb(es0, [128, 128], BF16, "onesb")
            antib = S.sb(es0, [128, 128], BF16, "antib")
            S.op("dve", [cm], [identb], lambda e: e.tensor_copy(out=identb.t[:], in_=cm.t[:, 0, :]))
            S.op("dve", [cm], [onesb], lambda e: e.tensor_copy(out=onesb.t[:], in_=cm.t[:, 1, :]))
            S.op("dve", [cm], [antib], lambda e: e.tensor_copy(out=antib.t[:], in_=cm.t[:, 6, :]))

            S.mute = start > 0
            with ExitStack() as es:
                CB = 2048
                fin = S.pool(es, 3, [128, CB], F32, "wc_in")
                fout = S.pool(es, 3, [128, CB], BF16, "wc_out")
                k = 0
                for src, dst in ([(w_in_d, wi_b), (w_gate_d, wg_b), (w_kv_d, wkv_b)] +
                                 [(wb_d[i], wb_b[i]) for i in range(3)] +
                                 [(w_out_d, wo_b), (w_up_d, wu_b), (w_down_d, wd_b)]):
                    R, C = src.shape
                    for r0 in range(0, R, 128):
                        for c0 in range(0, C, CB):
                            cw = min(CB, C - c0)
                            a = fin.next()
                            b = fout.next()
                            S.dma("sp", a.t[:, 0:cw], src[r0:r0 + 128, c0:c0 + cw], writes=[a])
                            eng = ("dve", "act", "pool")[k % 3]
                            k += 1
                            if eng == "act":
                                S.op(eng, [a], [b], lambda e: e.copy(out=b.t[:, 0:cw], in_=a.t[:, 0:cw]))
                            else:
                                S.op(eng, [a], [b], lambda e: e.tensor_copy(out=b.t[:, 0:cw], in_=a.t[:, 0:cw]))
                            S.dma("pool", dst[r0:r0 + 128, c0:c0 + cw], b.t[:, 0:cw], reads=[b])
            S.barrier()
            if upto == 0:
                raise _Stop()
            S.mute = start > 1

            def xpose_phase(src, dst, ntok, sup):
                with ExitStack() as es:
                    xin = S.pool(es, 2, [128, sup // 128, D], F32, "xin")
                    xo = S.pool(es, 2, [128, 16, sup], BF16, "xo")
                    pp = Ring([S.ps(es, [128, 512], F32, "xp") for _ in range(4)])
                    ev = 0
                    for t0 in range(0, ntok, sup):
                        a = xin.next()
                        o = xo.next()
                        S.dma("sp", a.t[:], src[t0:t0 + sup, :].rearrange("(j p) d -> p j d", p=128), writes=[a])
                        for kc in range(16):
                            for j0 in range(0, sup // 128, 4):
                                nj = min(4, sup // 128 - j0)
                                p = pp.next()

                                def f(e, p=p, j0=j0, nj=nj, kc=kc):
                                    for j in range(nj):
                                        i = e.transpose(p.t[:, j * 128:(j + 1) * 128],
                                                        a.t[:, j0 + j, kc * 128:(kc + 1) * 128], ident)
                                    return i
                                S.op("pe", [a, cm], [p], f)
                                eng = ("dve", "act")[ev % 2]
                                ev += 1
                                if eng == "act":
                                    S.op(eng, [p], [o], lambda e: e.copy(out=o.t[:, kc, j0 * 128:(j0 + nj) * 128],
                                                                        in_=p.t[:, 0:nj * 128]))
                                else:
                                    S.op(eng, [p], [o], lambda e: e.tensor_copy(out=o.t[:, kc, j0 * 128:(j0 + nj) * 128],
                                                                               in_=p.t[:, 0:nj * 128]))
                        S.dma("pool", dst[:, t0:t0 + sup].rearrange("(c p) t -> p c t", p=128), o.t[:], reads=[o])

            xpose_phase(x_d, xT, T, 512 if T % 512 == 0 else 128)
            S.barrier2 = None
            for _e in (1,):
                S.barrier()
            xpose_phase(mem_d, memT, NS * MEM, MEM)
            S.barrier()
            if upto == 1:
                raise _Stop()
            S.mute = start > 2

            with ExitStack() as es:
                LM = max(s[1] for s in segs)
                xs_pool = S.pool(es, 1, [128, 16, LM], BF16, "xseg")
                wc_pool = S.pool(es, 2, [128, 16, 512], BF16, "wcb")
                stf = S.pool(es, 2, [128, LM], F32, "stf")
                stb = S.pool(es, 2, [128, LM], BF16, "stb")
                stt = S.pool(es, 2, [128, 4, 512], BF16, "stt")
                stg = S.pool(es, 2, [128, 32], F32, "stg")
                pp = Ring([S.ps(es, [128, 512], F32, "pj") for _ in range(6)])
                bg = S.sb(es, [128, 48], F32, "bg")
                S.dma("sp", bg.t[:], bgate_d[:, :], writes=[bg])
                evc = [0]

                def load_w(wsrc, c0, cw):
                    w = wc_pool.next()
                    S.dma("sp", w.t[:, :, 0:cw], wsrc[:, c0:c0 + cw].rearrange("(c p) n -> p c n", p=128), writes=[w])
                    return w

                def fm_block(xs, t0, L, wsrc, c0, ncols, dst, r0, dt, sig_chunk0=None):
                    for cb in range(0, ncols, 512):
                        cw = min(512, ncols - cb)
                        w = load_w(wsrc, c0 + cb, cw)
                        for fc in range(cw // 128):
                            st = (stf if dt == F32 else stb).next()
                            for tt in range(0, L, 512):
                                tw = min(512, L - tt)
                                p = pp.next()

                                def f(e, p=p, tt=tt, tw=tw, fc=fc):
                                    for kc in range(16):
                                        i = e.matmul(p.t[:, 0:tw], lhsT=w.t[:, kc, fc * 128:(fc + 1) * 128],
                                                     rhs=xs.t[:, kc, tt:tt + tw], start=(kc == 0), stop=(kc == 15))
                                    return i
                                S.op("pe", [w, xs], [p], f)
                                if sig_chunk0 is not None:
                                    ch = sig_chunk0 + (cb // 128) + fc
                                    S.op("act", [p, bg], [st], lambda e: e.activation(
                                        out=st.t[:, tt:tt + tw], in_=p.t[:, 0:tw], func=AF.Sigmoid,
                                        bias=bg.t[:, ch:ch + 1], scale=1.0))
                                else:
                                    evc[0] += 1
                                    if evc[0] % 2:
                                        S.op("dve", [p], [st], lambda e: e.tensor_copy(out=st.t[:, tt:tt + tw], in_=p.t[:, 0:tw]))
                                    else:
                                        S.op("act", [p], [st], lambda e: e.copy(out=st.t[:, tt:tt + tw], in_=p.t[:, 0:tw]))
                            rr = r0 + cb + fc * 128
                            S.dma("pool", dst[rr:rr + 128, t0:t0 + L], st.t[:, 0:L], reads=[st])

                def tm_block(xs, t0, L, wsrc, c0, ncols, dst, dc0, silu):
                    for cb in range(0, ncols, 512):
                        cw = min(512, ncols - cb)
                        w = load_w(wsrc, c0 + cb, cw)
                        for tg in range(0, L, 512):
                            ng = min(4, (L - tg) // 128)
                            st = stt.next()
                            for j in range(ng):
                                tt = tg + j * 128
                                p = pp.next()

                                def f(e, p=p, tt=tt):
                                    for kc in range(16):
                                        i = e.matmul(p.t[:, 0:cw], lhsT=xs.t[:, kc, tt:tt + 128],
                                                     rhs=w.t[:, kc, 0:cw], start=(kc == 0), stop=(kc == 15))
                                    return i
                                S.op("pe", [w, xs], [p], f)
                                if silu:
                                    S.op("act", [p], [st], lambda e: e.activation(out=st.t[:, j, 0:cw], in_=p.t[:, 0:cw], func=AF.Silu))
                                else:
                                    S.op("dve", [p], [st], lambda e: e.tensor_copy(out=st.t[:, j, 0:cw], in_=p.t[:, 0:cw]))
                            S.dma("pool", dst[t0 + tg:t0 + tg + ng * 128, dc0 + cb:dc0 + cb + cw].rearrange("(j p) n -> p j n", p=128),
                                  st.t[:, 0:ng, 0:cw], reads=[st])

                for (t0, L, _l, _r) in segs:
                    xs = xs_pool.next()
                    S.dma("sp", xs.t[:, :, 0:L], xT[:, t0:t0 + L].rearrange("(c p) t -> p c t", p=128), writes=[xs])
                    fm_block(xs, t0, L, wi_b, 0, 3072, gqkvT, 0, F32)
                    fm_block(xs, t0, L, wi_b, 4128, 2048, dqkT, 0, BF16)
                    fm_block(xs, t0, L, wi_b, 7200, 1024, cqT, 0, BF16)
                    fm_block(xs, t0, L, wg_b, 0, 3 * D, gatesT, 0, BF16, sig_chunk0=0)
                    tm_block(xs, t0, L, wi_b, 3072, 1024, gz, 0, True)
                    tm_block(xs, t0, L, wi_b, 6176, 1024, dv, 0, False)
                    w = wc_pool.next()
                    S.dma("sp", w.t[:, :, 0:32], wi_b[:, 4096:4128].rearrange("(c p) n -> p c n", p=128), writes=[w])
                    for tt in range(0, L, 128):
                        p = pp.next()
                        sg = stg.next()

                        def f(e, p=p, tt=tt):
                            for kc in range(16):
                                i = e.matmul(p.t[:, 0:32], lhsT=xs.t[:, kc, tt:tt + 128], rhs=w.t[:, kc, 0:32],
                                             start=(kc == 0), stop=(kc == 15))
                            return i
                        S.op("pe", [w, xs], [p], f)
                        S.op("dve", [p], [sg], lambda e: e.tensor_copy(out=sg.t[:], in_=p.t[:, 0:32]))
                        S.dma("pool", gab[t0 + tt:t0 + tt + 128, :], sg.t[:], reads=[sg])
                xs = xs_pool.next()
                NM = NS * MEM
                S.dma("sp", xs.t[:, :, 0:NM], memT[:, :].rearrange("(c p) t -> p c t", p=128), writes=[xs])
                fm_block(xs, 0, NM, wkv_b, 0, 1024, kmT, 0, BF16)
                tm_block(xs, 0, NM, wkv_b, 1024, 1024, vm, 0, False)
            S.barrier()
            if upto == 2:
                raise _Stop()
            S.mute = start > 3

            with ExitStack() as es:
                SM = max(seqs)
                gcw = S.sb(es, [128, 24, 5], F32, "gcw")
                S.dma("sp", gcw.t[:], gconv_d[:, :, :], writes=[gcw])
                xin = S.pool(es, 2, [128, SM + 4], F32, "cxin")
                acc = S.pool(es, 2, [128, SM], F32, "cacc")
                ys = S.pool(es, 2, [128, SM], F32, "cys")
                sq = S.pool(es, 2, [128, SM], BF16, "csq")
                rn = S.pool(es, 2, [128, 512], F32, "crn")
                yn = S.pool(es, 2, [128, SM], BF16, "cyn")
                ynf = S.pool(es, 1, [128, SM], F32, "cynf")
                tst = S.pool(es, 2, [128, 4, 128], BF16, "ctst")
                pss = Ring([S.ps(es, [128, 512], F32, "cps") for _ in range(2)])
                ptr = Ring([S.ps(es, [128, 512], F32, "cpt") for _ in range(2)])
                work = [(s, c) for s in range(NS) for c in range(24)]

                def stage_a(s, c):
                    t0, L = soff[s], seqs[s]
                    a = xin.next()
                    S.op("pool", [], [a], lambda e: e.memset(a.t[:, 0:2], 0.0))
                    S.op("pool", [], [a], lambda e: e.memset(a.t[:, L + 2:L + 4], 0.0))
                    S.dma("sp", a.t[:, 2:L + 2], gqkvT[c * 128:(c + 1) * 128, t0:t0 + L], writes=[a])
                    ac = acc.next()
                    S.op("dve", [a, gcw], [ac], lambda e: e.tensor_scalar(
                        out=ac.t[:, 0:L], in0=a.t[:, 0:L], scalar1=gcw.t[:, c, 0:1], scalar2=None, op0=ALU.mult))
                    for j in range(1, 5):
                        S.op("dve", [a, gcw, ac], [ac], lambda e: e.scalar_tensor_tensor(
                            out=ac.t[:, 0:L], in0=a.t[:, j:j + L], scalar=gcw.t[:, c, j:j + 1], in1=ac.t[:, 0:L],
                            op0=ALU.mult, op1=ALU.add))
                    return ac

                def stage_b(s, c, ac):
                    t0, L = soff[s], seqs[s]
                    y = ys.next()
                    S.op("act", [ac], [y], lambda e: e.activation(out=y.t[:, 0:L], in_=ac.t[:, 0:L], func=AF.Silu))
                    if c < 16:
                        q2 = sq.next()
                        S.op("pool", [y], [q2], lambda e: e.tensor_tensor(out=q2.t[:, 0:L], in0=y.t[:, 0:L], in1=y.t[:, 0:L], op=ALU.mult))
                        o = yn.next()
                        for tt in range(0, L, 512):
                            tw = min(512, L - tt)
                            p = pss.next()
                            S.op("pe", [q2, onesb], [p], lambda e: e.matmul(p.t[:, 0:tw], lhsT=onesb.t[:], rhs=q2.t[:, tt:tt + tw], start=True, stop=True))
                            r = rn.next()
                            S.op("act", [p], [r], lambda e: e.activation(out=r.t[:, 0:tw], in_=p.t[:, 0:tw], func=AF.Sqrt, bias=1e-6, scale=1.0))
                            S.op("dve", [r], [r], lambda e: e.reciprocal(out=r.t[:, 0:tw], in_=r.t[:, 0:tw]))
                            sc = (128.0 ** -0.5) if c < 8 else 1.0
                            S.op("dve", [r, y], [o], lambda e: e.scalar_tensor_tensor(
                                out=o.t[:, tt:tt + tw], in0=y.t[:, tt:tt + tw], scalar=sc, in1=r.t[:, 0:tw], op0=ALU.mult, op1=ALU.mult))
                        dstT = qTn if c < 8 else kTn
                        h = c % 8
                        S.dma("pool", dstT[h * 128:(h + 1) * 128, t0:t0 + L], o.t[:, 0:L], reads=[o])
                    if c >= 8:
                        h = c % 8
                        if c < 16:
                            yf = ynf.next()
                            S.op("pool", [o], [yf], lambda e: e.tensor_copy(out=yf.t[:, 0:L], in_=o.t[:, 0:L]))
                        else:
                            yf = y
                        dst = ktok if c < 16 else vtok
                        for tg in range(0, L, 512):
                            ng = min(4, (L - tg) // 128)
                            p = ptr.next()

                            def f(e, p=p, tg=tg, ng=ng, yf=yf):
                                for j in range(ng):
                                    i = e.transpose(p.t[:, j * 128:(j + 1) * 128], yf.t[:, tg + j * 128:tg + (j + 1) * 128], ident)
                                return i
                            S.op("pe", [yf, cm], [p], f)
                            st = tst.next()
                            S.op("act", [p], [st], lambda e: e.copy(out=st.t[:, 0:ng, :], in_=p.t[:, 0:ng * 128].rearrange("p (j d) -> p j d", d=128)))
                            S.dma("pool", dst[t0 + tg:t0 + tg + ng * 128, h * 128:(h + 1) * 128].rearrange("(j p) d -> p j d", p=128),
                                  st.t[:, 0:ng, :], reads=[st])

                nxt = stage_a(*work[0])
                for i, (s, c) in enumerate(work):
                    cur = nxt
                    if i + 1 < len(work):
                        nxt = stage_a(*work[i + 1])
                    stage_b(s, c, cur)
            S.barrier()
            if upto == 3:
                raise _Stop()
            S.mute = start > 4

            with ExitStack() as es:
                alog = S.sb(es, [128, 16], F32, "alog")
                dtb = S.sb(es, [128, 16], F32, "dtb")
                S.dma("sp", alog.t[:], alog_d[:, :], writes=[alog])
                S.dma("sp", dtb.t[:], dtb_d[:, :], writes=[dtb])
                nea = S.sb(es, [128, 16], F32, "nea")
                S.op("act", [alog], [nea], lambda e: e.activation(out=nea.t[:], in_=alog.t[:], func=AF.Exp))
                S.op("dve", [nea], [nea], lambda e: e.tensor_scalar(out=nea.t[:], in0=nea.t[:], scalar1=-1.0, scalar2=None, op0=ALU.mult))
                qT_p = S.pool(es, 2, [128, 8, 128], BF16, "gqT")
                kT_p = S.pool(es, 2, [128, 8, 128], BF16, "gkT")
                kt_p = S.pool(es, 2, [128, 8, 128], BF16, "gkt")
                vt_p = S.pool(es, 2, [128, 8, 128], BF16, "gvt")
                ab_p = S.pool(es, 2, [128, 32], F32, "gab")
                sm = {n: S.sb(es, [128, 8], F32, "g" + n) for n in
                      ("g", "beta", "nbeta", "gc", "gl", "egc", "egl", "ekd", "bg", "tmp", "ngc")}
                dg = S.sb(es, [128, 8, 128], F32, "dg")
                Dm = S.sb(es, [128, 8, 128], F32, "Dm")
                DmT = S.sb(es, [128, 8, 128], F32, "DmT")
                Ei = S.sb(es, [128, 8, 128], F32, "Ei")
                Es = S.sb(es, [128, 8, 128], F32, "Es")
                EiT = S.sb(es, [128, 8, 128], F32, "EiT")
                Pm = Ring([S.sb(es, [128, 8, 128], F32, "Pm") for _ in range(2)])
                PTm = Ring([S.sb(es, [128, 8, 128], F32, "PTm") for _ in range(2)])
                TT = S.sb(es, [128, 8, 128], F32, "TT")
                QKmT = S.sb(es, [128, 8, 128], BF16, "QKmT")
                rv = S.sb(es, [128, 8, 128], F32, "rv")
                rk = S.sb(es, [128, 8, 128], F32, "rk")
                kdec = S.sb(es, [128, 8, 128], BF16, "kdec")
                wv = S.sb(es, [128, 8, 128], F32, "wv")
                wkT = S.sb(es, [128, 8, 128], BF16, "wkT")
                u_b = S.sb(es, [128, 8, 128], BF16, "u_b")
                St = S.sb(es, [128, 8, 128], F32, "St")
                Stmp = S.sb(es, [128, 8, 128], F32, "Stmp")
                S_b = S.sb(es, [128, 8, 128], BF16, "S_b")
                o1 = S.sb(es, [128, 8, 128], F32, "o1")
                o2 = S.sb(es, [128, 8, 128], F32, "o2")
                oo = S.pool(es, 2, [128, 8, 128], F32, "oo")
                PA = S.ps(es, [128, 8, 128], F32, "PA")
                PB = S.ps(es, [128, 8, 128], F32, "PB")
                PC = S.ps(es, [128, 8, 128], F32, "PC")
                PD = S.ps(es, [128, 512], F32, "PD")
                ones32 = cm.t[:, 1, :]

                def flat(t):
                    return t.t[:].rearrange("p h d -> p (h d)")

                for s in range(NS):
                    t0s, L = soff[s], seqs[s]
                    ntile = L // 128
                    for di in range(2):
                        if di == 0:
                            CS, MI_, MS_, MIT_ = cm.t[:, 3, :], cm.t[:, 2, :], cm.t[:, 4, :], cm.t[:, 3, :]
                        else:
                            CS, MI_, MS_, MIT_ = cm.t[:, 2, :], cm.t[:, 3, :], cm.t[:, 5, :], cm.t[:, 2, :]
                        S.op("pool", [], [St], lambda e: e.memset(St.t[:], 0.0))
                        S.op("pool", [], [S_b], lambda e: e.memset(S_b.t[:], 0.0))
                        order = range(ntile) if di == 0 else range(ntile - 1, -1, -1)
                        for j in order:
                            tt = t0s + j * 128
                            qT_, kT_, kt_, vt_, ab = qT_p.next(), kT_p.next(), kt_p.next(), vt_p.next(), ab_p.next()
                            S.dma("sp", qT_.t[:], qTn[:, tt:tt + 128].rearrange("(h p) t -> p h t", p=128), writes=[qT_])
                            S.dma("sp", kT_.t[:], kTn[:, tt:tt + 128].rearrange("(h p) t -> p h t", p=128), writes=[kT_])
                            S.dma("sp", kt_.t[:], ktok[tt:tt + 128, :].rearrange("p (h d) -> p h d", d=128), writes=[kt_])
                            S.dma("sp", vt_.t[:], vtok[tt:tt + 128, :].rearrange("p (h d) -> p h d", d=128), writes=[vt_])
                            S.dma("sp", ab.t[:], gab[tt:tt + 128, :], writes=[ab])
                            g, beta, nbeta, gc, gl = sm["g"], sm["beta"], sm["nbeta"], sm["gc"], sm["gl"]
                            egc, egl, ekd, bgm, tmp, ngc = sm["egc"], sm["egl"], sm["ekd"], sm["bg"], sm["tmp"], sm["ngc"]
                            a0 = di * 8
                            S.op("dve", [ab, dtb], [tmp], lambda e: e.tensor_tensor(out=tmp.t[:], in0=ab.t[:, a0:a0 + 8], in1=dtb.t[:, a0:a0 + 8], op=ALU.add))
                            S.op("act", [tmp], [tmp], lambda e: e.activation(out=tmp.t[:], in_=tmp.t[:], func=AF.Exp))
                            S.op("act", [tmp], [tmp], lambda e: e.activation(out=tmp.t[:], in_=tmp.t[:], func=AF.Ln, bias=1.0, scale=1.0))
                            S.op("dve", [tmp, nea], [g], lambda e: e.tensor_tensor(out=g.t[:], in0=tmp.t[:], in1=nea.t[:, a0:a0 + 8], op=ALU.mult))
                            S.op("act", [ab], [beta], lambda e: e.activation(out=beta.t[:], in_=ab.t[:, 16 + a0:24 + a0], func=AF.Sigmoid))
                            S.op("dve", [beta], [nbeta], lambda e: e.tensor_scalar(out=nbeta.t[:], in0=beta.t[:], scalar1=-1.0, scalar2=None, op0=ALU.mult))

                            def f(e):
                                e.matmul(PD.t[:, 0:8], lhsT=CS, rhs=g.t[:], start=True, stop=True)
                                return e.matmul(PD.t[:, 8:16], lhsT=ones32, rhs=g.t[:], start=True, stop=True)
                            S.op("pe", [g, cm], [PD], f)
                            S.op("dve", [PD], [gc], lambda e: e.tensor_copy(out=gc.t[:], in_=PD.t[:, 0:8]))
                            S.op("dve", [PD], [gl], lambda e: e.tensor_copy(out=gl.t[:], in_=PD.t[:, 8:16]))
                            S.op("dve", [gc], [ngc], lambda e: e.tensor_scalar(out=ngc.t[:], in0=gc.t[:], scalar1=-1.0, scalar2=None, op0=ALU.mult))
                            S.op("act", [gc], [egc], lambda e: e.activation(out=egc.t[:], in_=gc.t[:], func=AF.Exp))
                            S.op("act", [gl], [egl], lambda e: e.activation(out=egl.t[:], in_=gl.t[:], func=AF.Exp))
                            S.op("dve", [gl, gc], [ekd], lambda e: e.tensor_tensor(out=ekd.t[:], in0=gl.t[:], in1=gc.t[:], op=ALU.subtract))
                            S.op("act", [ekd], [ekd], lambda e: e.activation(out=ekd.t[:], in_=ekd.t[:], func=AF.Exp))
                            S.op("dve", [beta, egc], [bgm], lambda e: e.tensor_tensor(out=bgm.t[:], in0=beta.t[:], in1=egc.t[:], op=ALU.mult))
                            if bisect == 1:
                                S.mute = True
                            S.op("dve", [ngc, cm], [dg], lambda e: e.tensor_tensor(
                                out=dg.t[:], in0=bc_mid(cm.t[:, 0, :], 8), in1=bc_last(ngc.t[:], 128), op=ALU.mult))

                            def f(e):
                                for hh in range(0, 8, 4):
                                    i = e.matmul(PA.t[:, hh:hh + 4, :], lhsT=ones32, rhs=dg.t[:, hh:hh + 4, :], start=True, stop=True)
                                return i
                            S.op("pe", [dg, cm], [PA], f)
                            S.op("dve", [PA, gc], [Dm], lambda e: e.tensor_tensor(out=Dm.t[:], in0=PA.t[:], in1=bc_last(gc.t[:], 128), op=ALU.add))
                            S.op("pool", [Dm], [DmT], lambda e: e.tensor_scalar(out=DmT.t[:], in0=Dm.t[:], scalar1=0.0, scalar2=None, op0=ALU.max))
                            S.op("dve", [Dm], [Dm], lambda e: e.tensor_scalar(out=Dm.t[:], in0=Dm.t[:], scalar1=0.0, scalar2=None, op0=ALU.min))
                            S.op("act", [Dm], [Ei], lambda e: e.activation(out=Ei.t[:], in_=Dm.t[:], func=AF.Exp))
                            S.op("act", [DmT], [EiT], lambda e: e.activation(out=EiT.t[:], in_=DmT.t[:], func=AF.Exp, scale=-1.0))
                            S.op("dve", [Ei, cm], [Es], lambda e: e.tensor_tensor(out=Es.t[:], in0=Ei.t[:], in1=bc_mid(MS_, 8), op=ALU.mult))
                            S.op("dve", [EiT, cm], [EiT], lambda e: e.tensor_tensor(out=EiT.t[:], in0=EiT.t[:], in1=bc_mid(MIT_, 8), op=ALU.mult))
                            if bisect == 2:
                                S.mute = True

                            def f(e):
                                for h in range(8):
                                    i = e.matmul(PB.t[:, h, :], lhsT=kT_.t[:, h, :], rhs=kT_.t[:, h, :], start=True, stop=True)
                                return i
                            S.op("pe", [kT_], [PB], f)

                            def f(e):
                                for h in range(8):
                                    i = e.matmul(PC.t[:, h, :], lhsT=kT_.t[:, h, :], rhs=qT_.t[:, h, :], start=True, stop=True)
                                return i
                            S.op("pe", [kT_, qT_], [PC], f)
                            if bisect == 31:
                                S.mute = True
                            P0 = Pm.next()
                            PT0 = PTm.next()
                            S.op("dve", [PB, Es], [P0], lambda e: e.tensor_tensor(out=P0.t[:], in0=PB.t[:], in1=Es.t[:], op=ALU.mult))
                            S.op("dve", [P0, nbeta], [P0], lambda e: e.tensor_tensor(out=P0.t[:], in0=P0.t[:], in1=bc_last(nbeta.t[:], 128), op=ALU.mult))
                            S.op("dve", [PC, EiT], [QKmT], lambda e: e.tensor_tensor(out=QKmT.t[:], in0=PC.t[:], in1=EiT.t[:], op=ALU.mult))
                            if bisect == 32:
                                S.mute = True

                            def f(e):
                                for h in range(8):
                                    i = e.matmul(PA.t[:, h, :], lhsT=P0.t[:, h, :], rhs=ident, start=True, stop=True)
                                return i
                            S.op("pe", [P0, cm], [PA], f)
                            if bisect == 33:
                                S.mute = True
                            S.op("act", [PA], [PT0], lambda e: act_copy2(e, PT0, PA))
                            S.op("dve", [PT0, cm], [TT], lambda e: e.tensor_tensor(out=TT.t[:], in0=PT0.t[:], in1=bc_mid(cm.t[:, 0, :], 8), op=ALU.add))
                            if bisect == 3:
                                S.mute = True
                            Pc, PTc = P0, PT0
                            for lvl in range(1, 7):
                                Pn = Pm.next()
                                PTn = PTm.next()

                                def f(e, Pc=Pc, PTc=PTc):
                                    for h in range(8):
                                        i = e.matmul(PA.t[:, h, :], lhsT=PTc.t[:, h, :], rhs=Pc.t[:, h, :], start=True, stop=True)
                                    return i
                                S.op("pe", [Pc, PTc], [PA], f)
                                S.op("act", [PA], [Pn], lambda e: act_copy2(e, Pn, PA))
                                if lvl < 6:
                                    def f(e, Pc=Pc, PTc=PTc):
                                        for h in range(8):
                                            i = e.matmul(PB.t[:, h, :], lhsT=Pc.t[:, h, :], rhs=PTc.t[:, h, :], start=True, stop=True)
                                        return i
                                    S.op("pe", [Pc, PTc], [PB], f)
                                    S.op("dve", [PB], [PTn], lambda e: e.tensor_copy(out=PTn.t[:], in_=PB.t[:]))

                                def f(e, Pn=Pn):
                                    for h in range(8):
                                        i = e.matmul(PC.t[:, h, :], lhsT=Pn.t[:, h, :], rhs=TT.t[:, h, :], start=True, stop=True)
                                    return i
                                S.op("pe", [Pn, TT], [PC], f)
                                S.op("dve", [PC, TT], [TT], lambda e: e.tensor_tensor(out=TT.t[:], in0=PC.t[:], in1=TT.t[:], op=ALU.add))
                                Pc, PTc = Pn, PTn
                            if bisect == 4:
                                S.mute = True
                            S.op("pool", [vt_, beta], [rv], lambda e: e.tensor_tensor(out=rv.t[:], in0=vt_.t[:], in1=bc_last(beta.t[:], 128), op=ALU.mult))
                            S.op("pool", [kt_, bgm], [rk], lambda e: e.tensor_tensor(out=rk.t[:], in0=kt_.t[:], in1=bc_last(bgm.t[:], 128), op=ALU.mult))
                            S.op("pool", [kt_, ekd], [kdec], lambda e: e.tensor_tensor(out=kdec.t[:], in0=kt_.t[:], in1=bc_last(ekd.t[:], 128), op=ALU.mult))

                            def f(e):
                                for h in range(8):
                                    i = e.matmul(PA.t[:, h, :], lhsT=TT.t[:, h, :], rhs=rv.t[:, h, :], start=True, stop=True)
                                return i
                            S.op("pe", [TT, rv], [PA], f)

                            def f(e):
                                for h in range(8):
                                    i = e.matmul(PB.t[:, h, :], lhsT=rk.t[:, h, :], rhs=TT.t[:, h, :], start=True, stop=True)
                                return i
                            S.op("pe", [TT, rk], [PB], f)
                            S.op("act", [PA], [wv], lambda e: act_copy2(e, wv, PA))
                            S.op("dve", [PB], [wkT], lambda e: e.tensor_copy(out=wkT.t[:], in_=PB.t[:]))
                            if bisect == 5:
                                S.mute = True

                            def f(e):
                                for h in range(8):
                                    i = e.matmul(PA.t[:, h, :], lhsT=wkT.t[:, h, :], rhs=S_b.t[:, h, :], start=True, stop=True)
                                return i
                            S.op("pe", [wkT, S_b], [PA], f)

                            def f(e):
                                for h in range(8):
                                    i = e.matmul(PB.t[:, h, :], lhsT=qT_.t[:, h, :], rhs=S_b.t[:, h, :], start=True, stop=True)
                                return i
                            S.op("pe", [qT_, S_b], [PB], f)
                            S.op("dve", [PA, wv], [u_b], lambda e: e.tensor_tensor(out=u_b.t[:], in0=wv.t[:], in1=PA.t[:], op=ALU.subtract))
                            S.op("dve", [PB, egc], [o1], lambda e: e.tensor_tensor(out=o1.t[:], in0=PB.t[:], in1=bc_last(egc.t[:], 128), op=ALU.mult))

                            def f(e):
                                for h in range(8):
                                    i = e.matmul(PC.t[:, h, :], lhsT=QKmT.t[:, h, :], rhs=u_b.t[:, h, :], start=True, stop=True)
                                return i
                            S.op("pe", [QKmT, u_b], [PC], f)

                            def f(e):
                                for h in range(8):
                                    i = e.matmul(PA.t[:, h, :], lhsT=kdec.t[:, h, :], rhs=u_b.t[:, h, :], start=True, stop=True)
                                return i
                            S.op("pe", [kdec, u_b], [PA], f)
                            S.op("act", [PC], [o2], lambda e: act_copy2(e, o2, PC))
                            ot = oo.next()
                            S.op("pool", [o1, o2], [ot], lambda e: e.tensor_tensor(out=ot.t[:], in0=o1.t[:], in1=o2.t[:], op=ALU.add))
                            S.dma("pool", odir[di, tt:tt + 128, :], flat(ot), reads=[ot])
                            S.op("pool", [St, egl], [Stmp], lambda e: e.tensor_tensor(out=Stmp.t[:], in0=St.t[:], in1=bc_last(egl.t[:], 128), op=ALU.mult))
                            S.op("dve", [PA, Stmp], [St], lambda e: e.tensor_tensor(out=St.t[:], in0=PA.t[:], in1=Stmp.t[:], op=ALU.add))
                            S.op("act", [St], [S_b], lambda e: e.copy(out=S_b.t[:], in_=St.t[:]))
            S.barrier()
            if upto == 4:
                raise _Stop()
            S.mute = start > 5

            def tm_to_fm(es_, name):
                ptr = Ring([S.ps(es_, [128, 512], F32, name + "pt") for _ in range(2)])
                return ptr

            with ExitStack() as es:
                gnw = S.sb(es, [128, 128], F32, "gnw")
                S.dma("sp", gnw.t[:], gnw_d[:, :], writes=[gnw])
                of_p = S.pool(es, 2, [128, 8, 128], F32, "of")
                ob_p = S.pool(es, 2, [128, 8, 128], F32, "ob")
                z_p = S.pool(es, 2, [128, 8, 128], BF16, "zz")
                sq_p = S.pool(es, 2, [128, 8, 128], F32, "gsq")
                ss_p = S.pool(es, 2, [128, 8], F32, "gss")
                yg_p = S.pool(es, 2, [128, 8, 128], F32, "yg")
                stg_p = S.pool(es, 2, [128, 8, 512], BF16, "ygst")
                ptr = Ring([S.ps(es, [128, 512], F32, "gpt") for _ in range(4)])
                sup = 512 if all(L % 512 == 0 for L in seqs) else 128
                for tg in range(0, T, sup):
                    stg = stg_p.next()
                    for jj in range(sup // 128):
                        tt = tg + jj * 128
                        of, ob, zt = of_p.next(), ob_p.next(), z_p.next()
                        S.dma("sp", of.t[:], odir[0, tt:tt + 128, :].rearrange("p (h d) -> p h d", d=128), writes=[of])
                        S.dma("sp", ob.t[:], odir[1, tt:tt + 128, :].rearrange("p (h d) -> p h d", d=128), writes=[ob])
                        S.dma("sp", zt.t[:], gz[tt:tt + 128, :].rearrange("p (h d) -> p h d", d=128), writes=[zt])
                        S.op("pool", [of, ob], [of], lambda e: e.tensor_tensor(out=of.t[:], in0=of.t[:], in1=ob.t[:], op=ALU.add))
                        sqt, sst, yg = sq_p.next(), ss_p.next(), yg_p.next()
                        S.op("act", [of], [sqt], lambda e: e.activation(out=sqt.t[:], in_=of.t[:], func=AF.Square))
                        S.op("dve", [sqt], [sst], lambda e: e.tensor_reduce(out=sst.t[:], in_=sqt.t[:], axis=AX.X, op=ALU.add))
                        S.op("act", [sst], [sst], lambda e: e.activation(out=sst.t[:], in_=sst.t[:], func=AF.Sqrt, bias=1e-6, scale=1.0 / 128.0))
                        S.op("dve", [sst], [sst], lambda e: e.reciprocal(out=sst.t[:], in_=sst.t[:]))
                        S.op("dve", [of, sst], [yg], lambda e: e.tensor_tensor(out=yg.t[:], in0=of.t[:], in1=bc_last(sst.t[:], 128), op=ALU.mult))
                        S.op("dve", [yg, gnw], [yg], lambda e: e.tensor_tensor(out=yg.t[:], in0=yg.t[:], in1=bc_mid(gnw.t[:], 8), op=ALU.mult))
                        S.op("dve", [yg, zt], [yg], lambda e: e.tensor_tensor(out=yg.t[:], in0=yg.t[:], in1=zt.t[:], op=ALU.mult))
                        for h0 in range(0, 8, 4):
                            p = ptr.next()

                            def f(e, p=p, h0=h0, yg=yg):
                                for h in range(4):
                                    i = e.transpose(p.t[:, h * 128:(h + 1) * 128], yg.t[:, h0 + h, :], ident)
                                return i
                            S.op("pe", [yg, cm], [p], f)
                            S.op("act", [p], [stg], lambda e: e.copy(out=stg.t[:, h0:h0 + 4, jj * 128:(jj + 1) * 128],
                                                                      in_=p.t[:].rearrange("p (h t) -> p h t", t=128)))
                    S.dma("pool", ybT[0][:, tg:tg + sup].rearrange("(h p) t -> p h t", p=128), stg.t[:, :, 0:sup], reads=[stg])
            S.barrier()
            if upto == 5:
                raise _Stop()
            S.mute = start > 6

            with ExitStack() as es:
                relb = S.sb(es, [32, 4], F32, "relb")
                oh = S.sb(es, [32, 1280], F32, "oh")
                S.dma("sp", relb.t[:], relb_d[:, :], writes=[relb])
                S.dma("sp", oh.t[:], oh_d[:, :], writes=[oh])
                tvs = S.sb(es, [4, 1280], BF16, "tvs")
                pq = Ring([S.ps(es, [128, 512], F32, "dps") for _ in range(3)])
                po = [S.ps(es, [128, 512], F32, "dpo") for _ in range(4)]
                for c0 in range(0, 1280, 512):
                    cw = min(512, 1280 - c0)
                    p = pq.next()
                    S.op("pe", [relb, oh], [p], lambda e: e.matmul(p.t[0:4, 0:cw], lhsT=relb.t[:, :], rhs=oh.t[:, c0:c0 + cw], start=True, stop=True))
                    S.op("dve", [p], [tvs], lambda e: e.tensor_scalar(out=tvs.t[:, c0:c0 + cw], in0=p.t[0:4, 0:cw], scalar1=128.0 ** 0.5, scalar2=None, op0=ALU.mult))
                S.dma("pool", tvec[:, :], tvs.t[:], reads=[tvs])
                relbb = S.sb(es, [128, 128], F32, "relbb")
                S.dma("sp", relbb.t[:], bass.AP(relb_d.tensor, 0, [[0, 128], [1, 128]]), writes=[relbb])
                dl = S.sb(es, [128, 4, 128], F32, "dl")
                S.dma("sp", dl.t[:], dlam_d[:, :, :], writes=[dl])
                dnw = S.sb(es, [128, 256], F32, "dnw")
                S.dma("sp", dnw.t[:], dnw_d[:, :], writes=[dnw])
                lt = S.sb(es, [128, 2, 128], F32, "lt")
                l2 = S.sb(es, [128, 2], F32, "l2")
                nlam = S.sb(es, [128, 1], F32, "nlam")
                S.op("dve", [dl], [lt], lambda e: e.tensor_tensor(out=lt.t[:, 0, :], in0=dl.t[:, 0, :], in1=dl.t[:, 1, :], op=ALU.mult))
                S.op("dve", [dl, lt], [lt], lambda e: e.tensor_tensor(out=lt.t[:, 1, :], in0=dl.t[:, 2, :], in1=dl.t[:, 3, :], op=ALU.mult))
                S.op("dve", [lt], [l2], lambda e: e.tensor_reduce(out=l2.t[:], in_=lt.t[:], axis=AX.X, op=ALU.add))
                S.op("act", [l2], [l2], lambda e: e.activation(out=l2.t[:], in_=l2.t[:], func=AF.Exp))
                S.op("dve", [l2], [nlam], lambda e: e.tensor_tensor(out=nlam.t[:], in0=l2.t[:, 1:2], in1=l2.t[:, 0:1], op=ALU.subtract))
                S.op("dve", [nlam], [nlam], lambda e: e.tensor_scalar(out=nlam.t[:], in0=nlam.t[:], scalar1=-LAMBDA_INIT, scalar2=None, op0=ALU.add))
                S.barrier()
                SM = max(seqs)
                Rt = [[S.sb(es, [128, 512], BF16, "Rt") for _ in range(6)] for _ in range(4)]
                for h in range(4):
                    for dlt in range(-1, 5):
                        off = 639 - dlt * 128 - 127
                        S.dma("sp", Rt[h][dlt + 1].t[:], bass.AP(tvec.tensor, h * 1280 + off, [[1, 128], [1, 512]]), writes=[Rt[h][dlt + 1]])
                kT_p = S.pool(es, 1, [128, 2, SM], BF16, "dkT")
                qT_p = S.pool(es, 1, [128, 2, SM], BF16, "dqT")
                v_p = S.pool(es, 1, [128, SM // 128, 258], BF16, "dv")
                pT_p = S.pool(es, 3, [128, 512], BF16, "dpT")
                om = S.sb(es, [128, 4, 2, 256], F32, "om")
                rs = S.pool(es, 2, [128, 1], F32, "drs")
                od = S.pool(es, 2, [128, 256], F32, "dod")
                sqd = S.pool(es, 2, [128, 256], F32, "dsq")
                ssd = S.pool(es, 2, [128, 1], F32, "dss")
                stq = S.pool(es, 2, [128, 2, 512], BF16, "dstq")
                ptr = Ring([S.ps(es, [128, 512], F32, "dpt") for _ in range(1)])
                scale = 128.0 ** -0.5
                for s in range(NS):
                    t0s, L = soff[s], seqs[s]
                    nkb = L // 128
                    QT = 512 if L % 512 == 0 else 128
                    for h in range(4):
                        kT_, qT_, v_ = kT_p.next(), qT_p.next(), v_p.next()
                        S.dma("sp", qT_.t[:, :, 0:L], dqkT[h * 256:(h + 1) * 256, t0s:t0s + L].rearrange("(m p) t -> p m t", p=128), writes=[qT_])
                        S.dma("sp", kT_.t[:, :, 0:L], dqkT[1024 + h * 256:1024 + (h + 1) * 256, t0s:t0s + L].rearrange("(m p) t -> p m t", p=128), writes=[kT_])
                        S.dma("sp", v_.t[:, 0:nkb, 0:256], dv[t0s:t0s + L, h * 256:(h + 1) * 256].rearrange("(j p) d -> p j d", p=128), writes=[v_])
                        S.op("pool", [], [v_], lambda e: e.memset(v_.t[:, 0:nkb, 256:258], 1.0))
                        for q0 in range(0, L, QT):
                            nqs = QT // 128
                            items = [(m, kb) for m in range(2) for kb in range(nkb)]
                            LA = 2

                            def emit_qk(m, kb):
                                k0 = kb * 128
                                rmin = k0 - (q0 + QT - 1)
                                rmax = k0 + 127 - q0
                                near = not (rmin >= 91 or rmax <= -91)
                                p = pq.next()
                                if near:
                                    dlt = (k0 - q0) // 128
                                    R = Rt[h][dlt + 1]

                                    def f(e):
                                        e.matmul(p.t[:, 0:QT], lhsT=kT_.t[:, m, k0:k0 + 128], rhs=qT_.t[:, m, q0:q0 + QT], start=True, stop=False)
                                        return e.matmul(p.t[:, 0:QT], lhsT=antib.t[:], rhs=R.t[:, 0:QT], start=False, stop=True)
                                    S.op("pe", [kT_, qT_, antib, R], [p], f)
                                    return p, None
                                S.op("pe", [kT_, qT_], [p], lambda e: e.matmul(p.t[:, 0:QT], lhsT=kT_.t[:, m, k0:k0 + 128], rhs=qT_.t[:, m, q0:q0 + QT], start=True, stop=True))
                                return p, (31 if rmin > 0 else 15) * 4 + h

                            pend = {}
                            for idx in range(min(LA, len(items))):
                                pend[idx] = emit_qk(*items[idx])
                            for idx, (m, kb) in enumerate(items):
                                if idx + LA < len(items):
                                    pend[idx + LA] = emit_qk(*items[idx + LA])
                                p, bcol = pend.pop(idx)
                                pT = pT_p.next()
                                if bcol is None:
                                    S.op("act", [p], [pT], lambda e: e.activation(out=pT.t[:, 0:QT], in_=p.t[:, 0:QT], func=AF.Exp, scale=scale))
                                else:
                                    S.op("act", [p, relbb], [pT], lambda e: e.activation(out=pT.t[:, 0:QT], in_=p.t[:, 0:QT], func=AF.Exp,
                                                                                          bias=relbb.t[:, bcol:bcol + 1], scale=scale))

                                def f(e):
                                    for qs in range(nqs):
                                        i = e.matmul(po[qs].t[:, 0:258], lhsT=pT.t[:, qs * 128:(qs + 1) * 128], rhs=v_.t[:, kb, :],
                                                     start=(kb == 0), stop=(kb == nkb - 1))
                                    return i
                                S.op("pe", [pT, v_], po[0:nqs], f)
                                if kb == nkb - 1:
                                    for qs in range(nqs):
                                        r = rs.next()
                                        S.op("dve", [po[qs]], [r], lambda e: e.reciprocal(out=r.t[:], in_=po[qs].t[:, 256:257]))
                                        if m == 1:
                                            S.op("dve", [r, nlam], [r], lambda e: e.tensor_tensor(out=r.t[:], in0=r.t[:], in1=nlam.t[:], op=ALU.mult))
                                        S.op("dve", [po[qs], r], [om], lambda e: e.tensor_scalar(out=om.t[:, qs, m, :], in0=po[qs].t[:, 0:256], scalar1=r.t[:, 0:1], scalar2=None, op0=ALU.mult))
                            st = stq.next()
                            for qs in range(nqs):
                                o, sqq, ssq = od.next(), sqd.next(), ssd.next()
                                S.op("pool", [om], [o], lambda e: e.tensor_tensor(out=o.t[:], in0=om.t[:, qs, 0, :], in1=om.t[:, qs, 1, :], op=ALU.add))
                                S.op("act", [o], [sqq], lambda e: e.activation(out=sqq.t[:], in_=o.t[:], func=AF.Square))
                                S.op("dve", [sqq], [ssq], lambda e: e.tensor_reduce(out=ssq.t[:], in_=sqq.t[:], axis=AX.X, op=ALU.add))
                                S.op("act", [ssq], [ssq], lambda e: e.activation(out=ssq.t[:], in_=ssq.t[:], func=AF.Sqrt, bias=1e-6, scale=1.0 / 256.0))
                                S.op("dve", [ssq], [ssq], lambda e: e.reciprocal(out=ssq.t[:], in_=ssq.t[:]))
                                S.op("dve", [o, ssq], [o], lambda e: e.tensor_scalar(out=o.t[:], in0=o.t[:], scalar1=ssq.t[:, 0:1], scalar2=(1.0 - LAMBDA_INIT), op0=ALU.mult, op1=ALU.mult))
                                S.op("pool", [o, dnw], [o], lambda e: e.tensor_tensor(out=o.t[:], in0=o.t[:], in1=dnw.t[:], op=ALU.mult))
                                p = ptr.next()

                                def f(e, p=p, o=o):
                                    e.transpose(p.t[:, 0:128], o.t[:, 0:128], ident)
                                    return e.transpose(p.t[:, 128:256], o.t[:, 128:256], ident)
                                S.op("pe", [o, cm], [p], f)
                                S.op("act", [p], [st], lambda e: e.copy(out=st.t[:, :, qs * 128:(qs + 1) * 128], in_=p.t[:, 0:256].rearrange("p (c t) -> p c t", t=128)))
                            S.dma("pool", ybT[1][h * 256:(h + 1) * 256, t0s + q0:t0s + q0 + QT].rearrange("(c p) t -> p c t", p=128), st.t[:, :, 0:QT], reads=[st])
            S.barrier()
            if upto == 6:
                raise _Stop()
            S.mute = start > 7

            with ExitStack() as es:
                km_p = S.pool(es, 2, [128, 8, MEM], BF16, "ckm")
                vm_p = S.pool(es, 2, [128, 2, 4, 258], BF16, "cvm")
                q_p = S.pool(es, 2, [128, 8, 512], BF16, "cq")
                pT_p = S.pool(es, 3, [128, 512], BF16, "cpT")
                pq = Ring([S.ps(es, [128, 512], F32, "cps") for _ in range(2)])
                po = Ring([S.ps(es, [128, 512], F32, "cpo") for _ in range(4)])
                ptr = Ring([S.ps(es, [128, 512], F32, "cpt") for _ in range(2)])
                rs = S.pool(es, 2, [128, 1], F32, "crs")
                oc = S.pool(es, 2, [128, 256], F32, "coc")
                stq = S.pool(es, 2, [128, 8, 512], BF16, "cstq")
                scale = 256.0 ** -0.5
                for s in range(NS):
                    t0s, L = soff[s], seqs[s]
                    km, vmt = km_p.next(), vm_p.next()
                    S.dma("sp", km.t[:], kmT[:, s * MEM:(s + 1) * MEM].rearrange("(c p) t -> p c t", p=128), writes=[km])
                    for b in range(2):
                        S.dma("sp", vmt.t[:, b, :, 0:256], vm[s * MEM + b * 128:s * MEM + (b + 1) * 128, :].rearrange("p (h d) -> p h d", d=256), writes=[vmt])
                    S.op("pool", [], [vmt], lambda e: e.memset(vmt.t[:, :, :, 256:258], 1.0))
                    QT = 512 if L % 512 == 0 else 128
                    for q0 in range(0, L, QT):
                        nqs = QT // 128
                        qt = q_p.next()
                        S.dma("sp", qt.t[:, :, 0:QT], cqT[:, t0s + q0:t0s + q0 + QT].rearrange("(c p) t -> p c t", p=128), writes=[qt])
                        st = stq.next()
                        for h in range(4):
                            pTs = []
                            for kb in range(2):
                                p = pq.next()

                                def f(e, p=p, kb=kb):
                                    e.matmul(p.t[:, 0:QT], lhsT=km.t[:, 2 * h, kb * 128:(kb + 1) * 128], rhs=qt.t[:, 2 * h, 0:QT], start=True, stop=False)
                                    return e.matmul(p.t[:, 0:QT], lhsT=km.t[:, 2 * h + 1, kb * 128:(kb + 1) * 128], rhs=qt.t[:, 2 * h + 1, 0:QT], start=False, stop=True)
                                S.op("pe", [km, qt], [p], f)
                                pT = pT_p.next()
                                S.op("act", [p], [pT], lambda e: e.activation(out=pT.t[:, 0:QT], in_=p.t[:, 0:QT], func=AF.Exp, scale=scale))
                                pTs.append(pT)
                            for qs in range(nqs):
                                pp_ = po.next()

                                def f(e, pp_=pp_, qs=qs):
                                    e.matmul(pp_.t[:, 0:258], lhsT=pTs[0].t[:, qs * 128:(qs + 1) * 128], rhs=vmt.t[:, 0, h, :], start=True, stop=False)
                                    return e.matmul(pp_.t[:, 0:258], lhsT=pTs[1].t[:, qs * 128:(qs + 1) * 128], rhs=vmt.t[:, 1, h, :], start=False, stop=True)
                                S.op("pe", [pTs[0], pTs[1], vmt], [pp_], f)
                                r, o = rs.next(), oc.next()
                                S.op("dve", [pp_], [r], lambda e: e.reciprocal(out=r.t[:], in_=pp_.t[:, 256:257]))
                                S.op("dve", [pp_, r], [o], lambda e: e.tensor_scalar(out=o.t[:], in0=pp_.t[:, 0:256], scalar1=r.t[:, 0:1], scalar2=None, op0=ALU.mult))
                                p = ptr.next()

                                def f(e, p=p, o=o):
                                    e.transpose(p.t[:, 0:128], o.t[:, 0:128], ident)
                                    return e.transpose(p.t[:, 128:256], o.t[:, 128:256], ident)
                                S.op("pe", [o, cm], [p], f)
                                S.op("act", [p], [st], lambda e: e.copy(out=st.t[:, 2 * h:2 * h + 2, qs * 128:(qs + 1) * 128], in_=p.t[:, 0:256].rearrange("p (c t) -> p c t", t=128)))
                        S.dma("pool", ybT[2][:, t0s + q0:t0s + q0 + QT].rearrange("(c p) t -> p c t", p=128), st.t[:, :, 0:QT], reads=[st])
            S.barrier()
            if upto == 7:
                raise _Stop()
            S.mute = start > 8

            with ExitStack() as es:
                wb_p = S.pool(es, 2, [128, 3, 8, 512], BF16, "mwb")
                y_p = S.pool(es, 2, [128, 3, 8, 512], BF16, "my")
                g_p = S.pool(es, 2, [128, 3, 4, 512], BF16, "mg")
                t_p = S.pool(es, 3, [128, 3, 512], F32, "mt")
                st_p = S.pool(es, 2, [128, 4, 512], BF16, "mst")
                pq = Ring([S.ps(es, [128, 512], F32, "mps") for _ in range(6)])
                sup = 512 if T % 512 == 0 else 128
                for fb in range(4):
                    wt = wb_p.next()
                    for b in range(3):
                        S.dma("sp", wt.t[:, b, :, :], wb_b[b][:, fb * 512:(fb + 1) * 512].rearrange("(c p) n -> p c n", p=128), writes=[wt])
                    for tg in range(0, T, sup):
                        yt, gt = y_p.next(), g_p.next()
                        for b in range(3):
                            S.dma("sp", yt.t[:, b, :, 0:sup], ybT[b][:, tg:tg + sup].rearrange("(c p) t -> p c t", p=128), writes=[yt])
                            S.dma("sp", gt.t[:, b, :, 0:sup], gatesT[b * D + fb * 512:b * D + (fb + 1) * 512, tg:tg + sup].rearrange("(c p) t -> p c t", p=128), writes=[gt])
                        st = st_p.next()
                        for fc in range(4):
                            tmp = t_p.next()
                            for b in range(3):
                                p = pq.next()

                                def f(e, p=p, b=b, fc=fc):
                                    for kc in range(8):
                                        i = e.matmul(p.t[:, 0:sup], lhsT=wt.t[:, b, kc, fc * 128:(fc + 1) * 128], rhs=yt.t[:, b, kc, 0:sup], start=(kc == 0), stop=(kc == 7))
                                    return i
                                S.op("pe", [wt, yt], [p], f)
                                S.op("dve", [p, gt], [tmp], lambda e: e.tensor_tensor(out=tmp.t[:, b, 0:sup], in0=p.t[:, 0:sup], in1=gt.t[:, b, fc, 0:sup], op=ALU.mult))
                            S.op("pool", [tmp], [tmp], lambda e: e.tensor_tensor(out=tmp.t[:, 0, 0:sup], in0=tmp.t[:, 0, 0:sup], in1=tmp.t[:, 1, 0:sup], op=ALU.add))
                            S.op("pool", [tmp], [st], lambda e: e.tensor_tensor(out=st.t[:, fc, 0:sup], in0=tmp.t[:, 0, 0:sup], in1=tmp.t[:, 2, 0:sup], op=ALU.add))
                        S.dma("pool", mT[fb * 512:(fb + 1) * 512, tg:tg + sup].rearrange("(c p) t -> p c t", p=128), st.t[:, :, 0:sup], reads=[st])
            S.barrier()
            if upto == 8:
                raise _Stop()
            S.mute = start > 9

            def layer_norm(r, lnp, gi, es_pools):
                stats, mv, outt = es_pools
                stt = stats.next()
                for c in range(4):
                    S.op("dve", [r], [stt], lambda e: e.bn_stats(out=stt.t[:, c, :], in_=r.t[:, c * 512:(c + 1) * 512]))
                m = mv.next()
                S.op("dve", [stt], [m], lambda e: e.bn_aggr(out=m.t[:, 0:2], in_=stt.t[:]))
                S.op("act", [m], [m], lambda e: e.activation(out=m.t[:, 2:3], in_=m.t[:, 1:2], func=AF.Sqrt, bias=1e-5, scale=1.0))
                S.op("dve", [m], [m], lambda e: e.reciprocal(out=m.t[:, 2:3], in_=m.t[:, 2:3]))
                o = outt.next()
                S.op("dve", [r, m], [o], lambda e: e.tensor_scalar(out=o.t[:], in0=r.t[:], scalar1=m.t[:, 0:1], scalar2=m.t[:, 2:3], op0=ALU.subtract, op1=ALU.mult))
                S.op("pool", [o, lnp], [o], lambda e: e.tensor_tensor(out=o.t[:], in0=o.t[:], in1=lnp.t[:, 0, :], op=ALU.mult))
                S.op("dve", [o, lnp], [o], lambda e: e.tensor_tensor(out=o.t[:], in0=o.t[:], in1=lnp.t[:, 1, :], op=ALU.add))
                return o

            with ExitStack() as es:
                lnp = S.sb(es, [128, 2, D], F32, "lnp")
                S.dma("sp", lnp.t[:], ln_d[:, 0:2, :], writes=[lnp])
                wo = S.sb(es, [128, 16, D], BF16, "wo")
                for c in range(0, 16, 4):
                    S.dma("sp", wo.t[:, c:c + 4, :], wo_b[c * 128:(c + 4) * 128, :].rearrange("(c p) n -> p c n", p=128), writes=[wo])
                sup = 512 if T % 512 == 0 else 128
                m_p = S.pool(es, 2, [128, 16, sup], BF16, "omT")
                x_p = S.pool(es, 2, [128, D], F32, "ox")
                r_p = S.pool(es, 2, [128, D], F32, "or")
                pools = (S.pool(es, 2, [128, 4, 6], F32, "ost"), S.pool(es, 2, [128, 4], F32, "omv"), S.pool(es, 2, [128, D], F32, "oout"))
                xo = S.pool(es, 2, [128, 16, sup], BF16, "oxo")
                pq = Ring([S.ps(es, [128, 512], F32, "ops") for _ in range(4)])
                ptr = Ring([S.ps(es, [128, 512], F32, "opt") for _ in range(4)])
                for tg in range(0, T, sup):
                    mt = m_p.next()
                    S.dma("sp", mt.t[:], mT[:, tg:tg + sup].rearrange("(c p) t -> p c t", p=128), writes=[mt])
                    xot = xo.next()
                    for jj in range(sup // 128):
                        tt = tg + jj * 128
                        xt = x_p.next()
                        S.dma("sp", xt.t[:], x_d[tt:tt + 128, :], writes=[xt])
                        r = r_p.next()
                        for nb in range(4):
                            p = pq.next()

                            def f(e, p=p, nb=nb):
                                for kc in range(16):
                                    i = e.matmul(p.t[:], lhsT=mt.t[:, kc, jj * 128:(jj + 1) * 128], rhs=wo.t[:, kc, nb * 512:(nb + 1) * 512], start=(kc == 0), stop=(kc == 15))
                                return i
                            S.op("pe", [mt, wo], [p], f)
                            S.op("dve", [p, xt], [r], lambda e: e.scalar_tensor_tensor(out=r.t[:, nb * 512:(nb + 1) * 512], in0=xt.t[:, nb * 512:(nb + 1) * 512], scalar=ALPHA,
                                                                                      in1=p.t[:], op0=ALU.mult, op1=ALU.add))
                        o = layer_norm(r, lnp, 0, pools)
                        S.dma("pool", x1[tt:tt + 128, :], o.t[:], reads=[o])
                        for kc0 in range(0, 16, 4):
                            p = ptr.next()

                            def f(e, p=p, kc0=kc0, o=o):
                                for c in range(4):
                                    i = e.transpose(p.t[:, c * 128:(c + 1) * 128], o.t[:, (kc0 + c) * 128:(kc0 + c + 1) * 128], ident)
                                return i
                            S.op("pe", [o, cm], [p], f)
                            S.op("act", [p], [xot], lambda e: e.copy(out=xot.t[:, kc0:kc0 + 4, jj * 128:(jj + 1) * 128], in_=p.t[:].rearrange("p (c t) -> p c t", t=128)))
                    S.dma("pool", x1T[:, tg:tg + sup].rearrange("(c p) t -> p c t", p=128), xot.t[:], reads=[xot])
            S.barrier()
            if upto == 9:
                raise _Stop()
            S.mute = start > 10

            with ExitStack() as es:
                LM = max(s[1] for s in segs)
                fcw = S.sb(es, [128, 2 * NFC, 3], F32, "fcw")
                S.dma("sp", fcw.t[:], fconv_d[:, :, :], writes=[fcw])
                xs_p = S.pool(es, 1, [128, 16, LM + 2], BF16, "fxs")
                w_p = S.pool(es, 3, [128, 2, 16, 128], BF16, "fw")
                u_p = S.pool(es, 2, [128, 2, LM + 2], F32, "fu")
                c_p = S.pool(es, 2, [128, 2, LM], F32, "fc")
                h_p = S.pool(es, 2, [128, LM], BF16, "fh")
                pq = Ring([S.ps(es, [128, 512], F32, "fps") for _ in range(6)])
                for (t0, L, hl, hr) in segs:
                    xs = xs_p.next()
                    lo = 1 - (1 if hl else 0)
                    hi = L + 1 + (1 if hr else 0)
                    if not hl:
                        S.op("pool", [], [xs], lambda e: e.memset(xs.t[:, :, 0:1], 0.0))
                    if not hr:
                        S.op("pool", [], [xs], lambda e: e.memset(xs.t[:, :, L + 1:L + 2], 0.0))
                    S.dma("sp", xs.t[:, :, lo:hi], x1T[:, t0 - 1 + lo:t0 - 1 + hi].rearrange("(c p) t -> p c t", p=128), writes=[xs])
                    cols = list(range(0, L + 2, 512))
                    for c in range(NFC):
                        w = w_p.next()
                        for gv in range(2):
                            cc = gv * DFF + c * 128
                            S.dma("sp", w.t[:, gv, :, :], wu_b[:, cc:cc + 128].rearrange("(c p) n -> p c n", p=128), writes=[w])
                        u = u_p.next()
                        for gv in range(2):
                            for ci, c0 in enumerate(cols):
                                cw = min(512, L + 2 - c0)
                                p = pq.next()

                                def f(e, p=p, gv=gv, c0=c0, cw=cw):
                                    for kc in range(16):
                                        i = e.matmul(p.t[:, 0:cw], lhsT=w.t[:, gv, kc, :], rhs=xs.t[:, kc, c0:c0 + cw], start=(kc == 0), stop=(kc == 15))
                                    return i
                                S.op("pe", [w, xs], [p], f)
                                if (ci + gv) % 2 == 0:
                                    S.op("act", [p], [u], lambda e: e.copy(out=u.t[:, gv, c0:c0 + cw], in_=p.t[:, 0:cw]))
                                else:
                                    S.op("dve", [p], [u], lambda e: e.tensor_copy(out=u.t[:, gv, c0:c0 + cw], in_=p.t[:, 0:cw]))
                        cv = c_p.next()
                        for gv in range(2):
                            ch = gv * NFC + c
                            eng = "dve"
                            S.op(eng, [u, fcw], [cv], lambda e: e.tensor_scalar(out=cv.t[:, gv, 0:L], in0=u.t[:, gv, 0:L], scalar1=fcw.t[:, ch, 0:1], scalar2=None, op0=ALU.mult))
                            for j in (1, 2):
                                S.op(eng, [u, fcw, cv], [cv], lambda e: e.scalar_tensor_tensor(out=cv.t[:, gv, 0:L], in0=u.t[:, gv, j:j + L], scalar=fcw.t[:, ch, j:j + 1],
                                                                                                 in1=cv.t[:, gv, 0:L], op0=ALU.mult, op1=ALU.add))
                        S.op("act", [cv], [cv], lambda e: e.activation(out=cv.t[:, 0, 0:L], in_=cv.t[:, 0, 0:L], func=AF.Silu))
                        hh = h_p.next()
                        S.op("dve", [cv], [hh], lambda e: e.tensor_tensor(out=hh.t[:, 0:L], in0=cv.t[:, 0, 0:L], in1=cv.t[:, 1, 0:L], op=ALU.mult))
                        S.dma("pool", hT[c * 128:(c + 1) * 128, t0:t0 + L], hh.t[:, 0:L], reads=[hh])
            S.barrier()
            if upto == 10:
                raise _Stop()
            S.mute = start > 11

            with ExitStack() as es:
                lnp = S.sb(es, [128, 2, D], F32, "lnp2")
                S.dma("sp", lnp.t[:], ln_d[:, 2:4, :], writes=[lnp])
                wd = S.sb(es, [128, NFC, 1024], BF16, "wd")
                TG = 256 if T % 256 == 0 else 128
                h_p = S.pool(es, 2, [128, NFC, TG], BF16, "dh")
                yp_p = S.pool(es, 2, [128, 1024], F32, "dyp")
                x_p = S.pool(es, 2, [128, D], F32, "dx1")
                r_p = S.pool(es, 1, [128, D], F32, "dr")
                pools = (S.pool(es, 2, [128, 4, 6], F32, "dst"), S.pool(es, 2, [128, 4], F32, "dmv"), S.pool(es, 2, [128, D], F32, "dout"))
                pq = Ring([S.ps(es, [128, 512], F32, "dps") for _ in range(6)])
                for half in range(2):
                    for c in range(0, NFC, 8):
                        ce = min(NFC, c + 8)
                        S.dma("sp", wd.t[:, c:ce, :], wd_b[c * 128:ce * 128, half * 1024:(half + 1) * 1024].rearrange("(c p) n -> p c n", p=128), writes=[wd])
                    for tg in range(0, T, TG):
                        ht = h_p.next()
                        S.dma("sp", ht.t[:], hT[:, tg:tg + TG].rearrange("(c p) t -> p c t", p=128), writes=[ht])
                        for jj in range(TG // 128):
                            tt = tg + jj * 128
                            ps2 = [pq.next(), pq.next()]
                            for nb in range(2):
                                p = ps2[nb]

                                def f(e, p=p, nb=nb):
                                    for kc in range(NFC):
                                        i = e.matmul(p.t[:], lhsT=ht.t[:, kc, jj * 128:(jj + 1) * 128], rhs=wd.t[:, kc, nb * 512:(nb + 1) * 512], start=(kc == 0), stop=(kc == NFC - 1))
                                    return i
                                S.op("pe", [ht, wd], [p], f)
                            if half == 0:
                                yp = yp_p.next()
                                S.op("act", [ps2[0]], [yp], lambda e: e.copy(out=yp.t[:, 0:512], in_=ps2[0].t[:]))
                                S.op("dve", [ps2[1]], [yp], lambda e: e.tensor_copy(out=yp.t[:, 512:1024], in_=ps2[1].t[:]))
                                S.dma("pool", ypart[tt:tt + 128, :], yp.t[:], reads=[yp])
                            else:
                                yp, xt, r = yp_p.next(), x_p.next(), r_p.next()
                                S.dma("sp", yp.t[:], ypart[tt:tt + 128, :], writes=[yp])
                                S.dma("sp", xt.t[:], x1[tt:tt + 128, :], writes=[xt])
                                S.op("dve", [xt, yp], [r], lambda e: e.scalar_tensor_tensor(out=r.t[:, 0:1024], in0=xt.t[:, 0:1024], scalar=ALPHA, in1=yp.t[:], op0=ALU.mult, op1=ALU.add))
                                for nb in range(2):
                                    S.op("dve", [ps2[nb], xt], [r], lambda e: e.scalar_tensor_tensor(
                                        out=r.t[:, 1024 + nb * 512:1024 + (nb + 1) * 512], in0=xt.t[:, 1024 + nb * 512:1024 + (nb + 1) * 512], scalar=ALPHA,
                                        in1=ps2[nb].t[:], op0=ALU.mult, op1=ALU.add))
                                o = layer_norm(r, lnp, 2, pools)
                                S.dma("pool", y_d[tt:tt + 128, :], o.t[:], reads=[o])
                    S.barrier()
            S.barrier()
            if upto == 11:
                raise _Stop()
            S.mute = start > 12

        except _Stop:
            pass
        S.mute = False
        S.barrier()

    return nc


def _t5_bucket(rel):
    nb = 16
    max_exact = 8
    ret = np.where(rel > 0, nb, 0)
    n = np.abs(rel)
    nf = np.maximum(n, 1).astype(np.float32)
    large = max_exact + (np.log(nf / np.float32(max_exact)) / np.float32(math.log(128 / max_exact)) * np.float32(nb - max_exact)).astype(np.int32)
    large = np.minimum(large, nb - 1)
    return ret + np.where(n < max_exact, n, large)


def _consts():
    p = np.arange(128)[:, None]
    f = np.arange(128)[None, :]
    cm = np.zeros((128, 7, 128), np.float32)
    cm[:, 0] = (p == f)
    cm[:, 1] = 1.0
    cm[:, 2] = (p >= f)
    cm[:, 3] = (p <= f)
    cm[:, 4] = (p > f)
    cm[:, 5] = (p < f)
    cm[:, 6] = (p + f == 127)
    rel = 639 - np.arange(1280)
    b = _t5_bucket(rel)
    oh = np.zeros((32, 1280), np.float32)
    oh[b, np.arange(1280)] = 1.0
    return cm, oh


def make_in_maps(inp, per_core_x, per_core_mem):
    f = lambda a: np.ascontiguousarray(a, dtype=np.float32)
    cm, oh = _consts()
    bc = lambda v, n=128: np.ascontiguousarray(np.broadcast_to(np.asarray(v, np.float32).reshape(1, -1), (n, np.asarray(v).size)))
    shared = {
        "w_in": f(inp["w_in"][0]), "w_gate": f(inp["w_gate"][0]), "w_kv": f(inp["w_mem_kv"][0]),
        "wb0": f(inp["w_branch_gdn"][0]), "wb1": f(inp["w_branch_diff"][0]), "wb2": f(inp["w_branch_cross"][0]),
        "w_out": f(inp["w_out"][0]), "w_up": f(inp["w_up"][0]), "w_down": f(inp["w_down"][0]),
        "gconv": f(np.asarray(inp["gdn_conv"][0]).reshape(5, 24, 128).transpose(2, 1, 0)),
        "fconv": f(np.asarray(inp["ffn_conv"][0]).reshape(3, 2 * NFC, 128).transpose(2, 1, 0)),
        "bgate": f(np.asarray(inp["b_gate"][0]).reshape(48, 128).T),
        "alog": bc(np.asarray(inp["gdn_a_log"][0]).reshape(-1)),
        "dtb": bc(np.asarray(inp["gdn_dt_bias"][0]).reshape(-1)),
        "gnw": bc(inp["gdn_norm_w"][0]),
        "dlam": f(np.broadcast_to(np.asarray(inp["diff_lambda"][0], np.float32)[None], (128, 4, 128))),
        "dnw": bc(inp["diff_norm_w"][0]),
        "lnp": f(np.broadcast_to(np.stack([np.asarray(inp[k][0], np.float32) for k in ("ln1_g", "ln1_b", "ln2_g", "ln2_b")])[None], (128, 4, D))),
        "relb": f(inp["rel_bias"]),
        "cmask": cm, "oh": oh,
    }
    maps = []
    for xc, mc in zip(per_core_x, per_core_mem):
        d = dict(shared)
        d["x"] = f(xc)
        d["mem"] = f(mc)
        maps.append(d)
    return maps


_NC_CACHE = {}


def kernel(**inp):
    xp = np.asarray(inp["x_prompt"], np.float32)
    xs = np.asarray(inp["x_sample"], np.float32)
    mp = np.asarray(inp["mem_prompt"], np.float32)
    ms = np.asarray(inp["mem_sample"], np.float32)
    n = 8
    seqs = (2048, 2048, 4096)
    px, pm = [], []
    for c in range(n):
        px.append(np.concatenate([xp[2 * c], xp[2 * c + 1], xs[c]], axis=0))
        pm.append(np.concatenate([mp[2 * c], mp[2 * c + 1], ms[c]], axis=0))
    maps = make_in_maps(inp, px, pm)
    if seqs not in _NC_CACHE:
        _NC_CACHE[seqs] = build_program(list(seqs))
    nc = _NC_CACHE[seqs]
    res = run_bass_kernel_spmd(nc, maps, core_ids=list(range(n)))
    yp = np.empty_like(xp)
    ys = np.empty_like(xs)
    for c in range(n):
        y = res.results[c]["y"]
        yp[2 * c] = y[0:2048]
        yp[2 * c + 1] = y[2048:4096]
        ys[c] = y[4096:8192]
    return (yp, ys)
```

```python
import math
from contextlib import ExitStack
import numpy as np
import concourse.bass as bass
import concourse.mybir as mybir
from concourse.bass_utils import run_bass_kernel_spmd

F32 = mybir.dt.float32
BF16 = mybir.dt.bfloat16
AF = mybir.ActivationFunctionType
ALU = mybir.AluOpType
AX = mybir.AxisListType

D = 2048
GW = 1024
NH = 8
IN_COLS = 8224
DFF = 5504
NFC = DFF // 128
MEM = 256
ALPHA = 2.0 ** 0.25
LAMBDA_INIT = 0.8 - 0.6 * math.exp(0.0)
SAME_ENGINE_SYNC = True
NDMASEM = 24


class T_:
    __slots__ = ("t", "w", "r")

    def __init__(self, t):
        self.t = t
        self.w = {}
        self.r = {}


class Sy:
    def __init__(self, nc, es):
        self.nc = nc
        self.es = es
        self.eng = {"pe": nc.tensor, "act": nc.scalar, "dve": nc.vector, "pool": nc.gpsimd, "sp": nc.sync}
        self.sem = {}
        self.cnt = {}
        self.known = {k: {} for k in self.eng}
        self.semobj = {}
        for k in self.eng:
            s = es.enter_context(nc.semaphore("s_" + k))
            self.sem[k] = s
            self.cnt[k] = 0
            self.semobj[id(s)] = s
        self.dsem = []
        self.dval = []
        for i in range(NDMASEM):
            s = es.enter_context(nc.semaphore("d%d" % i))
            self.dsem.append(s)
            self.dval.append(0)
            self.semobj[id(s)] = s
        self.di = 0
        self.nwait = 0
        self.uid = 0
        self.mute = False

    def sb(self, es, shape, dt, name=None):
        self.uid += 1
        return T_(es.enter_context(self.nc.sbuf_tensor("%s_%d" % (name or "t", self.uid), list(shape), dt)))

    def ps(self, es, shape, dt, name=None):
        self.uid += 1
        return T_(es.enter_context(self.nc.psum_tensor("%s_%d" % (name or "p", self.uid), list(shape), dt)))

    def pool(self, es, n, shape, dt, name=None):
        return Ring([self.sb(es, shape, dt, name) for _ in range(n)])

    def _wait(self, e, waits):
        E = self.eng[e]
        kn = self.known[e]
        for sid, v in waits.items():
            if kn.get(sid, 0) < v:
                E.wait_ge(self.semobj[sid], v)
                kn[sid] = v
                self.nwait += 1

    def _collect(self, e, reads, writes):
        waits = {}
        own = id(self.sem[e])

        def add(d):
            for sid, v in d.items():
                if sid == own and (e == "pe" or not SAME_ENGINE_SYNC):
                    continue
                if waits.get(sid, 0) < v:
                    waits[sid] = v

        for t in reads:
            add(t.w)
        for t in writes:
            add(t.w)
            add(t.r)
        return waits

    def _commit(self, reads, writes, sid, v):
        for t in reads:
            if t.r.get(sid, 0) < v:
                t.r[sid] = v
        for t in writes:
            t.w = {sid: v}
            t.r = {}

    def op(self, e, reads, writes, fn):
        if self.mute:
            return
        self._wait(e, self._collect(e, reads, writes))
        inst = fn(self.eng[e])
        self.cnt[e] += 1
        inst.then_inc(self.sem[e], 1)
        self._commit(reads, writes, id(self.sem[e]), self.cnt[e])

    def dma(self, q, out_ap, in_ap, reads=(), writes=(), **kw):
        if self.mute:
            return
        i = self.di
        self.di = (self.di + 1) % NDMASEM
        s = self.dsem[i]
        waits = self._collect(q, reads, writes)
        if self.dval[i] > 0:
            waits[id(s)] = max(waits.get(id(s), 0), self.dval[i])
        self._wait(q, waits)
        self.dval[i] += 16
        self.eng[q].dma_start(out=out_ap, in_=in_ap, **kw).then_inc(s, 16)
        self._commit(reads, writes, id(s), self.dval[i])

    def barrier(self):
        if self.mute:
            return
        allw = {}
        for k in self.eng:
            if self.cnt[k] > 0:
                allw[id(self.sem[k])] = self.cnt[k]
        for i in range(NDMASEM):
            if self.dval[i] > 0:
                allw[id(self.dsem[i])] = self.dval[i]
        for e in self.eng:
            w = dict(allw)
            w.pop(id(self.sem[e]), None)
            self._wait(e, w)
        for e in ("act", "dve", "pool"):
            if self.cnt[e] > 0:
                self._wait(e, {id(self.sem[e]): self.cnt[e]})


class Ring:
    def __init__(self, tiles):
        self.tiles = tiles
        self.i = 0

    def next(self):
        t = self.tiles[self.i]
        self.i = (self.i + 1) % len(self.tiles)
        return t


def act_copy2(e, dst, src):
    e.copy(out=dst.t[:, 0:4, :], in_=src.t[:, 0:4, :])
    return e.copy(out=dst.t[:, 4:8, :], in_=src.t[:, 4:8, :])


def bc_last(ap, n):
    sh = list(ap.shape)
    return ap.unsqueeze(len(sh)).to_broadcast(sh + [n])


def bc_mid(ap, n):
    sh = list(ap.shape)
    return ap.unsqueeze(1).to_broadcast([sh[0], n] + sh[1:])


class _Stop(Exception):
    pass


def build_program(seqs, dbg=False, upto=99, start=0, ext_in=(), bisect=0):
    T = sum(seqs)
    NS = len(seqs)
    soff = [sum(seqs[:i]) for i in range(NS)]
    segs = []
    for s, L in enumerate(seqs):
        n = max(1, L // 2048)
        sl = L // n
        for i in range(n):
            segs.append((soff[s] + i * sl, sl, i > 0, i < n - 1))

    nc = bass.Bass("TRN2", target_bir_lowering=False)
    kin = "ExternalInput"
    kint = "ExternalOutput" if dbg else "Internal"

    def din(name, shape, dt=F32):
        return nc.dram_tensor(name, list(shape), dt, kind=kin).ap()

    def dsc(name, shape, dt):
        if name in ext_in:
            return nc.dram_tensor(name, list(shape), dt, kind="ExternalInput").ap()
        return nc.dram_tensor(name, list(shape), dt, kind=("Internal" if name.startswith("w") else kint)).ap()

    x_d = din("x", [T, D])
    mem_d = din("mem", [NS * MEM, D])
    w_in_d = din("w_in", [D, IN_COLS])
    w_gate_d = din("w_gate", [D, 3 * D])
    w_kv_d = din("w_kv", [D, 2 * GW])
    wb_d = [din("wb%d" % i, [GW, D]) for i in range(3)]
    w_out_d = din("w_out", [D, D])
    w_up_d = din("w_up", [D, 2 * DFF])
    w_down_d = din("w_down", [DFF, D])
    gconv_d = din("gconv", [128, 24, 5])
    fconv_d = din("fconv", [128, 2 * NFC, 3])
    bgate_d = din("bgate", [128, 48])
    alog_d = din("alog", [128, 16])
    dtb_d = din("dtb", [128, 16])
    gnw_d = din("gnw", [128, 128])
    dlam_d = din("dlam", [128, 4, 128])
    dnw_d = din("dnw", [128, 256])
    ln_d = din("lnp", [128, 4, D])
    relb_d = din("relb", [32, 4])
    cm_d = din("cmask", [128, 7, 128])
    oh_d = din("oh", [32, 1280])

    y_d = nc.dram_tensor("y", [T, D], F32, kind="ExternalOutput").ap()

    wi_b = dsc("wi_b", [D, IN_COLS], BF16)
    wg_b = dsc("wg_b", [D, 3 * D], BF16)
    wkv_b = dsc("wkv_b", [D, 2 * GW], BF16)
    wb_b = [dsc("wb_b%d" % i, [GW, D], BF16) for i in range(3)]
    wo_b = dsc("wo_b", [D, D], BF16)
    wu_b = dsc("wu_b", [D, 2 * DFF], BF16)
    wd_b = dsc("wd_b", [DFF, D], BF16)
    xT = dsc("xT", [D, T], BF16)
    memT = dsc("memT", [D, NS * MEM], BF16)
    gqkvT = dsc("gqkvT", [3 * GW, T], F32)
    dqkT = dsc("dqkT", [2 * GW, T], BF16)
    cqT = dsc("cqT", [GW, T], BF16)
    gatesT = dsc("gatesT", [3 * D, T], BF16)
    gz = dsc("gz", [T, GW], BF16)
    gab = dsc("gab", [T, 32], F32)
    dv = dsc("dv", [T, GW], BF16)
    kmT = dsc("kmT", [GW, NS * MEM], BF16)
    vm = dsc("vm", [NS * MEM, GW], BF16)
    qTn = dsc("qTn", [GW, T], BF16)
    kTn = dsc("kTn", [GW, T], BF16)
    ktok = dsc("ktok", [T, GW], BF16)
    vtok = dsc("vtok", [T, GW], BF16)
    odir = dsc("odir", [2, T, GW], F32)
    ybT = [dsc("ybT%d" % i, [GW, T], BF16) for i in range(3)]
    mT = dsc("mT", [D, T], BF16)
    x1 = dsc("x1", [T, D], F32)
    x1T = dsc("x1T", [D, T], BF16)
    hT = dsc("hT", [DFF, T], BF16)
    ypart = dsc("ypart", [T, GW], F32)
    tvec = dsc("tvec", [4, 1280], BF16)

    with ExitStack() as es0:
        S = Sy(nc, es0)
        try:
            cm = S.sb(es0, [128, 7, 128], F32, "cm")
            S.dma("sp", cm.t[:], cm_d[:, :, :], writes=[cm])
            ident = cm.t[:, 0, :]
            identb = S.sb(es0, [128, 128], BF16, "identb")
            onesb = S.sb(es0, [128, 128], BF16, "onesb")
            antib = S.sb(es0, [128, 128], BF16, "antib")
            S.op("dve", [cm], [identb], lambda e: e.tensor_copy(out=identb.t[:], in_=cm.t[:, 0, :]))
            S.op("dve", [cm], [onesb], lambda e: e.tensor_copy(out=onesb.t[:], in_=cm.t[:, 1, :]))
            S.op("dve", [cm], [antib], lambda e: e.tensor_copy(out=antib.t[:], in_=cm.t[:, 6, :]))

            S.mute = start > 0
            with ExitStack() as es:
                CB = 2048
                fin = S.pool(es, 3, [128, CB], F32, "wc_in")
                fout = S.pool(es, 3, [128, CB], BF16, "wc_out")
                k = 0
                for src, dst in ([(w_in_d, wi_b), (w_gate_d, wg_b), (w_kv_d, wkv_b)] +
                                 [(wb_d[i], wb_b[i]) for i in range(3)] +
                                 [(w_out_d, wo_b), (w_up_d, wu_b), (w_down_d, wd_b)]):
                    R, C = src.shape
                    for r0 in range(0, R, 128):
                        for c0 in range(0, C, CB):
                            cw = min(CB, C - c0)
                            a = fin.next()
                            b = fout.next()
                            S.dma("sp", a.t[:, 0:cw], src[r0:r0 + 128, c0:c0 + cw], writes=[a])
                            eng = ("dve", "act", "pool")[k % 3]
                            k += 1
                            if eng == "act":
                                S.op(eng, [a], [b], lambda e: e.copy(out=b.t[:, 0:cw], in_=a.t[:, 0:cw]))
                            else:
                                S.op(eng, [a], [b], lambda e: e.tensor_copy(out=b.t[:, 0:cw], in_=a.t[:, 0:cw]))
                            S.dma("pool", dst[r0:r0 + 128, c0:c0 + cw], b.t[:, 0:cw], reads=[b])
            S.barrier()
            if upto == 0:
                raise _Stop()
            S.mute = start > 1

            def xpose_phase(src, dst, ntok, sup):
                with ExitStack() as es:
                    xin = S.pool(es, 2, [128, sup // 128, D], F32, "xin")
                    xo = S.pool(es, 2, [128, 16, sup], BF16, "xo")
                    pp = Ring([S.ps(es, [128, 512], F32, "xp") for _ in range(4)])
                    ev = 0
                    for t0 in range(0, ntok, sup):
                        a = xin.next()
                        o = xo.next()
                        S.dma("sp", a.t[:], src[t0:t0 + sup, :].rearrange("(j p) d -> p j d", p=128), writes=[a])
                        for kc in range(16):
                            for j0 in range(0, sup // 128, 4):
                                nj = min(4, sup // 128 - j0)
                                p = pp.next()

                                def f(e, p=p, j0=j0, nj=nj, kc=kc):
                                    for j in range(nj):
                                        i = e.transpose(p.t[:, j * 128:(j + 1) * 128],
                                                        a.t[:, j0 + j, kc * 128:(kc + 1) * 128], ident)
                                    return i
                                S.op("pe", [a, cm], [p], f)
                                eng = ("dve", "act")[ev % 2]
                                ev += 1
                                if eng == "act":
                                    S.op(eng, [p], [o], lambda e: e.copy(out=o.t[:, kc, j0 * 128:(j0 + nj) * 128],
                                                                        in_=p.t[:, 0:nj * 128]))
                                else:
                                    S.op(eng, [p], [o], lambda e: e.tensor_copy(out=o.t[:, kc, j0 * 128:(j0 + nj) * 128],
                                                                               in_=p.t[:, 0:nj * 128]))
                        S.dma("pool", dst[:, t0:t0 + sup].rearrange("(c p) t -> p c t", p=128), o.t[:], reads=[o])

            xpose_phase(x_d, xT, T, 512 if T % 512 == 0 else 128)
            S.barrier2 = None
            for _e in (1,):
                S.barrier()
            xpose_phase(mem_d, memT, NS * MEM, MEM)
            S.barrier()
            if upto == 1:
                raise _Stop()
            S.mute = start > 2

            with ExitStack() as es:
                LM = max(s[1] for s in segs)
                xs_pool = S.pool(es, 1, [128, 16, LM], BF16, "xseg")
                wc_pool = S.pool(es, 2, [128, 16, 512], BF16, "wcb")
                stf = S.pool(es, 2, [128, LM], F32, "stf")
                stb = S.pool(es, 2, [128, LM], BF16, "stb")
                stt = S.pool(es, 2, [128, 4, 512], BF16, "stt")
                stg = S.pool(es, 2, [128, 32], F32, "stg")
                pp = Ring([S.ps(es, [128, 512], F32, "pj") for _ in range(6)])
                bg = S.sb(es, [128, 48], F32, "bg")
                S.dma("sp", bg.t[:], bgate_d[:, :], writes=[bg])
                evc = [0]

                def load_w(wsrc, c0, cw):
                    w = wc_pool.next()
                    S.dma("sp", w.t[:, :, 0:cw], wsrc[:, c0:c0 + cw].rearrange("(c p) n -> p c n", p=128), writes=[w])
                    return w

                def fm_block(xs, t0, L, wsrc, c0, ncols, dst, r0, dt, sig_chunk0=None):
                    for cb in range(0, ncols, 512):
                        cw = min(512, ncols - cb)
                        w = load_w(wsrc, c0 + cb, cw)
                        for fc in range(cw // 128):
                            st = (stf if dt == F32 else stb).next()
                            for tt in range(0, L, 512):
                                tw = min(512, L - tt)
                                p = pp.next()

                                def f(e, p=p, tt=tt, tw=tw, fc=fc):
                                    for kc in range(16):
                                        i = e.matmul(p.t[:, 0:tw], lhsT=w.t[:, kc, fc * 128:(fc + 1) * 128],
                                                     rhs=xs.t[:, kc, tt:tt + tw], start=(kc == 0), stop=(kc == 15))
                                    return i
                                S.op("pe", [w, xs], [p], f)
                                if sig_chunk0 is not None:
                                    ch = sig_chunk0 + (cb // 128) + fc
                                    S.op("act", [p, bg], [st], lambda e: e.activation(
                                        out=st.t[:, tt:tt + tw], in_=p.t[:, 0:tw], func=AF.Sigmoid,
                                        bias=bg.t[:, ch:ch + 1], scale=1.0))
                                else:
                                    evc[0] += 1
                                    if evc[0] % 2:
                                        S.op("dve", [p], [st], lambda e: e.tensor_copy(out=st.t[:, tt:tt + tw], in_=p.t[:, 0:tw]))
                                    else:
                                        S.op("act", [p], [st], lambda e: e.copy(out=st.t[:, tt:tt + tw], in_=p.t[:, 0:tw]))
                            rr = r0 + cb + fc * 128
                            S.dma("pool", dst[rr:rr + 128, t0:t0 + L], st.t[:, 0:L], reads=[st])

                def tm_block(xs, t0, L, wsrc, c0, ncols, dst, dc0, silu):
                    for cb in range(0, ncols, 512):
                        cw = min(512, ncols - cb)
                        w = load_w(wsrc, c0 + cb, cw)
                        for tg in range(0, L, 512):
                            ng = min(4, (L - tg) // 128)
                            st = stt.next()
                            for j in range(ng):
                                tt = tg + j * 128
                                p = pp.next()

                                def f(e, p=p, tt=tt):
                                    for kc in range(16):
                                        i = e.matmul(p.t[:, 0:cw], lhsT=xs.t[:, kc, tt:tt + 128],
                                                     rhs=w.t[:, kc, 0:cw], start=(kc == 0), stop=(kc == 15))
                                    return i
                                S.op("pe", [w, xs], [p], f)
                                if silu:
                                    S.op("act", [p], [st], lambda e: e.activation(out=st.t[:, j, 0:cw], in_=p.t[:, 0:cw], func=AF.Silu))
                                else:
                                    S.op("dve", [p], [st], lambda e: e.tensor_copy(out=st.t[:, j, 0:cw], in_=p.t[:, 0:cw]))
                            S.dma("pool", dst[t0 + tg:t0 + tg + ng * 128, dc0 + cb:dc0 + cb + cw].rearrange("(j p) n -> p j n", p=128),
                                  st.t[:, 0:ng, 0:cw], reads=[st])

                for (t0, L, _l, _r) in segs:
                    xs = xs_pool.next()
                    S.dma("sp", xs.t[:, :, 0:L], xT[:, t0:t0 + L].rearrange("(c p) t -> p c t", p=128), writes=[xs])
                    fm_block(xs, t0, L, wi_b, 0, 3072, gqkvT, 0, F32)
                    fm_block(xs, t0, L, wi_b, 4128, 2048, dqkT, 0, BF16)
                    fm_block(xs, t0, L, wi_b, 7200, 1024, cqT, 0, BF16)
                    fm_block(xs, t0, L, wg_b, 0, 3 * D, gatesT, 0, BF16, sig_chunk0=0)
                    tm_block(xs, t0, L, wi_b, 3072, 1024, gz, 0, True)
                    tm_block(xs, t0, L, wi_b, 6176, 1024, dv, 0, False)
                    w = wc_pool.next()
                    S.dma("sp", w.t[:, :, 0:32], wi_b[:, 4096:4128].rearrange("(c p) n -> p c n", p=128), writes=[w])
                    for tt in range(0, L, 128):
                        p = pp.next()
                        sg = stg.next()

                        def f(e, p=p, tt=tt):
                            for kc in range(16):
                                i = e.matmul(p.t[:, 0:32], lhsT=xs.t[:, kc, tt:tt + 128], rhs=w.t[:, kc, 0:32],
                                             start=(kc == 0), stop=(kc == 15))
                            return i
                        S.op("pe", [w, xs], [p], f)
                        S.op("dve", [p], [sg], lambda e: e.tensor_copy(out=sg.t[:], in_=p.t[:, 0:32]))
                        S.dma("pool", gab[t0 + tt:t0 + tt + 128, :], sg.t[:], reads=[sg])
                xs = xs_pool.next()
                NM = NS * MEM
                S.dma("sp", xs.t[:, :, 0:NM], memT[:, :].rearrange("(c p) t -> p c t", p=128), writes=[xs])
                fm_block(xs, 0, NM, wkv_b, 0, 1024, kmT, 0, BF16)
                tm_block(xs, 0, NM, wkv_b, 1024, 1024, vm, 0, False)
            S.barrier()
            if upto == 2:
                raise _Stop()
            S.mute = start > 3

            with ExitStack() as es:
                SM = max(seqs)
                gcw = S.sb(es, [128, 24, 5], F32, "gcw")
                S.dma("sp", gcw.t[:], gconv_d[:, :, :], writes=[gcw])
                xin = S.pool(es, 2, [128, SM + 4], F32, "cxin")
                acc = S.pool(es, 2, [128, SM], F32, "cacc")
                ys = S.pool(es, 2, [128, SM], F32, "cys")
                sq = S.pool(es, 2, [128, SM], BF16, "csq")
                rn = S.pool(es, 2, [128, 512], F32, "crn")
                yn = S.pool(es, 2, [128, SM], BF16, "cyn")
                ynf = S.pool(es, 1, [128, SM], F32, "cynf")
                tst = S.pool(es, 2, [128, 4, 128], BF16, "ctst")
                pss = Ring([S.ps(es, [128, 512], F32, "cps") for _ in range(2)])
                ptr = Ring([S.ps(es, [128, 512], F32, "cpt") for _ in range(2)])
                work = [(s, c) for s in range(NS) for c in range(24)]

                def stage_a(s, c):
                    t0, L = soff[s], seqs[s]
                    a = xin.next()
                    S.op("pool", [], [a], lambda e: e.memset(a.t[:, 0:2], 0.0))
                    S.op("pool", [], [a], lambda e: e.memset(a.t[:, L + 2:L + 4], 0.0))
                    S.dma("sp", a.t[:, 2:L + 2], gqkvT[c * 128:(c + 1) * 128, t0:t0 + L], writes=[a])
                    ac = acc.next()
                    S.op("dve", [a, gcw], [ac], lambda e: e.tensor_scalar(
                        out=ac.t[:, 0:L], in0=a.t[:, 0:L], scalar1=gcw.t[:, c, 0:1], scalar2=None, op0=ALU.mult))
                    for j in range(1, 5):
                        S.op("dve", [a, gcw, ac], [ac], lambda e: e.scalar_tensor_tensor(
                            out=ac.t[:, 0:L], in0=a.t[:, j:j + L], scalar=gcw.t[:, c, j:j + 1], in1=ac.t[:, 0:L],
                            op0=ALU.mult, op1=ALU.add))
                    return ac

                def stage_b(s, c, ac):
                    t0, L = soff[s], seqs[s]
                    y = ys.next()
                    S.op("act", [ac], [y], lambda e: e.activation(out=y.t[:, 0:L], in_=ac.t[:, 0:L], func=AF.Silu))
                    if c < 16:
                        q2 = sq.next()
                        S.op("pool", [y], [q2], lambda e: e.tensor_tensor(out=q2.t[:, 0:L], in0=y.t[:, 0:L], in1=y.t[:, 0:L], op=ALU.mult))
                        o = yn.next()
                        for tt in range(0, L, 512):
                            tw = min(512, L - tt)
                            p = pss.next()
                            S.op("pe", [q2, onesb], [p], lambda e: e.matmul(p.t[:, 0:tw], lhsT=onesb.t[:], rhs=q2.t[:, tt:tt + tw], start=True, stop=True))
                            r = rn.next()
                            S.op("act", [p], [r], lambda e: e.activation(out=r.t[:, 0:tw], in_=p.t[:, 0:tw], func=AF.Sqrt, bias=1e-6, scale=1.0))
                            S.op("dve", [r], [r], lambda e: e.reciprocal(out=r.t[:, 0:tw], in_=r.t[:, 0:tw]))
                            sc = (128.0 ** -0.5) if c < 8 else 1.0
                            S.op("dve", [r, y], [o], lambda e: e.scalar_tensor_tensor(
                                out=o.t[:, tt:tt + tw], in0=y.t[:, tt:tt + tw], scalar=sc, in1=r.t[:, 0:tw], op0=ALU.mult, op1=ALU.mult))
                        dstT = qTn if c < 8 else kTn
                        h = c % 8
                        S.dma("pool", dstT[h * 128:(h + 1) * 128, t0:t0 + L], o.t[:, 0:L], reads=[o])
                    if c >= 8:
                        h = c % 8
                        if c < 16:
                            yf = ynf.next()
                            S.op("pool", [o], [yf], lambda e: e.tensor_copy(out=yf.t[:, 0:L], in_=o.t[:, 0:L]))
                        else:
                            yf = y
                        dst = ktok if c < 16 else vtok
                        for tg in range(0, L, 512):
                            ng = min(4, (L - tg) // 128)
                            p = ptr.next()

                            def f(e, p=p, tg=tg, ng=ng, yf=yf):
                                for j in range(ng):
                                    i = e.transpose(p.t[:, j * 128:(j + 1) * 128], yf.t[:, tg + j * 128:tg + (j + 1) * 128], ident)
                                return i
                            S.op("pe", [yf, cm], [p], f)
                            st = tst.next()
                            S.op("act", [p], [st], lambda e: e.copy(out=st.t[:, 0:ng, :], in_=p.t[:, 0:ng * 128].rearrange("p (j d) -> p j d", d=128)))
                            S.dma("pool", dst[t0 + tg:t0 + tg + ng * 128, h * 128:(h + 1) * 128].rearrange("(j p) d -> p j d", p=128),
                                  st.t[:, 0:ng, :], reads=[st])

                nxt = stage_a(*work[0])
                for i, (s, c) in enumerate(work):
                    cur = nxt
                    if i + 1 < len(work):
                        nxt = stage_a(*work[i + 1])
                    stage_b(s, c, cur)
            S.barrier()
            if upto == 3:
                raise _Stop()
            S.mute = start > 4

            with ExitStack() as es:
                alog = S.sb(es, [128, 16], F32, "alog")
                dtb = S.sb(es, [128, 16], F32, "dtb")
                S.dma("sp", alog.t[:], alog_d[:, :], writes=[alog])
                S.dma("sp", dtb.t[:], dtb_d[:, :], writes=[dtb])
                nea = S.sb(es, [128, 16], F32, "nea")
                S.op("act", [alog], [nea], lambda e: e.activation(out=nea.t[:], in_=alog.t[:], func=AF.Exp))
                S.op("dve", [nea], [nea], lambda e: e.tensor_scalar(out=nea.t[:], in0=nea.t[:], scalar1=-1.0, scalar2=None, op0=ALU.mult))
                qT_p = S.pool(es, 2, [128, 8, 128], BF16, "gqT")
                kT_p = S.pool(es, 2, [128, 8, 128], BF16, "gkT")
                kt_p = S.pool(es, 2, [128, 8, 128], BF16, "gkt")
                vt_p = S.pool(es, 2, [128, 8, 128], BF16, "gvt")
                ab_p = S.pool(es, 2, [128, 32], F32, "gab")
                sm = {n: S.sb(es, [128, 8], F32, "g" + n) for n in
                      ("g", "beta", "nbeta", "gc", "gl", "egc", "egl", "ekd", "bg", "tmp", "ngc")}
                dg = S.sb(es, [128, 8, 128], F32, "dg")
                Dm = S.sb(es, [128, 8, 128], F32, "Dm")
                DmT = S.sb(es, [128, 8, 128], F32, "DmT")
                Ei = S.sb(es, [128, 8, 128], F32, "Ei")
                Es = S.sb(es, [128, 8, 128], F32, "Es")
                EiT = S.sb(es, [128, 8, 128], F32, "EiT")
                Pm = Ring([S.sb(es, [128, 8, 128], F32, "Pm") for _ in range(2)])
                PTm = Ring([S.sb(es, [128, 8, 128], F32, "PTm") for _ in range(2)])
                TT = S.sb(es, [128, 8, 128], F32, "TT")
                QKmT = S.sb(es, [128, 8, 128], BF16, "QKmT")
                rv = S.sb(es, [128, 8, 128], F32, "rv")
                rk = S.sb(es, [128, 8, 128], F32, "rk")
                kdec = S.sb(es, [128, 8, 128], BF16, "kdec")
                wv = S.sb(es, [128, 8, 128], F32, "wv")
                wkT = S.sb(es, [128, 8, 128], BF16, "wkT")
                u_b = S.sb(es, [128, 8, 128], BF16, "u_b")
                St = S.sb(es, [128, 8, 128], F32, "St")
                Stmp = S.sb(es, [128, 8, 128], F32, "Stmp")
                S_b = S.sb(es, [128, 8, 128], BF16, "S_b")
                o1 = S.sb(es, [128, 8, 128], F32, "o1")
                o2 = S.sb(es, [128, 8, 128], F32, "o2")
                oo = S.pool(es, 2, [128, 8, 128], F32, "oo")
                PA = S.ps(es, [128, 8, 128], F32, "PA")
                PB = S.ps(es, [128, 8, 128], F32, "PB")
                PC = S.ps(es, [128, 8, 128], F32, "PC")
                PD = S.ps(es, [128, 512], F32, "PD")
                ones32 = cm.t[:, 1, :]

                def flat(t):
                    return t.t[:].rearrange("p h d -> p (h d)")

                for s in range(NS):
                    t0s, L = soff[s], seqs[s]
                    ntile = L // 128
                    for di in range(2):
                        if di == 0:
                            CS, MI_, MS_, MIT_ = cm.t[:, 3, :], cm.t[:, 2, :], cm.t[:, 4, :], cm.t[:, 3, :]
                        else:
                            CS, MI_, MS_, MIT_ = cm.t[:, 2, :], cm.t[:, 3, :], cm.t[:, 5, :], cm.t[:, 2, :]
                        S.op("pool", [], [St], lambda e: e.memset(St.t[:], 0.0))
                        S.op("pool", [], [S_b], lambda e: e.memset(S_b.t[:], 0.0))
                        order = range(ntile) if di == 0 else range(ntile - 1, -1, -1)
                        for j in order:
                            tt = t0s + j * 128
                            qT_, kT_, kt_, vt_, ab = qT_p.next(), kT_p.next(), kt_p.next(), vt_p.next(), ab_p.next()
                            S.dma("sp", qT_.t[:], qTn[:, tt:tt + 128].rearrange("(h p) t -> p h t", p=128), writes=[qT_])
                            S.dma("sp", kT_.t[:], kTn[:, tt:tt + 128].rearrange("(h p) t -> p h t", p=128), writes=[kT_])
                            S.dma("sp", kt_.t[:], ktok[tt:tt + 128, :].rearrange("p (h d) -> p h d", d=128), writes=[kt_])
                            S.dma("sp", vt_.t[:], vtok[tt:tt + 128, :].rearrange("p (h d) -> p h d", d=128), writes=[vt_])
                            S.dma("sp", ab.t[:], gab[tt:tt + 128, :], writes=[ab])
                            g, beta, nbeta, gc, gl = sm["g"], sm["beta"], sm["nbeta"], sm["gc"], sm["gl"]
                            egc, egl, ekd, bgm, tmp, ngc = sm["egc"], sm["egl"], sm["ekd"], sm["bg"], sm["tmp"], sm["ngc"]
                            a0 = di * 8
                            S.op("dve", [ab, dtb], [tmp], lambda e: e.tensor_tensor(out=tmp.t[:], in0=ab.t[:, a0:a0 + 8], in1=dtb.t[:, a0:a0 + 8], op=ALU.add))
                            S.op("act", [tmp], [tmp], lambda e: e.activation(out=tmp.t[:], in_=tmp.t[:], func=AF.Exp))
                            S.op("act", [tmp], [tmp], lambda e: e.activation(out=tmp.t[:], in_=tmp.t[:], func=AF.Ln, bias=1.0, scale=1.0))
                            S.op("dve", [tmp, nea], [g], lambda e: e.tensor_tensor(out=g.t[:], in0=tmp.t[:], in1=nea.t[:, a0:a0 + 8], op=ALU.mult))
                            S.op("act", [ab], [beta], lambda e: e.activation(out=beta.t[:], in_=ab.t[:, 16 + a0:24 + a0], func=AF.Sigmoid))
                            S.op("dve", [beta], [nbeta], lambda e: e.tensor_scalar(out=nbeta.t[:], in0=beta.t[:], scalar1=-1.0, scalar2=None, op0=ALU.mult))

                            def f(e):
                                e.matmul(PD.t[:, 0:8], lhsT=CS, rhs=g.t[:], start=True, stop=True)
                                return e.matmul(PD.t[:, 8:16], lhsT=ones32, rhs=g.t[:], start=True, stop=True)
                            S.op("pe", [g, cm], [PD], f)
                            S.op("dve", [PD], [gc], lambda e: e.tensor_copy(out=gc.t[:], in_=PD.t[:, 0:8]))
                            S.op("dve", [PD], [gl], lambda e: e.tensor_copy(out=gl.t[:], in_=PD.t[:, 8:16]))
                            S.op("dve", [gc], [ngc], lambda e: e.tensor_scalar(out=ngc.t[:], in0=gc.t[:], scalar1=-1.0, scalar2=None, op0=ALU.mult))
                            S.op("act", [gc], [egc], lambda e: e.activation(out=egc.t[:], in_=gc.t[:], func=AF.Exp))
                            S.op("act", [gl], [egl], lambda e: e.activation(out=egl.t[:], in_=gl.t[:], func=AF.Exp))
                            S.op("dve", [gl, gc], [ekd], lambda e: e.tensor_tensor(out=ekd.t[:], in0=gl.t[:], in1=gc.t[:], op=ALU.subtract))
                            S.op("act", [ekd], [ekd], lambda e: e.activation(out=ekd.t[:], in_=ekd.t[:], func=AF.Exp))
                            S.op("dve", [beta, egc], [bgm], lambda e: e.tensor_tensor(out=bgm.t[:], in0=beta.t[:], in1=egc.t[:], op=ALU.mult))
                            if bisect == 1:
                                S.mute = True
                            S.op("dve", [ngc, cm], [dg], lambda e: e.tensor_tensor(
                                out=dg.t[:], in0=bc_mid(cm.t[:, 0, :], 8), in1=bc_last(ngc.t[:], 128), op=ALU.mult))

                            def f(e):
                                for hh in range(0, 8, 4):
                                    i = e.matmul(PA.t[:, hh:hh + 4, :], lhsT=ones32, rhs=dg.t[:, hh:hh + 4, :], start=True, stop=True)
                                return i
                            S.op("pe", [dg, cm], [PA], f)
                            S.op("dve", [PA, gc], [Dm], lambda e: e.tensor_tensor(out=Dm.t[:], in0=PA.t[:], in1=bc_last(gc.t[:], 128), op=ALU.add))
                            S.op("pool", [Dm], [DmT], lambda e: e.tensor_scalar(out=DmT.t[:], in0=Dm.t[:], scalar1=0.0, scalar2=None, op0=ALU.max))
                            S.op("dve", [Dm], [Dm], lambda e: e.tensor_scalar(out=Dm.t[:], in0=Dm.t[:], scalar1=0.0, scalar2=None, op0=ALU.min))
                            S.op("act", [Dm], [Ei], lambda e: e.activation(out=Ei.t[:], in_=Dm.t[:], func=AF.Exp))
                            S.op("act", [DmT], [EiT], lambda e: e.activation(out=EiT.t[:], in_=DmT.t[:], func=AF.Exp, scale=-1.0))
                            S.op("dve", [Ei, cm], [Es], lambda e: e.tensor_tensor(out=Es.t[:], in0=Ei.t[:], in1=bc_mid(MS_, 8), op=ALU.mult))
                            S.op("dve", [EiT, cm], [EiT], lambda e: e.tensor_tensor(out=EiT.t[:], in0=EiT.t[:], in1=bc_mid(MIT_, 8), op=ALU.mult))
                            if bisect == 2:
                                S.mute = True

                            def f(e):
                                for h in range(8):
                                    i = e.matmul(PB.t[:, h, :], lhsT=kT_.t[:, h, :], rhs=kT_.t[:, h, :], start=True, stop=True)
                                return i
                            S.op("pe", [kT_], [PB], f)

                            def f(e):
                                for h in range(8):
                                    i = e.matmul(PC.t[:, h, :], lhsT=kT_.t[:, h, :], rhs=qT_.t[:, h, :], start=True, stop=True)
                                return i
                            S.op("pe", [kT_, qT_], [PC], f)
                            if bisect == 31:
                                S.mute = True
                            P0 = Pm.next()
                            PT0 = PTm.next()
                            S.op("dve", [PB, Es], [P0], lambda e: e.tensor_tensor(out=P0.t[:], in0=PB.t[:], in1=Es.t[:], op=ALU.mult))
                            S.op("dve", [P0, nbeta], [P0], lambda e: e.tensor_tensor(out=P0.t[:], in0=P0.t[:], in1=bc_last(nbeta.t[:], 128), op=ALU.mult))
                            S.op("dve", [PC, EiT], [QKmT], lambda e: e.tensor_tensor(out=QKmT.t[:], in0=PC.t[:], in1=EiT.t[:], op=ALU.mult))
                            if bisect == 32:
                                S.mute = True

                            def f(e):
                                for h in range(8):
                                    i = e.matmul(PA.t[:, h, :], lhsT=P0.t[:, h, :], rhs=ident, start=True, stop=True)
                                return i
                            S.op("pe", [P0, cm], [PA], f)
                            if bisect == 33:
                                S.mute = True
                            S.op("act", [PA], [PT0], lambda e: act_copy2(e, PT0, PA))
                            S.op("dve", [PT0, cm], [TT], lambda e: e.tensor_tensor(out=TT.t[:], in0=PT0.t[:], in1=bc_mid(cm.t[:, 0, :], 8), op=ALU.add))
                            if bisect == 3:
                                S.mute = True
                            Pc, PTc = P0, PT0
                            for lvl in range(1, 7):
                                Pn = Pm.next()
                                PTn = PTm.next()

                                def f(e, Pc=Pc, PTc=PTc):
                                    for h in range(8):
                                        i = e.matmul(PA.t[:, h, :], lhsT=PTc.t[:, h, :], rhs=Pc.t[:, h, :], start=True, stop=True)
                                    return i
                                S.op("pe", [Pc, PTc], [PA], f)
                                S.op("act", [PA], [Pn], lambda e: act_copy2(e, Pn, PA))
                                if lvl < 6:
                                    def f(e, Pc=Pc, PTc=PTc):
                                        for h in range(8):
                                            i = e.matmul(PB.t[:, h, :], lhsT=Pc.t[:, h, :], rhs=PTc.t[:, h, :], start=True, stop=True)
                                        return i
                                    S.op("pe", [Pc, PTc], [PB], f)
                                    S.op("dve", [PB], [PTn], lambda e: e.tensor_copy(out=PTn.t[:], in_=PB.t[:]))

                                def f(e, Pn=Pn):
                                    for h in range(8):
                                        i = e.matmul(PC.t[:, h, :], lhsT=Pn.t[:, h, :], rhs=TT.t[:, h, :], start=True, stop=True)
                                    return i
                                S.op("pe", [Pn, TT], [PC], f)
                                S.op("dve", [PC, TT], [TT], lambda e: e.tensor_tensor(out=TT.t[:], in0=PC.t[:], in1=TT.t[:], op=ALU.add))
                                Pc, PTc = Pn, PTn
                            if bisect == 4:
                                S.mute = True
                            S.op("pool", [vt_, beta], [rv], lambda e: e.tensor_tensor(out=rv.t[:], in0=vt_.t[:], in1=bc_last(beta.t[:], 128), op=ALU.mult))
                            S.op("pool", [kt_, bgm], [rk], lambda e: e.tensor_tensor(out=rk.t[:], in0=kt_.t[:], in1=bc_last(bgm.t[:], 128), op=ALU.mult))
                            S.op("pool", [kt_, ekd], [kdec], lambda e: e.tensor_tensor(out=kdec.t[:], in0=kt_.t[:], in1=bc_last(ekd.t[:], 128), op=ALU.mult))

                            def f(e):
                                for h in range(8):
                                    i = e.matmul(PA.t[:, h, :], lhsT=TT.t[:, h, :], rhs=rv.t[:, h, :], start=True, stop=True)
                                return i
                            S.op("pe", [TT, rv], [PA], f)

                            def f(e):
                                for h in range(8):
                                    i = e.matmul(PB.t[:, h, :], lhsT=rk.t[:, h, :], rhs=TT.t[:, h, :], start=True, stop=True)
                                return i
                            S.op("pe", [TT, rk], [PB], f)
                            S.op("act", [PA], [wv], lambda e: act_copy2(e, wv, PA))
                            S.op("dve", [PB], [wkT], lambda e: e.tensor_copy(out=wkT.t[:], in_=PB.t[:]))
                            if bisect == 5:
                                S.mute = True

                            def f(e):
                                for h in range(8):
                                    i = e.matmul(PA.t[:, h, :], lhsT=wkT.t[:, h, :], rhs=S_b.t[:, h, :], start=True, stop=True)
                                return i
                            S.op("pe", [wkT, S_b], [PA], f)

                            def f(e):
                                for h in range(8):
                                    i = e.matmul(PB.t[:, h, :], lhsT=qT_.t[:, h, :], rhs=S_b.t[:, h, :], start=True, stop=True)
                                return i
                            S.op("pe", [qT_, S_b], [PB], f)
                            S.op("dve", [PA, wv], [u_b], lambda e: e.tensor_tensor(out=u_b.t[:], in0=wv.t[:], in1=PA.t[:], op=ALU.subtract))
                            S.op("dve", [PB, egc], [o1], lambda e: e.tensor_tensor(out=o1.t[:], in0=PB.t[:], in1=bc_last(egc.t[:], 128), op=ALU.mult))

                            def f(e):
                                for h in range(8):
                                    i = e.matmul(PC.t[:, h, :], lhsT=QKmT.t[:, h, :], rhs=u_b.t[:, h, :], start=True, stop=True)
                                return i
                            S.op("pe", [QKmT, u_b], [PC], f)

                            def f(e):
                                for h in range(8):
                                    i = e.matmul(PA.t[:, h, :], lhsT=kdec.t[:, h, :], rhs=u_b.t[:, h, :], start=True, stop=True)
                                return i
                            S.op("pe", [kdec, u_b], [PA], f)
                            S.op("act", [PC], [o2], lambda e: act_copy2(e, o2, PC))
                            ot = oo.next()
                            S.op("pool", [o1, o2], [ot], lambda e: e.tensor_tensor(out=ot.t[:], in0=o1.t[:], in1=o2.t[:], op=ALU.add))
                            S.dma("pool", odir[di, tt:tt + 128, :], flat(ot), reads=[ot])
                            S.op("pool", [St, egl], [Stmp], lambda e: e.tensor_tensor(out=Stmp.t[:], in0=St.t[:], in1=bc_last(egl.t[:], 128), op=ALU.mult))
                            S.op("dve", [PA, Stmp], [St], lambda e: e.tensor_tensor(out=St.t[:], in0=PA.t[:], in1=Stmp.t[:], op=ALU.add))
                            S.op("act", [St], [S_b], lambda e: e.copy(out=S_b.t[:], in_=St.t[:]))
            S.barrier()
            if upto == 4:
                raise _Stop()
            S.mute = start > 5

            def tm_to_fm(es_, name):
                ptr = Ring([S.ps(es_, [128, 512], F32, name + "pt") for _ in range(2)])
                return ptr

            with ExitStack() as es:
                gnw = S.sb(es, [128, 128], F32, "gnw")
                S.dma("sp", gnw.t[:], gnw_d[:, :], writes=[gnw])
                of_p = S.pool(es, 2, [128, 8, 128], F32, "of")
                ob_p = S.pool(es, 2, [128, 8, 128], F32, "ob")
                z_p = S.pool(es, 2, [128, 8, 128], BF16, "zz")
                sq_p = S.pool(es, 2, [128, 8, 128], F32, "gsq")
                ss_p = S.pool(es, 2, [128, 8], F32, "gss")
                yg_p = S.pool(es, 2, [128, 8, 128], F32, "yg")
                stg_p = S.pool(es, 2, [128, 8, 512], BF16, "ygst")
                ptr = Ring([S.ps(es, [128, 512], F32, "gpt") for _ in range(4)])
                sup = 512 if all(L % 512 == 0 for L in seqs) else 128
                for tg in range(0, T, sup):
                    stg = stg_p.next()
                    for jj in range(sup // 128):
                        tt = tg + jj * 128
                        of, ob, zt = of_p.next(), ob_p.next(), z_p.next()
                        S.dma("sp", of.t[:], odir[0, tt:tt + 128, :].rearrange("p (h d) -> p h d", d=128), writes=[of])
                        S.dma("sp", ob.t[:], odir[1, tt:tt + 128, :].rearrange("p (h d) -> p h d", d=128), writes=[ob])
                        S.dma("sp", zt.t[:], gz[tt:tt + 128, :].rearrange("p (h d) -> p h d", d=128), writes=[zt])
                        S.op("pool", [of, ob], [of], lambda e: e.tensor_tensor(out=of.t[:], in0=of.t[:], in1=ob.t[:], op=ALU.add))
                        sqt, sst, yg = sq_p.next(), ss_p.next(), yg_p.next()
                        S.op("act", [of], [sqt], lambda e: e.activation(out=sqt.t[:], in_=of.t[:], func=AF.Square))
                        S.op("dve", [sqt], [sst], lambda e: e.tensor_reduce(out=sst.t[:], in_=sqt.t[:], axis=AX.X, op=ALU.add))
                        S.op("act", [sst], [sst], lambda e: e.activation(out=sst.t[:], in_=sst.t[:], func=AF.Sqrt, bias=1e-6, scale=1.0 / 128.0))
                        S.op("dve", [sst], [sst], lambda e: e.reciprocal(out=sst.t[:], in_=sst.t[:]))
                        S.op("dve", [of, sst], [yg], lambda e: e.tensor_tensor(out=yg.t[:], in0=of.t[:], in1=bc_last(sst.t[:], 128), op=ALU.mult))
                        S.op("dve", [yg, gnw], [yg], lambda e: e.tensor_tensor(out=yg.t[:], in0=yg.t[:], in1=bc_mid(gnw.t[:], 8), op=ALU.mult))
                        S.op("dve", [yg, zt], [yg], lambda e: e.tensor_tensor(out=yg.t[:], in0=yg.t[:], in1=zt.t[:], op=ALU.mult))
                        for h0 in range(0, 8, 4):
                            p = ptr.next()

                            def f(e, p=p, h0=h0, yg=yg):
                                for h in range(4):
                                    i = e.transpose(p.t[:, h * 128:(h + 1) * 128], yg.t[:, h0 + h, :], ident)
                                return i
                            S.op("pe", [yg, cm], [p], f)
                            S.op("act", [p], [stg], lambda e: e.copy(out=stg.t[:, h0:h0 + 4, jj * 128:(jj + 1) * 128],
                                                                      in_=p.t[:].rearrange("p (h t) -> p h t", t=128)))
                    S.dma("pool", ybT[0][:, tg:tg + sup].rearrange("(h p) t -> p h t", p=128), stg.t[:, :, 0:sup], reads=[stg])
            S.barrier()
            if upto == 5:
                raise _Stop()
            S.mute = start > 6

            with ExitStack() as es:
                relb = S.sb(es, [32, 4], F32, "relb")
                oh = S.sb(es, [32, 1280], F32, "oh")
                S.dma("sp", relb.t[:], relb_d[:, :], writes=[relb])
                S.dma("sp", oh.t[:], oh_d[:, :], writes=[oh])
                tvs = S.sb(es, [4, 1280], BF16, "tvs")
                pq = Ring([S.ps(es, [128, 512], F32, "dps") for _ in range(3)])
                po = [S.ps(es, [128, 512], F32, "dpo") for _ in range(4)]
                for c0 in range(0, 1280, 512):
                    cw = min(512, 1280 - c0)
                    p = pq.next()
                    S.op("pe", [relb, oh], [p], lambda e: e.matmul(p.t[0:4, 0:cw], lhsT=relb.t[:, :], rhs=oh.t[:, c0:c0 + cw], start=True, stop=True))
                    S.op("dve", [p], [tvs], lambda e: e.tensor_scalar(out=tvs.t[:, c0:c0 + cw], in0=p.t[0:4, 0:cw], scalar1=128.0 ** 0.5, scalar2=None, op0=ALU.mult))
                S.dma("pool", tvec[:, :], tvs.t[:], reads=[tvs])
                relbb = S.sb(es, [128, 128], F32, "relbb")
                S.dma("sp", relbb.t[:], bass.AP(relb_d.tensor, 0, [[0, 128], [1, 128]]), writes=[relbb])
                dl = S.sb(es, [128, 4, 128], F32, "dl")
                S.dma("sp", dl.t[:], dlam_d[:, :, :], writes=[dl])
                dnw = S.sb(es, [128, 256], F32, "dnw")
                S.dma("sp", dnw.t[:], dnw_d[:, :], writes=[dnw])
                lt = S.sb(es, [128, 2, 128], F32, "lt")
                l2 = S.sb(es, [128, 2], F32, "l2")
                nlam = S.sb(es, [128, 1], F32, "nlam")
                S.op("dve", [dl], [lt], lambda e: e.tensor_tensor(out=lt.t[:, 0, :], in0=dl.t[:, 0, :], in1=dl.t[:, 1, :], op=ALU.mult))
                S.op("dve", [dl, lt], [lt], lambda e: e.tensor_tensor(out=lt.t[:, 1, :], in0=dl.t[:, 2, :], in1=dl.t[:, 3, :], op=ALU.mult))
                S.op("dve", [lt], [l2], lambda e: e.tensor_reduce(out=l2.t[:], in_=lt.t[:], axis=AX.X, op=ALU.add))
                S.op("act", [l2], [l2], lambda e: e.activation(out=l2.t[:], in_=l2.t[:], func=AF.Exp))
                S.op("dve", [l2], [nlam], lambda e: e.tensor_tensor(out=nlam.t[:], in0=l2.t[:, 1:2], in1=l2.t[:, 0:1], op=ALU.subtract))
                S.op("dve", [nlam], [nlam], lambda e: e.tensor_scalar(out=nlam.t[:], in0=nlam.t[:], scalar1=-LAMBDA_INIT, scalar2=None, op0=ALU.add))
                S.barrier()
                SM = max(seqs)
                Rt = [[S.sb(es, [128, 512], BF16, "Rt") for _ in range(6)] for _ in range(4)]
                for h in range(4):
                    for dlt in range(-1, 5):
                        off = 639 - dlt * 128 - 127
                        S.dma("sp", Rt[h][dlt + 1].t[:], bass.AP(tvec.tensor, h * 1280 + off, [[1, 128], [1, 512]]), writes=[Rt[h][dlt + 1]])
                kT_p = S.pool(es, 1, [128, 2, SM], BF16, "dkT")
                qT_p = S.pool(es, 1, [128, 2, SM], BF16, "dqT")
                v_p = S.pool(es, 1, [128, SM // 128, 258], BF16, "dv")
                pT_p = S.pool(es, 3, [128, 512], BF16, "dpT")
                om = S.sb(es, [128, 4, 2, 256], F32, "om")
                rs = S.pool(es, 2, [128, 1], F32, "drs")
                od = S.pool(es, 2, [128, 256], F32, "dod")
                sqd = S.pool(es, 2, [128, 256], F32, "dsq")
                ssd = S.pool(es, 2, [128, 1], F32, "dss")
                stq = S.pool(es, 2, [128, 2, 512], BF16, "dstq")
                ptr = Ring([S.ps(es, [128, 512], F32, "dpt") for _ in range(1)])
                scale = 128.0 ** -0.5
                for s in range(NS):
                    t0s, L = soff[s], seqs[s]
                    nkb = L // 128
                    QT = 512 if L % 512 == 0 else 128
                    for h in range(4):
                        kT_, qT_, v_ = kT_p.next(), qT_p.next(), v_p.next()
                        S.dma("sp", qT_.t[:, :, 0:L], dqkT[h * 256:(h + 1) * 256, t0s:t0s + L].rearrange("(m p) t -> p m t", p=128), writes=[qT_])
                        S.dma("sp", kT_.t[:, :, 0:L], dqkT[1024 + h * 256:1024 + (h + 1) * 256, t0s:t0s + L].rearrange("(m p) t -> p m t", p=128), writes=[kT_])
                        S.dma("sp", v_.t[:, 0:nkb, 0:256], dv[t0s:t0s + L, h * 256:(h + 1) * 256].rearrange("(j p) d -> p j d", p=128), writes=[v_])
                        S.op("pool", [], [v_], lambda e: e.memset(v_.t[:, 0:nkb, 256:258], 1.0))
                        for q0 in range(0, L, QT):
                            nqs = QT // 128
                            items = [(m, kb) for m in range(2) for kb in range(nkb)]
                            LA = 2

                            def emit_qk(m, kb):
                                k0 = kb * 128
                                rmin = k0 - (q0 + QT - 1)
                                rmax = k0 + 127 - q0
                                near = not (rmin >= 91 or rmax <= -91)
                                p = pq.next()
                                if near:
                                    dlt = (k0 - q0) // 128
                                    R = Rt[h][dlt + 1]

                                    def f(e):
                                        e.matmul(p.t[:, 0:QT], lhsT=kT_.t[:, m, k0:k0 + 128], rhs=qT_.t[:, m, q0:q0 + QT], start=True, stop=False)
                                        return e.matmul(p.t[:, 0:QT], lhsT=antib.t[:], rhs=R.t[:, 0:QT], start=False, stop=True)
                                    S.op("pe", [kT_, qT_, antib, R], [p], f)
                                    return p, None
                                S.op("pe", [kT_, qT_], [p], lambda e: e.matmul(p.t[:, 0:QT], lhsT=kT_.t[:, m, k0:k0 + 128], rhs=qT_.t[:, m, q0:q0 + QT], start=True, stop=True))
                                return p, (31 if rmin > 0 else 15) * 4 + h

                            pend = {}
                            for idx in range(min(LA, len(items))):
                                pend[idx] = emit_qk(*items[idx])
                            for idx, (m, kb) in enumerate(items):
                                if idx + LA < len(items):
                                    pend[idx + LA] = emit_qk(*items[idx + LA])
                                p, bcol = pend.pop(idx)
                                pT = pT_p.next()
                                if bcol is None:
                                    S.op("act", [p], [pT], lambda e: e.activation(out=pT.t[:, 0:QT], in_=p.t[:, 0:QT], func=AF.Exp, scale=scale))
                                else:
                                    S.op("act", [p, relbb], [pT], lambda e: e.activation(out=pT.t[:, 0:QT], in_=p.t[:, 0:QT], func=AF.Exp,
                                                                                          bias=relbb.t[:, bcol:bcol + 1], scale=scale))

                                def f(e):
                                    for qs in range(nqs):
                                        i = e.matmul(po[qs].t[:, 0:258], lhsT=pT.t[:, qs * 128:(qs + 1) * 128], rhs=v_.t[:, kb, :],
                                                     start=(kb == 0), stop=(kb == nkb - 1))
                                    return i
                                S.op("pe", [pT, v_], po[0:nqs], f)
                                if kb == nkb - 1:
                                    for qs in range(nqs):
                                        r = rs.next()
                                        S.op("dve", [po[qs]], [r], lambda e: e.reciprocal(out=r.t[:], in_=po[qs].t[:, 256:257]))
                                        if m == 1:
                                            S.op("dve", [r, nlam], [r], lambda e: e.tensor_tensor(out=r.t[:], in0=r.t[:], in1=nlam.t[:], op=ALU.mult))
                                        S.op("dve", [po[qs], r], [om], lambda e: e.tensor_scalar(out=om.t[:, qs, m, :], in0=po[qs].t[:, 0:256], scalar1=r.t[:, 0:1], scalar2=None, op0=ALU.mult))
                            st = stq.next()
                            for qs in range(nqs):
                                o, sqq, ssq = od.next(), sqd.next(), ssd.next()
                                S.op("pool", [om], [o], lambda e: e.tensor_tensor(out=o.t[:], in0=om.t[:, qs, 0, :], in1=om.t[:, qs, 1, :], op=ALU.add))
                                S.op("act", [o], [sqq], lambda e: e.activation(out=sqq.t[:], in_=o.t[:], func=AF.Square))
                                S.op("dve", [sqq], [ssq], lambda e: e.tensor_reduce(out=ssq.t[:], in_=sqq.t[:], axis=AX.X, op=ALU.add))
                                S.op("act", [ssq], [ssq], lambda e: e.activation(out=ssq.t[:], in_=ssq.t[:], func=AF.Sqrt, bias=1e-6, scale=1.0 / 256.0))
                                S.op("dve", [ssq], [ssq], lambda e: e.reciprocal(out=ssq.t[:], in_=ssq.t[:]))
                                S.op("dve", [o, ssq], [o], lambda e: e.tensor_scalar(out=o.t[:], in0=o.t[:], scalar1=ssq.t[:, 0:1], scalar2=(1.0 - LAMBDA_INIT), op0=ALU.mult, op1=ALU.mult))
                                S.op("pool", [o, dnw], [o], lambda e: e.tensor_tensor(out=o.t[:], in0=o.t[:], in1=dnw.t[:], op=ALU.mult))
                                p = ptr.next()

                                def f(e, p=p, o=o):
                                    e.transpose(p.t[:, 0:128], o.t[:, 0:128], ident)
                                    return e.transpose(p.t[:, 128:256], o.t[:, 128:256], ident)
                                S.op("pe", [o, cm], [p], f)
                                S.op("act", [p], [st], lambda e: e.copy(out=st.t[:, :, qs * 128:(qs + 1) * 128], in_=p.t[:, 0:256].rearrange("p (c t) -> p c t", t=128)))
                            S.dma("pool", ybT[1][h * 256:(h + 1) * 256, t0s + q0:t0s + q0 + QT].rearrange("(c p) t -> p c t", p=128), st.t[:, :, 0:QT], reads=[st])
            S.barrier()
            if upto == 6:
                raise _Stop()
            S.mute = start > 7

            with ExitStack() as es:
                km_p = S.pool(es, 2, [128, 8, MEM], BF16, "ckm")
                vm_p = S.pool(es, 2, [128, 2, 4, 258], BF16, "cvm")
                q_p = S.pool(es, 2, [128, 8, 512], BF16, "cq")
                pT_p = S.pool(es, 3, [128, 512], BF16, "cpT")
                pq = Ring([S.ps(es, [128, 512], F32, "cps") for _ in range(2)])
                po = Ring([S.ps(es, [128, 512], F32, "cpo") for _ in range(4)])
                ptr = Ring([S.ps(es, [128, 512], F32, "cpt") for _ in range(2)])
                rs = S.pool(es, 2, [128, 1], F32, "crs")
                oc = S.pool(es, 2, [128, 256], F32, "coc")
                stq = S.pool(es, 2, [128, 8, 512], BF16, "cstq")
                scale = 256.0 ** -0.5
                for s in range(NS):
                    t0s, L = soff[s], seqs[s]
                    km, vmt = km_p.next(), vm_p.next()
                    S.dma("sp", km.t[:], kmT[:, s * MEM:(s + 1) * MEM].rearrange("(c p) t -> p c t", p=128), writes=[km])
                    for b in range(2):
                        S.dma("sp", vmt.t[:, b, :, 0:256], vm[s * MEM + b * 128:s * MEM + (b + 1) * 128, :].rearrange("p (h d) -> p h d", d=256), writes=[vmt])
                    S.op("pool", [], [vmt], lambda e: e.memset(vmt.t[:, :, :, 256:258], 1.0))
                    QT = 512 if L % 512 == 0 else 128
                    for q0 in range(0, L, QT):
                        nqs = QT // 128
                        qt = q_p.next()
                        S.dma("sp", qt.t[:, :, 0:QT], cqT[:, t0s + q0:t0s + q0 + QT].rearrange("(c p) t -> p c t", p=128), writes=[qt])
                        st = stq.next()
                        for h in range(4):
                            pTs = []
                            for kb in range(2):
                                p = pq.next()

                                def f(e, p=p, kb=kb):
                                    e.matmul(p.t[:, 0:QT], lhsT=km.t[:, 2 * h, kb * 128:(kb + 1) * 128], rhs=qt.t[:, 2 * h, 0:QT], start=True, stop=False)
                                    return e.matmul(p.t[:, 0:QT], lhsT=km.t[:, 2 * h + 1, kb * 128:(kb + 1) * 128], rhs=qt.t[:, 2 * h + 1, 0:QT], start=False, stop=True)
                                S.op("pe", [km, qt], [p], f)
                                pT = pT_p.next()
                                S.op("act", [p], [pT], lambda e: e.activation(out=pT.t[:, 0:QT], in_=p.t[:, 0:QT], func=AF.Exp, scale=scale))
                                pTs.append(pT)
                            for qs in range(nqs):
                                pp_ = po.next()

                                def f(e, pp_=pp_, qs=qs):
                                    e.matmul(pp_.t[:, 0:258], lhsT=pTs[0].t[:, qs * 128:(qs + 1) * 128], rhs=vmt.t[:, 0, h, :], start=True, stop=False)
                                    return e.matmul(pp_.t[:, 0:258], lhsT=pTs[1].t[:, qs * 128:(qs + 1) * 128], rhs=vmt.t[:, 1, h, :], start=False, stop=True)
                                S.op("pe", [pTs[0], pTs[1], vmt], [pp_], f)
                                r, o = rs.next(), oc.next()
                                S.op("dve", [pp_], [r], lambda e: e.reciprocal(out=r.t[:], in_=pp_.t[:, 256:257]))
                                S.op("dve", [pp_, r], [o], lambda e: e.tensor_scalar(out=o.t[:], in0=pp_.t[:, 0:256], scalar1=r.t[:, 0:1], scalar2=None, op0=ALU.mult))
                                p = ptr.next()

                                def f(e, p=p, o=o):
                                    e.transpose(p.t[:, 0:128], o.t[:, 0:128], ident)
                                    return e.transpose(p.t[:, 128:256], o.t[:, 128:256], ident)
                                S.op("pe", [o, cm], [p], f)
                                S.op("act", [p], [st], lambda e: e.copy(out=st.t[:, 2 * h:2 * h + 2, qs * 128:(qs + 1) * 128], in_=p.t[:, 0:256].rearrange("p (c t) -> p c t", t=128)))
                        S.dma("pool", ybT[2][:, t0s + q0:t0s + q0 + QT].rearrange("(c p) t -> p c t", p=128), st.t[:, :, 0:QT], reads=[st])
            S.barrier()
            if upto == 7:
                raise _Stop()
            S.mute = start > 8

            with ExitStack() as es:
                wb_p = S.pool(es, 2, [128, 3, 8, 512], BF16, "mwb")
                y_p = S.pool(es, 2, [128, 3, 8, 512], BF16, "my")
                g_p = S.pool(es, 2, [128, 3, 4, 512], BF16, "mg")
                t_p = S.pool(es, 3, [128, 3, 512], F32, "mt")
                st_p = S.pool(es, 2, [128, 4, 512], BF16, "mst")
                pq = Ring([S.ps(es, [128, 512], F32, "mps") for _ in range(6)])
                sup = 512 if T % 512 == 0 else 128
                for fb in range(4):
                    wt = wb_p.next()
                    for b in range(3):
                        S.dma("sp", wt.t[:, b, :, :], wb_b[b][:, fb * 512:(fb + 1) * 512].rearrange("(c p) n -> p c n", p=128), writes=[wt])
                    for tg in range(0, T, sup):
                        yt, gt = y_p.next(), g_p.next()
                        for b in range(3):
                            S.dma("sp", yt.t[:, b, :, 0:sup], ybT[b][:, tg:tg + sup].rearrange("(c p) t -> p c t", p=128), writes=[yt])
                            S.dma("sp", gt.t[:, b, :, 0:sup], gatesT[b * D + fb * 512:b * D + (fb + 1) * 512, tg:tg + sup].rearrange("(c p) t -> p c t", p=128), writes=[gt])
                        st = st_p.next()
                        for fc in range(4):
                            tmp = t_p.next()
                            for b in range(3):
                                p = pq.next()

                                def f(e, p=p, b=b, fc=fc):
                                    for kc in range(8):
                                        i = e.matmul(p.t[:, 0:sup], lhsT=wt.t[:, b, kc, fc * 128:(fc + 1) * 128], rhs=yt.t[:, b, kc, 0:sup], start=(kc == 0), stop=(kc == 7))
                                    return i
                                S.op("pe", [wt, yt], [p], f)
                                S.op("dve", [p, gt], [tmp], lambda e: e.tensor_tensor(out=tmp.t[:, b, 0:sup], in0=p.t[:, 0:sup], in1=gt.t[:, b, fc, 0:sup], op=ALU.mult))
                            S.op("pool", [tmp], [tmp], lambda e: e.tensor_tensor(out=tmp.t[:, 0, 0:sup], in0=tmp.t[:, 0, 0:sup], in1=tmp.t[:, 1, 0:sup], op=ALU.add))
                            S.op("pool", [tmp], [st], lambda e: e.tensor_tensor(out=st.t[:, fc, 0:sup], in0=tmp.t[:, 0, 0:sup], in1=tmp.t[:, 2, 0:sup], op=ALU.add))
                        S.dma("pool", mT[fb * 512:(fb + 1) * 512, tg:tg + sup].rearrange("(c p) t -> p c t", p=128), st.t[:, :, 0:sup], reads=[st])
            S.barrier()
            if upto == 8:
                raise _Stop()
            S.mute = start > 9

            def layer_norm(r, lnp, gi, es_pools):
                stats, mv, outt = es_pools
                stt = stats.next()
                for c in range(4):
                    S.op("dve", [r], [stt], lambda e: e.bn_stats(out=stt.t[:, c, :], in_=r.t[:, c * 512:(c + 1) * 512]))
                m = mv.next()
                S.op("dve", [stt], [m], lambda e: e.bn_aggr(out=m.t[:, 0:2], in_=stt.t[:]))
                S.op("act", [m], [m], lambda e: e.activation(out=m.t[:, 2:3], in_=m.t[:, 1:2], func=AF.Sqrt, bias=1e-5, scale=1.0))
                S.op("dve", [m], [m], lambda e: e.reciprocal(out=m.t[:, 2:3], in_=m.t[:, 2:3]))
                o = outt.next()
                S.op("dve", [r, m], [o], lambda e: e.tensor_scalar(out=o.t[:], in0=r.t[:], scalar1=m.t[:, 0:1], scalar2=m.t[:, 2:3], op0=ALU.subtract, op1=ALU.mult))
                S.op("pool", [o, lnp], [o], lambda e: e.tensor_tensor(out=o.t[:], in0=o.t[:], in1=lnp.t[:, 0, :], op=ALU.mult))
                S.op("dve", [o, lnp], [o], lambda e: e.tensor_tensor(out=o.t[:], in0=o.t[:], in1=lnp.t[:, 1, :], op=ALU.add))
                return o

            with ExitStack() as es:
                lnp = S.sb(es, [128, 2, D], F32, "lnp")
                S.dma("sp", lnp.t[:], ln_d[:, 0:2, :], writes=[lnp])
                wo = S.sb(es, [128, 16, D], BF16, "wo")
                for c in range(0, 16, 4):
                    S.dma("sp", wo.t[:, c:c + 4, :], wo_b[c * 128:(c + 4) * 128, :].rearrange("(c p) n -> p c n", p=128), writes=[wo])
                sup = 512 if T % 512 == 0 else 128
                m_p = S.pool(es, 2, [128, 16, sup], BF16, "omT")
                x_p = S.pool(es, 2, [128, D], F32, "ox")
                r_p = S.pool(es, 2, [128, D], F32, "or")
                pools = (S.pool(es, 2, [128, 4, 6], F32, "ost"), S.pool(es, 2, [128, 4], F32, "omv"), S.pool(es, 2, [128, D], F32, "oout"))
                xo = S.pool(es, 2, [128, 16, sup], BF16, "oxo")
                pq = Ring([S.ps(es, [128, 512], F32, "ops") for _ in range(4)])
                ptr = Ring([S.ps(es, [128, 512], F32, "opt") for _ in range(4)])
                tiles = [(tg, jj) for tg in range(0, T, sup) for jj in range(sup // 128)]
                ctx = {}
                cur = {"mt": None, "xot": None}

                def a1(i):
                    tg, jj = tiles[i]
                    if jj == 0:
                        mt = m_p.next()
                        S.dma("sp", mt.t[:], mT[:, tg:tg + sup].rearrange("(c p) t -> p c t", p=128), writes=[mt])
                        cur["mt"] = mt
                    mt = cur["mt"]
                    tt = tg + jj * 128
                    xt = x_p.next()
                    S.dma("sp", xt.t[:], x_d[tt:tt + 128, :], writes=[xt])
                    ps4 = []
                    for nb in range(4):
                        p = pq.next()

                        def f(e, p=p, nb=nb):
                            for kc in range(16):
                                i_ = e.matmul(p.t[:], lhsT=mt.t[:, kc, jj * 128:(jj + 1) * 128], rhs=wo.t[:, kc, nb * 512:(nb + 1) * 512], start=(kc == 0), stop=(kc == 15))
                            return i_
                        S.op("pe", [mt, wo], [p], f)
                        ps4.append(p)
                    ctx[i] = {"xt": xt, "ps": ps4, "tt": tt}

                def a2(i):
                    c = ctx[i]
                    r = r_p.next()
                    xt = c["xt"]
                    for nb in range(4):
                        p = c["ps"][nb]
                        S.op("dve", [p, xt], [r], lambda e: e.scalar_tensor_tensor(out=r.t[:, nb * 512:(nb + 1) * 512], in0=xt.t[:, nb * 512:(nb + 1) * 512], scalar=ALPHA,
                                                                                  in1=p.t[:], op0=ALU.mult, op1=ALU.add))
                    c["r"] = r

                def b_ln(i):
                    c = ctx[i]
                    o = layer_norm(c["r"], lnp, 0, pools)
                    S.dma("pool", x1[c["tt"]:c["tt"] + 128, :], o.t[:], reads=[o])
                    c["o"] = o

                def b_tr(i):
                    tg, jj = tiles[i]
                    c = ctx.pop(i)
                    o = c["o"]
                    if jj == 0:
                        cur["xot"] = xo.next()
                    xot = cur["xot"]
                    for kc0 in range(0, 16, 4):
                        p = ptr.next()

                        def f(e, p=p, kc0=kc0, o=o):
                            for cc in range(4):
                                i_ = e.transpose(p.t[:, cc * 128:(cc + 1) * 128], o.t[:, (kc0 + cc) * 128:(kc0 + cc + 1) * 128], ident)
                            return i_
                        S.op("pe", [o, cm], [p], f)
                        S.op("act", [p], [xot], lambda e: e.copy(out=xot.t[:, kc0:kc0 + 4, jj * 128:(jj + 1) * 128], in_=p.t[:].rearrange("p (c t) -> p c t", t=128)))
                    if jj == sup // 128 - 1:
                        S.dma("pool", x1T[:, tg:tg + sup].rearrange("(c p) t -> p c t", p=128), xot.t[:], reads=[xot])

                a1(0)
                a2(0)
                for i in range(len(tiles)):
                    if i + 1 < len(tiles):
                        a1(i + 1)
                    b_ln(i)
                    if i + 1 < len(tiles):
                        a2(i + 1)
                    b_tr(i)
            S.barrier()
            if upto == 9:
                raise _Stop()
            S.mute = start > 10

            with ExitStack() as es:
                LM = max(s[1] for s in segs)
                fcw = S.sb(es, [128, 2 * NFC, 3], F32, "fcw")
                S.dma("sp", fcw.t[:], fconv_d[:, :, :], writes=[fcw])
                xs_p = S.pool(es, 1, [128, 16, LM + 2], BF16, "fxs")
                w_p = S.pool(es, 3, [128, 2, 16, 128], BF16, "fw")
                u_p = S.pool(es, 2, [128, 2, LM + 2], F32, "fu")
                c_p = S.pool(es, 2, [128, 2, LM], F32, "fc")
                h_p = S.pool(es, 2, [128, LM], BF16, "fh")
                pq = Ring([S.ps(es, [128, 512], F32, "fps") for _ in range(6)])
                for (t0, L, hl, hr) in segs:
                    xs = xs_p.next()
                    lo = 1 - (1 if hl else 0)
                    hi = L + 1 + (1 if hr else 0)
                    if not hl:
                        S.op("pool", [], [xs], lambda e: e.memset(xs.t[:, :, 0:1], 0.0))
                    if not hr:
                        S.op("pool", [], [xs], lambda e: e.memset(xs.t[:, :, L + 1:L + 2], 0.0))
                    S.dma("sp", xs.t[:, :, lo:hi], x1T[:, t0 - 1 + lo:t0 - 1 + hi].rearrange("(c p) t -> p c t", p=128), writes=[xs])
                    cols = list(range(0, L + 2, 512))
                    for c in range(NFC):
                        w = w_p.next()
                        for gv in range(2):
                            cc = gv * DFF + c * 128
                            S.dma("sp", w.t[:, gv, :, :], wu_b[:, cc:cc + 128].rearrange("(c p) n -> p c n", p=128), writes=[w])
                        u = u_p.next()
                        for gv in range(2):
                            for ci, c0 in enumerate(cols):
                                cw = min(512, L + 2 - c0)
                                p = pq.next()

                                def f(e, p=p, gv=gv, c0=c0, cw=cw):
                                    for kc in range(16):
                                        i = e.matmul(p.t[:, 0:cw], lhsT=w.t[:, gv, kc, :], rhs=xs.t[:, kc, c0:c0 + cw], start=(kc == 0), stop=(kc == 15))
                                    return i
                                S.op("pe", [w, xs], [p], f)
                                if (ci + gv) % 2 == 0:
                                    S.op("act", [p], [u], lambda e: e.copy(out=u.t[:, gv, c0:c0 + cw], in_=p.t[:, 0:cw]))
                                else:
                                    S.op("dve", [p], [u], lambda e: e.tensor_copy(out=u.t[:, gv, c0:c0 + cw], in_=p.t[:, 0:cw]))
                        cv = c_p.next()
                        for gv in range(2):
                            ch = gv * NFC + c
                            eng = "dve"
                            S.op(eng, [u, fcw], [cv], lambda e: e.tensor_scalar(out=cv.t[:, gv, 0:L], in0=u.t[:, gv, 0:L], scalar1=fcw.t[:, ch, 0:1], scalar2=None, op0=ALU.mult))
                            for j in (1, 2):
                                S.op(eng, [u, fcw, cv], [cv], lambda e: e.scalar_tensor_tensor(out=cv.t[:, gv, 0:L], in0=u.t[:, gv, j:j + L], scalar=fcw.t[:, ch, j:j + 1],
                                                                                                 in1=cv.t[:, gv, 0:L], op0=ALU.mult, op1=ALU.add))
                        S.op("act", [cv], [cv], lambda e: e.activation(out=cv.t[:, 0, 0:L], in_=cv.t[:, 0, 0:L], func=AF.Silu))
                        hh = h_p.next()
                        S.op("dve", [cv], [hh], lambda e: e.tensor_tensor(out=hh.t[:, 0:L], in0=cv.t[:, 0, 0:L], in1=cv.t[:, 1, 0:L], op=ALU.mult))
                        S.dma("pool", hT[c * 128:(c + 1) * 128, t0:t0 + L], hh.t[:, 0:L], reads=[hh])
            S.barrier()
            if upto == 10:
                raise _Stop()
            S.mute = start > 11

            with ExitStack() as es:
                lnp = S.sb(es, [128, 2, D], F32, "lnp2")
                S.dma("sp", lnp.t[:], ln_d[:, 2:4, :], writes=[lnp])
                wd = S.sb(es, [128, NFC, 1024], BF16, "wd")
                TG = 256 if T % 256 == 0 else 128
                h_p = S.pool(es, 2, [128, NFC, TG], BF16, "dh")
                yp_p = S.pool(es, 2, [128, 1024], F32, "dyp")
                x_p = S.pool(es, 2, [128, D], F32, "dx1")
                r_p = S.pool(es, 1, [128, D], F32, "dr")
                pools = (S.pool(es, 2, [128, 4, 6], F32, "dst"), S.pool(es, 2, [128, 4], F32, "dmv"), S.pool(es, 2, [128, D], F32, "dout"))
                pq = Ring([S.ps(es, [128, 512], F32, "dps") for _ in range(6)])
                for half in range(2):
                    for c in range(0, NFC, 8):
                        ce = min(NFC, c + 8)
                        S.dma("sp", wd.t[:, c:ce, :], wd_b[c * 128:ce * 128, half * 1024:(half + 1) * 1024].rearrange("(c p) n -> p c n", p=128), writes=[wd])
                    for tg in range(0, T, TG):
                        ht = h_p.next()
                        S.dma("sp", ht.t[:], hT[:, tg:tg + TG].rearrange("(c p) t -> p c t", p=128), writes=[ht])
                        for jj in range(TG // 128):
                            tt = tg + jj * 128
                            ps2 = [pq.next(), pq.next()]
                            for nb in range(2):
                                p = ps2[nb]

                                def f(e, p=p, nb=nb):
                                    for kc in range(NFC):
                                        i = e.matmul(p.t[:], lhsT=ht.t[:, kc, jj * 128:(jj + 1) * 128], rhs=wd.t[:, kc, nb * 512:(nb + 1) * 512], start=(kc == 0), stop=(kc == NFC - 1))
                                    return i
                                S.op("pe", [ht, wd], [p], f)
                            if half == 0:
                                yp = yp_p.next()
                                S.op("act", [ps2[0]], [yp], lambda e: e.copy(out=yp.t[:, 0:512], in_=ps2[0].t[:]))
                                S.op("dve", [ps2[1]], [yp], lambda e: e.tensor_copy(out=yp.t[:, 512:1024], in_=ps2[1].t[:]))
                                S.dma("pool", ypart[tt:tt + 128, :], yp.t[:], reads=[yp])
                            else:
                                yp, xt, r = yp_p.next(), x_p.next(), r_p.next()
                                S.dma("sp", yp.t[:], ypart[tt:tt + 128, :], writes=[yp])
                                S.dma("sp", xt.t[:], x1[tt:tt + 128, :], writes=[xt])
                                S.op("dve", [xt, yp], [r], lambda e: e.scalar_tensor_tensor(out=r.t[:, 0:1024], in0=xt.t[:, 0:1024], scalar=ALPHA, in1=yp.t[:], op0=ALU.mult, op1=ALU.add))
                                for nb in range(2):
                                    S.op("dve", [ps2[nb], xt], [r], lambda e: e.scalar_tensor_tensor(
                                        out=r.t[:, 1024 + nb * 512:1024 + (nb + 1) * 512], in0=xt.t[:, 1024 + nb * 512:1024 + (nb + 1) * 512], scalar=ALPHA,
                                        in1=ps2[nb].t[:], op0=ALU.mult, op1=ALU.add))
                                o = layer_norm(r, lnp, 2, pools)
                                S.dma("pool", y_d[tt:tt + 128, :], o.t[:], reads=[o])
                    S.barrier()
            S.barrier()
            if upto == 11:
                raise _Stop()
            S.mute = start > 12

        except _Stop:
            pass
        S.mute = False
        S.barrier()

    return nc


def _t5_bucket(rel):
    nb = 16
    max_exact = 8
    ret = np.where(rel > 0, nb, 0)
    n = np.abs(rel)
    nf = np.maximum(n, 1).astype(np.float32)
    large = max_exact + (np.log(nf / np.float32(max_exact)) / np.float32(math.log(128 / max_exact)) * np.float32(nb - max_exact)).astype(np.int32)
    large = np.minimum(large, nb - 1)
    return ret + np.where(n < max_exact, n, large)


def _consts():
    p = np.arange(128)[:, None]
    f = np.arange(128)[None, :]
    cm = np.zeros((128, 7, 128), np.float32)
    cm[:, 0] = (p == f)
    cm[:, 1] = 1.0
    cm[:, 2] = (p >= f)
    cm[:, 3] = (p <= f)
    cm[:, 4] = (p > f)
    cm[:, 5] = (p < f)
    cm[:, 6] = (p + f == 127)
    rel = 639 - np.arange(1280)
    b = _t5_bucket(rel)
    oh = np.zeros((32, 1280), np.float32)
    oh[b, np.arange(1280)] = 1.0
    return cm, oh


def make_in_maps(inp, per_core_x, per_core_mem):
    f = lambda a: np.ascontiguousarray(a, dtype=np.float32)
    cm, oh = _consts()
    bc = lambda v, n=128: np.ascontiguousarray(np.broadcast_to(np.asarray(v, np.float32).reshape(1, -1), (n, np.asarray(v).size)))
    shared = {
        "w_in": f(inp["w_in"][0]), "w_gate": f(inp["w_gate"][0]), "w_kv": f(inp["w_mem_kv"][0]),
        "wb0": f(inp["w_branch_gdn"][0]), "wb1": f(inp["w_branch_diff"][0]), "wb2": f(inp["w_branch_cross"][0]),
        "w_out": f(inp["w_out"][0]), "w_up": f(inp["w_up"][0]), "w_down": f(inp["w_down"][0]),
        "gconv": f(np.asarray(inp["gdn_conv"][0]).reshape(5, 24, 128).transpose(2, 1, 0)),
        "fconv": f(np.asarray(inp["ffn_conv"][0]).reshape(3, 2 * NFC, 128).transpose(2, 1, 0)),
        "bgate": f(np.asarray(inp["b_gate"][0]).reshape(48, 128).T),
        "alog": bc(np.asarray(inp["gdn_a_log"][0]).reshape(-1)),
        "dtb": bc(np.asarray(inp["gdn_dt_bias"][0]).reshape(-1)),
        "gnw": bc(inp["gdn_norm_w"][0]),
        "dlam": f(np.broadcast_to(np.asarray(inp["diff_lambda"][0], np.float32)[None], (128, 4, 128))),
        "dnw": bc(inp["diff_norm_w"][0]),
        "lnp": f(np.broadcast_to(np.stack([np.asarray(inp[k][0], np.float32) for k in ("ln1_g", "ln1_b", "ln2_g", "ln2_b")])[None], (128, 4, D))),
        "relb": f(inp["rel_bias"]),
        "cmask": cm, "oh": oh,
    }
    maps = []
    for xc, mc in zip(per_core_x, per_core_mem):
        d = dict(shared)
        d["x"] = f(xc)
        d["mem"] = f(mc)
        maps.append(d)
    return maps


_NC_CACHE = {}


def kernel(**inp):
    xp = np.asarray(inp["x_prompt"], np.float32)
    xs = np.asarray(inp["x_sample"], np.float32)
    mp = np.asarray(inp["mem_prompt"], np.float32)
    ms = np.asarray(inp["mem_sample"], np.float32)
    n = 8
    seqs = (2048, 2048, 4096)
    px, pm = [], []
    for c in range(n):
        px.append(np.concatenate([xp[2 * c], xp[2 * c + 1], xs[c]], axis=0))
        pm.append(np.concatenate([mp[2 * c], mp[2 * c + 1], ms[c]], axis=0))
    maps = make_in_maps(inp, px, pm)
    if seqs not in _NC_CACHE:
        _NC_CACHE[seqs] = build_program(list(seqs))
    nc = _NC_CACHE[seqs]
    res = run_bass_kernel_spmd(nc, maps, core_ids=list(range(n)))
    yp = np.empty_like(xp)
    ys = np.empty_like(xs)
    for c in range(n):
        y = res.results[c]["y"]
        yp[2 * c] = y[0:2048]
        yp[2 * c + 1] = y[2048:4096]
        ys[c] = y[4096:8192]
    return (yp, ys)
```
